# Optimizing a Trainium2 kernel written in Bass

```python
import jax, jax.numpy as jnp
from jax import lax
import numpy as np

D_MODEL = 1024
BATCH = 32
SEQ = 256
DEPTH = 2
DEC_BATCH = 2
DEC_SEQ = 1024
PAST_LEN = 512

GRID_W = 64
N_HEADS = 8
N_KV_HEADS = 2
HEAD_DIM = 64
ATTN_WIDTH = N_HEADS * HEAD_DIM
KV_WIDTH = N_KV_HEADS * HEAD_DIM
CONV_WIDTH = D_MODEL // 2
CONV_K = 3
IN_WIDTH = ATTN_WIDTH + 2 * KV_WIDTH + 3 * CONV_WIDTH
MIX_WIDTH = ATTN_WIDTH + CONV_WIDTH
D_FF = 2816
POOL_WINDOWS = (2, 4, 8, 16)
POOL_GROUP = D_MODEL // len(POOL_WINDOWS)
N_EVEN = (DEPTH + 1) // 2
N_ODD = DEPTH // 2
N_MOD = 9
Q_BLOCK = 128
ROPE_THETA = 10000.0
EPS = 1e-6

kernel_name = "hybrid_prefix_diffusion_step"


def rms_norm(x, g):
    xf = x.astype(jnp.float32)
    y = xf * lax.rsqrt(jnp.mean(xf * xf, axis=-1, keepdims=True) + EPS)
    return (y * g.astype(jnp.float32)).astype(x.dtype)


def modulate(x, g, shift, scale):
    return rms_norm(x, g) * (1 + scale[:, None, :]) + shift[:, None, :]


def swiglu(h, w1, w2):
    gate, up = jnp.split(h @ w1, 2, axis=-1)
    return (jax.nn.silu(gate) * up) @ w2


def rope_half(x, ang):
    cos = jnp.cos(ang)[None, :, None, :].astype(x.dtype)
    sin = jnp.sin(ang)[None, :, None, :].astype(x.dtype)
    x1, x2 = jnp.split(x, 2, axis=-1)
    return jnp.concatenate([x1 * cos - x2 * sin, x2 * cos + x1 * sin], axis=-1)


def axial_rope(x):
    rows = x.shape[1] // GRID_W
    t = jnp.arange(rows * GRID_W)
    row = (t // GRID_W).astype(jnp.float32)
    col = (t % GRID_W).astype(jnp.float32)
    half = HEAD_DIM // 2
    inv = ROPE_THETA ** (-jnp.arange(0, half, 2, dtype=jnp.float32) / half)
    xr, xc = jnp.split(x, 2, axis=-1)
    return jnp.concatenate([rope_half(xr, row[:, None] * inv[None, :]),
                            rope_half(xc, col[:, None] * inv[None, :])], axis=-1)


def attention(q, k, v):
    B, S = q.shape[:2]
    nb = S // Q_BLOCK
    G = N_HEADS // N_KV_HEADS
    qb = q.reshape(B, nb, Q_BLOCK, N_KV_HEADS, G, HEAD_DIM).transpose(1, 0, 2, 3, 4, 5)
    scale = HEAD_DIM ** -0.5

    def block(qi):
        s = jnp.einsum('bqhgd,bkhd->bhgqk', qi, k).astype(jnp.float32) * scale
        p = jax.nn.softmax(s, axis=-1).astype(v.dtype)
        return jnp.einsum('bhgqk,bkhd->bqhgd', p, v)

    o = lax.map(block, qb)
    return o.transpose(1, 0, 2, 3, 4, 5).reshape(B, S, ATTN_WIDTH)


def short_conv(x, w):
    S = x.shape[1]
    xp = jnp.pad(x, ((0, 0), (1, 1), (0, 0)))
    return xp[:, 0:S] * w[0] + xp[:, 1:S + 1] * w[1] + xp[:, 2:S + 2] * w[2]


def conv_attn_mixer(h, w_in, w_out, q_g, k_g, conv_w, ctx_kv):
    B, S, _ = h.shape
    splits = np.cumsum([ATTN_WIDTH, KV_WIDTH, KV_WIDTH, CONV_WIDTH, CONV_WIDTH]).tolist()
    q, k, v, bg, cg, xc = jnp.split(h @ w_in, splits, axis=-1)
    q = rms_norm(q.reshape(B, S, N_HEADS, HEAD_DIM), q_g)
    k = rms_norm(k.reshape(B, S, N_KV_HEADS, HEAD_DIM), k_g)
    v = v.reshape(B, S, N_KV_HEADS, HEAD_DIM)
    if ctx_kv is None:
        attn = attention(q, k, v)
        new_kv = (k, v)
    else:
        ck, cv = ctx_kv
        attn = attention(axial_rope(q),
                         jnp.concatenate([ck, axial_rope(k)], axis=1),
                         jnp.concatenate([cv, v], axis=1))
        new_kv = None
    conv = bg * short_conv(cg * xc, conv_w)
    return jnp.concatenate([attn, conv], axis=-1) @ w_out, new_kv


def pool_mixer(h, pool_w, pool_scale):
    B, S, D = h.shape
    hf = h.astype(jnp.float32)
    cs = jnp.concatenate([jnp.zeros((B, 1, D), jnp.float32), jnp.cumsum(hf, axis=1)], axis=1)
    t = jnp.arange(S)
    outs = []
    for gi, w in enumerate(POOL_WINDOWS):
        left = w // 2
        right = w - 1 - left
        lo = jnp.maximum(t - left, 0)
        hi = jnp.minimum(t + right + 1, S)
        sl = slice(gi * POOL_GROUP, (gi + 1) * POOL_GROUP)
        csg = cs[:, :, sl]
        mean = (csg[:, hi] - csg[:, lo]) / (hi - lo).astype(jnp.float32)[None, :, None]
        diff = (mean - hf[:, :, sl]).astype(h.dtype)
        outs.append(diff @ pool_w[gi])
    return jnp.concatenate(outs, axis=-1) * pool_scale


def run_trunk(x, cvec, cache_k, cache_v, ada_w, ada_b, norm_g, ffn_w1, ffn_w2,
              mix_w_in, mix_w_out, q_norm, k_norm, conv_w, pool_w, pool_scale, final_g):
    is_ctx = cache_k is None
    ks, vs = [], []
    for l in range(DEPTH):
        mod = jax.nn.silu(cvec) @ ada_w[l] + ada_b[l]
        sh1, sc1, g1, sh2, sc2, g2, sh3, sc3, g3 = jnp.split(mod, N_MOD, axis=-1)
        x = x + 0.5 * g1[:, None, :] * swiglu(modulate(x, norm_g[l, 0], sh1, sc1),
                                              ffn_w1[l, 0], ffn_w2[l, 0])
        h = modulate(x, norm_g[l, 1], sh2, sc2)
        if l % 2 == 0:
            e = l // 2
            ctx_kv = None if is_ctx else (cache_k[:, e], cache_v[:, e])
            out, kv = conv_attn_mixer(h, mix_w_in[e], mix_w_out[e], q_norm[e], k_norm[e],
                                      conv_w[e], ctx_kv)
            if is_ctx:
                ks.append(kv[0])
                vs.append(kv[1])
        else:
            o = l // 2
            out = pool_mixer(h, pool_w[o], pool_scale[o])
        x = x + g2[:, None, :] * out
        x = x + 0.5 * g3[:, None, :] * swiglu(modulate(x, norm_g[l, 2], sh3, sc3),
                                              ffn_w1[l, 1], ffn_w2[l, 1])
    return rms_norm(x, final_g), ks, vs


def setup_inputs(seed: int = 0) -> dict:
    key = jax.random.key(seed)
    ks = jax.random.split(key, 20)
    f32 = jnp.float32
    n = lambda k, s, sc: jax.random.normal(k, s, f32) * sc
    return {
        "x_prompt": n(ks[0], (BATCH, SEQ, D_MODEL), 1.0),
        "x_sample": n(ks[1], (DEC_BATCH, DEC_SEQ, D_MODEL), 1.0),
        "c": n(ks[2], (DEC_BATCH, D_MODEL), 1.0),
        "cache_k": n(ks[3], (DEC_BATCH, N_EVEN, PAST_LEN, N_KV_HEADS, HEAD_DIM), 1.0),
        "cache_v": n(ks[4], (DEC_BATCH, N_EVEN, PAST_LEN, N_KV_HEADS, HEAD_DIM), 1.0),
        "c_ctx": n(ks[5], (D_MODEL,), 1.0),
        "ada_w": n(ks[6], (DEPTH, D_MODEL, N_MOD * D_MODEL), 0.5 * D_MODEL ** -0.5),
        "ada_b": n(ks[7], (DEPTH, N_MOD * D_MODEL), 0.02),
        "norm_g": 1.0 + n(ks[8], (DEPTH, 3, D_MODEL), 0.1),
        "ffn_w1": n(ks[9], (DEPTH, 2, D_MODEL, 2 * D_FF), D_MODEL ** -0.5),
        "ffn_w2": n(ks[10], (DEPTH, 2, D_FF, D_MODEL), D_FF ** -0.5),
        "mix_w_in": n(ks[11], (N_EVEN, D_MODEL, IN_WIDTH), D_MODEL ** -0.5),
        "mix_w_out": n(ks[12], (N_EVEN, MIX_WIDTH, D_MODEL), MIX_WIDTH ** -0.5),
        "q_norm": 1.0 + n(ks[13], (N_EVEN, HEAD_DIM), 0.1),
        "k_norm": 1.0 + n(ks[14], (N_EVEN, HEAD_DIM), 0.1),
        "conv_w": n(ks[15], (N_EVEN, CONV_K, CONV_WIDTH), CONV_K ** -0.5),
        "pool_w": n(ks[16], (N_ODD, len(POOL_WINDOWS), POOL_GROUP, POOL_GROUP), POOL_GROUP ** -0.5),
        "pool_scale": 1.0 + n(ks[17], (N_ODD, D_MODEL), 0.1),
        "final_g": 1.0 + n(ks[18], (D_MODEL,), 0.1),
    }


def reference(x_prompt, x_sample, c, cache_k, cache_v, c_ctx, ada_w, ada_b, norm_g, ffn_w1, ffn_w2,
              mix_w_in, mix_w_out, q_norm, k_norm, conv_w, pool_w, pool_scale, final_g):
    weights = (ada_w, ada_b, norm_g, ffn_w1, ffn_w2, mix_w_in, mix_w_out, q_norm, k_norm,
               conv_w, pool_w, pool_scale, final_g)
    y_prompt, ks, vs = run_trunk(x_prompt, c_ctx[None, :], None, None, *weights)
    new_cache_k = jnp.stack(ks, axis=1)
    new_cache_v = jnp.stack(vs, axis=1)
    y_sample, _, _ = run_trunk(x_sample, c, cache_k, cache_v, *weights)
    return (y_prompt, y_sample, new_cache_k, new_cache_v)
```

```python
import numpy as np
from collections import deque
from contextlib import ExitStack
import concourse.bass as bass
import concourse.mybir as mybir
from concourse.bass_utils import run_bass_kernel_spmd

F32 = mybir.dt.float32
BF16 = mybir.dt.bfloat16
AF = mybir.ActivationFunctionType
ALU = mybir.AluOpType
AX = mybir.AxisListType
PE, ACT, DVE, POOL, SP = "tensor", "scalar", "vector", "gpsimd", "sync"
ENGS = (PE, ACT, DVE, POOL, SP)

D = 1024
DFF = 2816
NJ = 22
EPS = 1e-6
NP_COLS = 1024
NS_COLS = 288
HALO = 16
NM = NP_COLS + NS_COLS
NR = 1024 - NS_COLS
GRID_W = 64
POOL_WINDOWS = (2, 4, 8, 16)


class Buf:
    __slots__ = ("name", "w", "r")

    def __init__(self, name):
        self.name = name
        self.w = None
        self.r = []


class Op:
    __slots__ = ("eng", "fn", "deps", "marked", "sig", "is_dma", "key", "dval")

    def __init__(self, eng, fn, is_dma):
        self.eng = eng
        self.fn = fn
        self.deps = ()
        self.marked = False
        self.sig = 0
        self.is_dma = is_dma
        self.key = None
        self.dval = 0


class Prog:
    def __init__(self, nc):
        self.nc = nc
        self.ops = {e: [] for e in ENGS}
        self.dma_keys = {}

    def op(self, eng, fn, reads=(), writes=(), dma=False, key=None):
        o = Op(eng, fn, dma)
        deps = set()
        for b in reads:
            if b.w is not None:
                deps.add(b.w)
        for b in writes:
            if b.w is not None:
                deps.add(b.w)
            deps.update(b.r)
        if eng == PE and not dma:
            deps = {d for d in deps if not (d.eng == PE and not d.is_dma)}
        for d in deps:
            d.marked = True
        o.deps = deps
        for b in reads:
            b.r.append(o)
        for b in writes:
            b.w = o
            b.r = []
        if dma:
            if key is None:
                key = (writes[0] if writes else reads[0]).name
            o.key = key
            self.dma_keys[key] = self.dma_keys.get(key, 0) + 16
            o.dval = self.dma_keys[key]
        self.ops[eng].append(o)
        return o

    def dma(self, queue, out, in_, reads=(), writes=(), key=None):
        return self.op(queue, lambda e: e.dma_start(out=out, in_=in_), reads, writes, dma=True, key=key)

    @staticmethod
    def inherit(new_bufs, old_bufs):
        hz = []
        for b in old_bufs:
            if b.w is not None:
                hz.append(b.w)
            hz.extend(b.r)
        for nb in new_bufs:
            nb.r = list(nb.r) + hz

    def emit(self):
        nc = self.nc
        with ExitStack() as es:
            esem = {e: es.enter_context(nc.semaphore("s_" + e)) for e in ENGS}
            dsem = {k: es.enter_context(nc.semaphore("d%d" % i)) for i, k in enumerate(self.dma_keys)}
            for e in ENGS:
                c = 0
                for o in self.ops[e]:
                    if not o.is_dma and o.marked:
                        c += 1
                        o.sig = c
            block = es.enter_context(nc.Block())

            def run(e, eng):
                waited = {}
                for o in self.ops[e]:
                    need = {}
                    for d in o.deps:
                        if d.is_dma:
                            s, v = dsem[d.key], d.dval
                        else:
                            s, v = esem[d.eng], d.sig
                        if need.get(s, 0) < v:
                            need[s] = v
                    for s, v in need.items():
                        if waited.get(s, 0) < v:
                            eng.wait_ge(s, v)
                            waited[s] = v
                    ins = o.fn(eng)
                    if o.is_dma:
                        ins.then_inc(dsem[o.key], 16)
                    elif o.marked:
                        ins.then_inc(esem[e], 1)
                if e == SP:
                    for k, v in self.dma_keys.items():
                        eng.wait_ge(dsem[k], v)

            @block.tensor
            def _(eng):
                run(PE, eng)

            @block.scalar
            def _(eng):
                run(ACT, eng)

            @block.vector
            def _(eng):
                run(DVE, eng)

            @block.gpsimd
            def _(eng):
                run(POOL, eng)

            @block.sync
            def _(eng):
                run(SP, eng)


def mm_group(mms):
    def fn(e):
        ins = None
        for (o, l, r, st, sp) in mms:
            ins = e.matmul(o, lhsT=l, rhs=r, start=st, stop=sp)
        return ins
    return fn


class Builder:
    def __init__(self, stop_after=99):
        self.stop_after = stop_after
        self.nc = bass.Bass("TRN2", target_bir_lowering=False)
        self.P = Prog(self.nc)
        self.es = ExitStack()
        self.uid = 0

    def din(self, name, shape):
        return self.nc.dram_tensor(name, list(shape), F32, kind="ExternalInput").ap()

    def dout(self, name, shape):
        return self.nc.dram_tensor(name, list(shape), F32, kind="ExternalOutput").ap()

    def sb(self, name, shape, dt=F32):
        return self.es.enter_context(self.nc.sbuf_tensor("sb_" + name, list(shape), dt))

    def buf(self, name):
        self.uid += 1
        return Buf("%s#%d" % (name, self.uid))

    def av(self, off, n, dt=BF16):
        v = self.areg[:, off:off + n]
        if dt == F32:
            v = v.bitcast(F32)
        return v

    def psum(self, hold=False):
        while True:
            k = self.ps_i % 7
            self.ps_i += 1
            if k not in self.ps_hold:
                break
        if hold:
            self.ps_hold.add(k)
        return self.ps[k], self.psb[k]

    def psum_pool(self, name, banks, hold=False):
        while True:
            i = self.ps_pool_i.get(name, 0)
            self.ps_pool_i[name] = i + 1
            k = banks[i % len(banks)]
            if k not in self.ps_hold:
                break
        if hold:
            self.ps_hold.add(k)
        return self.ps[k], self.psb[k]

    def psum_release(self, pb):
        self.ps_hold.discard(self.psb.index(pb))

    def wload(self, src, shape):
        k = self.ring_i % len(self.ring)
        self.ring_i += 1
        npart = shape[0]
        n = int(np.prod(shape[1:]))
        v = self.ring[k][0:npart, 0:n]
        if len(shape) == 3:
            v = v.rearrange("p (a b) -> p a b", b=shape[2])
        elif len(shape) == 4:
            v = v.rearrange("p (a b c) -> p a b c", b=shape[2], c=shape[3])
        b = self.ringb[k]
        self.P.dma(POOL, v, src, writes=[b], key="ring%d" % k)
        return v, b

    def build(self):
        nc, P = self.nc, self.P
        self.d_xp = self.din("xp", [1024, 1024])
        self.d_xs = self.din("xs", [1024, 1024])
        self.d_cvec = self.din("cvec", [128, 8, 2])
        self.d_adaw = self.din("adaw", [2, 18, 128, 8, 512])
        self.d_adab = self.din("adab", [128, 2, 72])
        self.d_normg = self.din("normg", [128, 2, 3, 8])
        self.d_finalg = self.din("finalg", [128, 1024])
        self.d_w1 = self.din("w1t", [2, 2, 11, 128, 8, 512])
        self.d_w2 = self.din("w2t", [2, 2, 4, 2, 128, 11, 256])
        self.d_wq = self.din("wq", [128, 8, 512])
        self.d_wkv = self.din("wkv", [128, 8, 256])
        self.d_wcv = self.din("wcv", [4, 128, 8, 384])
        self.d_wo = self.din("wo", [2, 128, 4, 1024])
        self.d_qg = self.din("qg", [128, 512])
        self.d_kg = self.din("kg", [128, 128])
        self.d_convw = self.din("convw", [128, 4, 3])
        self.d_poolw = self.din("poolw", [128, 4, 2, 256])
        self.d_pscale = self.din("pscale", [128, 8])
        self.d_mpp = self.din("mpp", [128, 2, 4, 256])
        self.d_mps = self.din("mps", [128, 3, 4, 288])
        self.d_ropec = self.din("ropec", [128, 9, 64])
        self.d_ropes = self.din("ropes", [128, 9, 64])
        self.d_cmask = self.din("cmask", [128, 288])
        self.d_ck = self.din("ck", [512, 128])
        self.d_cv = self.din("cv", [512, 128])
        self.o_yp = self.dout("yp", [1024, 1024])
        self.o_ys = self.dout("ys", [256, 1024])
        self.o_nk = self.dout("nk", [1024, 128])
        self.o_nv = self.dout("nv", [1024, 128])

        self.xres = self.sb("xres", [128, 8, NM])
        self.hbuf = self.sb("hbuf", [128, 8, NM], BF16)
        self.areg = self.sb("areg", [128, NJ * NM], BF16)
        self.ring = [self.sb("ring%d" % i, [128, 4096], BF16) for i in range(5)]
        self.ringb = [Buf("ring%d" % i) for i in range(5)]
        self.ring_i = 0
        self.sq = self.sb("sq", [128, 8, 256])
        self.b_sq = Buf("sq")
        self.sq2 = self.sb("sq2", [128, 8, 256])
        self.b_sq2 = Buf("sq2")
        self.rsall = self.sb("rsall", [128, NM])
        self.nrm_i = 0
        self.sg = [self.sb("sg%d" % i, [128, 512]) for i in range(2)]
        self.b_sg = [Buf("sg%d" % i) for i in range(2)]
        self.sg_i = 0
        self.nt = [self.sb("nt%d" % i, [128, 512]) for i in range(2)]
        self.b_nt = [Buf("nt%d" % i) for i in range(2)]
        self.nt_i = 0
        self.ident = self.sb("ident", [128, 128])
        self.ones = self.sb("ones", [128, 128])
        self.onesb = self.sb("onesb", [128, 128], BF16)
        self.b_ident, self.b_ones, self.b_onesb = Buf("ident"), Buf("ones"), Buf("onesb")
        self.cvec = self.sb("cvec", [128, 8, 2])
        self.scb = self.sb("scb", [128, 8, 2], BF16)
        self.adab = self.sb("adab", [128, 2, 72])
        self.normg = self.sb("normg", [128, 2, 3, 8])
        self.finalg = self.sb("finalg", [128, 1024])
        self.qg = self.sb("qg", [128, 512])
        self.kg = self.sb("kg", [128, 128])
        self.convw = self.sb("convw", [128, 4, 3])
        self.pscale = self.sb("pscale", [128, 8])
        self.b_cvec, self.b_scb, self.b_adab, self.b_normg = Buf("cvec"), Buf("scb"), Buf("adab"), Buf("normg")
        self.b_finalg, self.b_qg, self.b_kg, self.b_convw, self.b_pscale = (
            Buf("finalg"), Buf("qg"), Buf("kg"), Buf("convw"), Buf("pscale"))
        self.modsb = [self.sb("modsb%d" % l, [128, 72, 2]) for l in range(2)]
        self.asc = [self.sb("asc%d" % l, [128, 3, 8, 2]) for l in range(2)]
        self.gsc = [self.sb("gsc%d" % l, [128, 3, 8, 2]) for l in range(2)]
        self.b_mod = [[Buf("mod%d_%d" % (l, i)) for i in range(3)] for l in range(2)]
        self.b_gate = [[Buf("gate%d_%d" % (l, i)) for i in range(3)] for l in range(2)]
        self.small = self.sb("small", [128, 64])
        self.b_small = Buf("small")
        self.ps = [self.es.enter_context(nc.psum_tensor("ps%d" % i, [128, 512], F32)) for i in range(8)]
        self.psb = [Buf("ps%d" % i) for i in range(8)]
        self.ps_i = 0
        self.ps_hold = set()
        self.ps_pool_i = {}

        self.ffn_tiles_M = [(0, 512, 0), (512, 512, 0), (1024, 288, 1)]
        self.ffn_tiles_R = [(0, 512, 1), (512, 224, 1)]
        self.sub_M = [(256 * i, 256, 0, i // 2) for i in range(4)] + [(1024, 256, 1, 2), (1280, 32, 1, 2)]
        self.sub_R = [(0, 256, 1, 0), (256, 256, 1, 0), (512, 224, 1, 1)]
        self.tok_M = [(128 * i, 128, i // 4) for i in range(8)] + [(1024, 128, 2), (1152, 128, 2), (1280, 32, 2)]
        self.tok_R = [(128 * i, 128, 0) for i in range(4)] + [(512, 128, 1), (640, 96, 1)]
        self.bx_M = [Buf("xM%d" % i) for i in range(3)]
        self.bh_M = [Buf("hM%d" % i) for i in range(3)]

        self.setup_consts()
        self.mod_q = deque()
        for l in range(2):
            for s in range(18):
                self.mod_q.append((l, s))
        self.mod_done = {}

        stg = [self.sq[:, 0:4, :].rearrange("p a b -> p (a b)"), self.sq[:, 4:8, :].rearrange("p a b -> p (a b)")]
        self.load_x(self.d_xp, 0, self.xres, [(128 * i, 128, 128 * i) for i in range(8)], self.bx_M,
                    [i // 4 for i in range(8)], stg)
        self.load_x(self.d_xs, 0, self.xres, [(0, 128, 1024), (128, 128, 1152), (256, 32, 1280)], self.bx_M,
                    [2, 2, 2], stg)

        a_M = self.av(0, NJ * NM).rearrange("p (j n) -> p j n", n=NM)
        ba_M = [Buf("aM%d" % i) for i in range(3)]
        self.cur_a = ba_M

        for l in range(2):
            if self.stop_after < 10 * l + 1:
                break
            self.norm(self.xres, self.hbuf, self.sub_M, self.bx_M, self.bh_M, l, 0)
            self.ffn(l, 0, self.xres, self.hbuf, a_M, self.ffn_tiles_M, self.bx_M, self.bh_M, ba_M)
            if self.stop_after < 10 * l + 2:
                break
            self.mod_need(l, 1)
            if l == 0:
                self.mixer0(a_M, ba_M)
            else:
                self.norm(self.xres, self.hbuf, self.sub_M, self.bx_M, self.bh_M, l, 1)
                self.mixer1(ba_M)
            if self.stop_after < 10 * l + 3:
                break
            nb = [Buf("aM%d" % i) for i in range(3)]
            Prog.inherit(nb, self.cur_a)
            ba_M = nb
            self.cur_a = nb
            self.mod_need(l, 2)
            self.norm(self.xres, self.hbuf, self.sub_M, self.bx_M, self.bh_M, l, 2)
            self.ffn(l, 1, self.xres, self.hbuf, a_M, self.ffn_tiles_M, self.bx_M, self.bh_M, ba_M)

        self.final_out()
        P.emit()
        self.es.close()
        return nc

    def setup_consts(self):
        P = self.P
        P.dma(SP, self.cvec[:], self.d_cvec[:, :, :], writes=[self.b_cvec])
        P.dma(SP, self.adab[:], self.d_adab[:, :, :], writes=[self.b_adab])
        P.dma(SP, self.normg[:], self.d_normg[:, :, :, :], writes=[self.b_normg])
        P.dma(SP, self.pscale[:], self.d_pscale[:, :], writes=[self.b_pscale])
        P.dma(SP, self.convw[:], self.d_convw[:, :, :], writes=[self.b_convw])
        ident, ones, onesb = self.ident, self.ones, self.onesb
        P.op(DVE, lambda e: e.memset(ident[:], 0.0), writes=[self.b_ident])
        P.op(POOL, lambda e: e.affine_select(out=ident[:], in_=ident[:], pattern=[[-1, 128]],
                                             compare_op=ALU.not_equal, fill=1.0, base=0, channel_multiplier=1),
             reads=[self.b_ident], writes=[self.b_ident])
        P.op(DVE, lambda e: e.memset(ones[:], 1.0), writes=[self.b_ones])
        P.op(DVE, lambda e: e.memset(onesb[:], 1.0), writes=[self.b_onesb])
        cvec, scb = self.cvec, self.scb
        P.op(ACT, lambda e: e.activation(out=scb[:], in_=cvec[:], func=AF.Silu), reads=[self.b_cvec], writes=[self.b_scb])

    def mod_emit(self, l, s):
        P = self.P
        W, bW = self.wload(self.d_adaw[l, s], [128, 8, 512])
        ps7 = self.ps[7][:, 0:144].rearrange("p (m v) -> p m v", v=2)
        mms = []
        for mt in range(4):
            m = 4 * s + mt
            for c in range(8):
                mms.append((ps7[:, m, :], W[:, c, mt * 128:(mt + 1) * 128], self.scb[:, c, :], c == 0, c == 7))
        P.op(PE, mm_group(mms), reads=[bW, self.b_scb], writes=[self.psb[7]])
        i = s // 6
        bm = self.b_mod[l][i]
        modsb, adab, asc, gsc, normg, pscale = self.modsb[l], self.adab, self.asc[l], self.gsc[l], self.normg, self.pscale
        if s % 6 == 3:
            lo, hi = 24 * i, 24 * i + 16
            P.op(DVE, lambda e: e.tensor_tensor(out=modsb[:, lo:hi, :], in0=ps7[:, lo:hi, :],
                                                in1=adab[:, l, lo:hi].unsqueeze(2).broadcast_to([128, 16, 2]), op=ALU.add),
                 reads=[self.psb[7], self.b_adab], writes=[bm])
            P.op(DVE, lambda e: e.tensor_scalar(out=asc[:, i, :, :], in0=modsb[:, lo + 8:lo + 16, :], scalar1=1.0,
                                                scalar2=None, op0=ALU.add), reads=[bm], writes=[bm])
            P.op(DVE, lambda e: e.tensor_tensor(out=asc[:, i, :, :], in0=asc[:, i, :, :],
                                                in1=normg[:, l, i, :].unsqueeze(2).broadcast_to([128, 8, 2]), op=ALU.mult),
                 reads=[bm, self.b_normg], writes=[bm])
            self.mod_done[(l, i)] = True
        if s % 6 == 5:
            bg = self.b_gate[l][i]
            lo, hi = 24 * i + 16, 24 * i + 24
            P.op(DVE, lambda e: e.tensor_tensor(out=modsb[:, lo:hi, :], in0=ps7[:, lo:hi, :],
                                                in1=adab[:, l, lo:hi].unsqueeze(2).broadcast_to([128, 8, 2]), op=ALU.add),
                 reads=[self.psb[7], self.b_adab], writes=[bg])
            if i == 1 and l == 1:
                P.op(DVE, lambda e: e.tensor_tensor(out=gsc[:, i, :, :], in0=modsb[:, lo:hi, :],
                                                    in1=pscale[:, :].unsqueeze(2).broadcast_to([128, 8, 2]), op=ALU.mult),
                     reads=[bg, self.b_pscale], writes=[bg])
            else:
                f = 1.0 if i == 1 else 0.5
                P.op(DVE, lambda e: e.tensor_scalar(out=gsc[:, i, :, :], in0=modsb[:, lo:hi, :], scalar1=f,
                                                    scalar2=None, op0=ALU.mult), reads=[bg], writes=[bg])
            self.mod_done[(l, i, "g")] = True

    def mod_pump(self, n):
        for _ in range(n):
            if not self.mod_q:
                return
            l, s = self.mod_q.popleft()
            self.mod_emit(l, s)

    def mod_need(self, l, i, gate=False):
        key = (l, i, "g") if gate else (l, i)
        while not self.mod_done.get(key):
            self.mod_pump(1)

    def load_x(self, dram, row0, xbuf, tiles, bx, parents, stg):
        P = self.P
        for ti, (r0, npk, c0) in enumerate(tiles):
            k = self.sg_i % 2
            self.sg_i += 1
            st = stg[k]
            P.dma(SP, st[0:npk, :], dram[row0 + r0:row0 + r0 + npk, :], writes=[self.b_sq], key="stg")
            for half in range(2):
                ps, pb = self.psum()
                def tr(e, ps=ps, st=st, half=half, npk=npk):
                    ins = None
                    for j in range(4):
                        c = 4 * half + j
                        ins = e.transpose(out=ps[:, j * 128:j * 128 + npk], in_=st[0:npk, c * 128:(c + 1) * 128],
                                          identity=self.ident[0:npk, 0:npk])
                    return ins
                P.op(PE, tr, reads=[self.b_sq, self.b_ident], writes=[pb])
                src = ps[:, :].rearrange("p (j n) -> p j n", n=128)[:, :, 0:npk]
                dst = xbuf[:, 4 * half:4 * half + 4, c0:c0 + npk]
                P.op(ACT if half == 0 else DVE,
                     (lambda e, dst=dst, src=src: e.activation(out=dst, in_=src, func=AF.Copy)) if half == 0 else
                     (lambda e, dst=dst, src=src: e.tensor_copy(out=dst, in_=src)),
                     reads=[pb], writes=[bx[parents[ti]]])

    def norm(self, xbuf, hbuf, subs, bx, bh, l, i, tiles=None):
        P = self.P
        if tiles is None:
            tiles = self.ffn_tiles_M if len(subs) == len(self.sub_M) else self.ffn_tiles_R
        asc, modsb, bm = self.asc[l], self.modsb[l], self.b_mod[l][i]
        rsall = self.rsall
        b_rs = [Buf("rs_t%d" % t) for t in range(len(tiles))]
        Prog.inherit(b_rs, getattr(self, "b_rs_prev", []))
        self.b_rs_prev = b_rs
        pend = None

        def fin(pd):
            ps, pb, c0, n, par = pd
            P.op(ACT, lambda e: e.activation(out=rsall[:, c0:c0 + n], in_=ps[:, 0:n], func=AF.Sqrt, bias=EPS, scale=1.0 / D),
                 reads=[pb], writes=[b_rs[par]])
            P.op(DVE, lambda e: e.reciprocal(out=rsall[:, c0:c0 + n], in_=rsall[:, c0:c0 + n]),
                 reads=[b_rs[par]], writes=[b_rs[par]])

        for (c0, n, v, par) in subs:
            k = self.nrm_i % 2
            self.nrm_i += 1
            sq, bsq = (self.sq, self.b_sq) if k == 0 else (self.sq2, self.b_sq2)
            P.op(ACT, lambda e, sq=sq, c0=c0, n=n: e.activation(out=sq[:, :, 0:n], in_=xbuf[:, :, c0:c0 + n], func=AF.Square),
                 reads=[bx[par]], writes=[bsq])
            ps, pb = self.psum()
            P.op(PE, mm_group([(ps[:, 0:n], self.ones[:], sq[:, c, 0:n], c == 0, c == 7) for c in range(8)]),
                 reads=[bsq, self.b_ones], writes=[pb])
            if pend is not None:
                fin(pend)
            pend = (ps, pb, c0, n, par)
        fin(pend)
        self.mod_need(l, i)
        for ti, (c0, n, v) in enumerate(tiles):
            for c in range(8):
                k2 = self.nt_i % 2
                self.nt_i += 1
                nt, bnt = self.nt[k2], self.b_nt[k2]
                P.op(DVE, lambda e, nt=nt, c=c, c0=c0, n=n, v=v: e.scalar_tensor_tensor(
                    out=nt[:, 0:n], in0=xbuf[:, c, c0:c0 + n], scalar=asc[:, i, c, v:v + 1], in1=rsall[:, c0:c0 + n],
                    op0=ALU.mult, op1=ALU.mult), reads=[bx[ti], b_rs[ti], bm], writes=[bnt])
                P.op(ACT, lambda e, nt=nt, c=c, c0=c0, n=n, v=v: e.activation(
                    out=hbuf[:, c, c0:c0 + n], in_=nt[:, 0:n], func=AF.Identity,
                    bias=modsb[:, 24 * i + c, v:v + 1], scale=1.0), reads=[bnt, bm], writes=[bh[ti]])

    def ffn(self, l, s, xbuf, hbuf, abuf, tiles, bx, bh, ba):
        P = self.P
        gi = 0 if s == 0 else 2
        gsc, bm = self.gsc[l], self.b_gate[l][gi]
        for sl in range(11):
            W, bW = self.wload(self.d_w1[l, s, sl], [128, 8, 512])
            for ti, (c0, n, v) in enumerate(tiles):
                for jj in range(2):
                    j = 2 * sl + jj
                    pg, bg = self.psum()
                    pu, bu = self.psum()
                    P.op(PE, mm_group([(pg[:, 0:n], W[:, c, jj * 256:jj * 256 + 128], hbuf[:, c, c0:c0 + n], c == 0, c == 7)
                                       for c in range(8)]), reads=[bW, bh[ti]], writes=[bg])
                    P.op(PE, mm_group([(pu[:, 0:n], W[:, c, jj * 256 + 128:jj * 256 + 256], hbuf[:, c, c0:c0 + n], c == 0, c == 7)
                                       for c in range(8)]), reads=[bW, bh[ti]], writes=[bu])
                    k = self.sg_i % 2
                    self.sg_i += 1
                    sg, bsg = self.sg[k], self.b_sg[k]
                    P.op(ACT, lambda e, sg=sg, pg=pg, n=n: e.activation(out=sg[:, 0:n], in_=pg[:, 0:n], func=AF.Silu),
                         reads=[bg], writes=[bsg])
                    P.op(DVE, lambda e, sg=sg, pu=pu, n=n, j=j, c0=c0: e.tensor_tensor(
                        out=abuf[:, j, c0:c0 + n], in0=pu[:, 0:n], in1=sg[:, 0:n], op=ALU.mult),
                        reads=[bu, bsg], writes=[ba[ti]])
            self.mod_pump(2)
        self.mod_need(l, gi, gate=True)
        for g in range(4):
            Wa, bWa = self.wload(self.d_w2[l, s, g, 0], [128, 11, 256])
            Wb, bWb = self.wload(self.d_w2[l, s, g, 1], [128, 11, 256])
            for ti, (c0, n, v) in enumerate(tiles):
                for dd in range(2):
                    d = 2 * g + dd
                    py, by = self.psum()
                    mms = []
                    for j in range(NJ):
                        Wx = Wa if j < 11 else Wb
                        mms.append((py[:, 0:n], Wx[:, j % 11, dd * 128:(dd + 1) * 128], abuf[:, j, c0:c0 + n], j == 0, j == NJ - 1))
                    P.op(PE, mm_group(mms), reads=[bWa, bWb, ba[ti]], writes=[by])
                    P.op(DVE, lambda e, py=py, n=n, d=d, c0=c0, v=v: e.scalar_tensor_tensor(
                        out=xbuf[:, d, c0:c0 + n], in0=py[:, 0:n], scalar=gsc[:, gi, d, v:v + 1], in1=xbuf[:, d, c0:c0 + n],
                        op0=ALU.mult, op1=ALU.add), reads=[by, bm, bx[ti]], writes=[bx[ti]])
            self.mod_pump(1)

    def kv_tile(self, hsrc, c0, npk, bh, Wkv, bWkv, kT, V, bkT, bV, kcol, vt, rope_t, out_row, T, par_=None):
        P = self.P
        if par_ is None:
            par_ = T["i"] % 2
        tA, tB, kst, small, bT, bkst = T["tA"][par_], T["tB"][par_], T["kst"][par_], self.small, T["bT"][par_], T["bkst"][par_]
        bsm = T["bsm"][par_]
        sc = slice(50 + 2 * par_, 52 + 2 * par_)
        T["i"] += 1
        ps, pb = self.psum()
        P.op(PE, mm_group([(ps[0:npk, 0:256], hsrc[:, c, c0:c0 + npk], Wkv[:, c, :], c == 0, c == 7) for c in range(8)]),
             reads=[bWkv, bh], writes=[pb])
        yield
        P.op(ACT, lambda e: e.activation(out=tA[0:npk, 0:128], in_=ps[0:npk, 0:128], func=AF.Square),
             reads=[pb], writes=[bT])
        yield
        P.op(DVE, lambda e: e.tensor_reduce(out=small[0:npk, sc], in_=tA[0:npk, 0:128].rearrange("p (h d) -> p h d", d=64),
                                            axis=AX.X, op=ALU.add), reads=[bT], writes=[bsm])
        yield
        P.op(ACT, lambda e: e.activation(out=small[0:npk, sc], in_=small[0:npk, sc], func=AF.Sqrt, bias=EPS, scale=1.0 / 64),
             reads=[bsm], writes=[bsm])
        yield
        P.op(DVE, lambda e: e.reciprocal(out=small[0:npk, sc], in_=small[0:npk, sc]), reads=[bsm], writes=[bsm])
        yield
        P.op(DVE, lambda e: e.tensor_tensor(out=tB[0:npk, 0:128].rearrange("p (h d) -> p h d", d=64),
                                            in0=ps[0:npk, 0:128].rearrange("p (h d) -> p h d", d=64),
                                            in1=small[0:npk, sc].unsqueeze(2).broadcast_to([npk, 2, 64]), op=ALU.mult),
             reads=[pb, bsm], writes=[bT])
        yield
        P.op(DVE, lambda e: e.tensor_tensor(out=kst[0:npk, 0:128], in0=tB[0:npk, 0:128], in1=self.kg[0:npk, :], op=ALU.mult),
             reads=[bT, self.b_kg], writes=[bkst])
        yield
        P.op(ACT, lambda e: e.activation(out=V[0:npk, vt, :], in_=ps[0:npk, 128:256], func=AF.Copy), reads=[pb], writes=[bV])
        yield
        ksrc = kst
        if out_row is not None:
            P.op(ACT, lambda e: e.activation(out=kst[0:npk, 128:256], in_=ps[0:npk, 128:256], func=AF.Copy), reads=[pb], writes=[bkst])
            yield
            P.dma(SP, self.o_nk[out_row:out_row + npk, :], kst[0:npk, 0:128], reads=[bkst], key=bkst.name + "k")
            yield
            P.dma(SP, self.o_nv[out_row:out_row + npk, :], kst[0:npk, 128:256], reads=[bkst], key=bkst.name + "v")
            yield
        if rope_t is not None:
            yield from self.rope(kst[0:npk, 0:128], tA[0:npk, 0:128], tB[0:npk, 0:128], 2, npk, rope_t, [bkst], bT)
            ksrc = tA
        ps2, pb2 = self.psum()
        P.op(PE, lambda e: e.transpose(out=ps2[:, 0:npk], in_=ksrc[0:npk, 0:128], identity=self.ident[0:npk, 0:npk]),
             reads=[bkst, bT, self.b_ident], writes=[pb2])
        yield
        P.op(ACT, lambda e: e.activation(out=kT[:, kcol:kcol + npk], in_=ps2[:, 0:npk], func=AF.Copy), reads=[pb2], writes=[bkT])
        yield

    def interleave(self, gens, width=2):
        pending = deque(gens)
        active = []
        while pending or active:
            while pending and len(active) < width:
                active.append(pending.popleft())
            for g in list(active):
                try:
                    next(g)
                except StopIteration:
                    active.remove(g)

    def interleave_w(self, gens_w):
        active = [[g, w] for g, w in gens_w]
        while active:
            for gw in list(active):
                g, w = gw
                for _ in range(w):
                    try:
                        next(g)
                    except StopIteration:
                        active.remove(gw)
                        break

    def rope(self, x, t1, t2, H, npk, rt, bx_list, bT, bT2=None):
        P = self.P
        cosf, sinf = self.ropec[0:npk, rt, :], self.ropes[0:npk, rt, :]
        v5 = lambda a: a.rearrange("p (h r f s) -> p h r f s", h=H, r=2, f=2, s=16)
        c4 = cosf.rearrange("p (r f s) -> p r f s", r=2, f=2, s=16)
        s4 = sinf.rearrange("p (r f s) -> p r f s", r=2, f=2, s=16)
        P.op(DVE, lambda e: e.tensor_tensor(out=v5(t1), in0=v5(x), in1=c4.unsqueeze(1).broadcast_to([npk, H, 2, 2, 16]), op=ALU.mult),
             reads=bx_list + [self.b_rope], writes=[bT])
        yield
        for f in range(2):
            P.op(DVE, lambda e, f=f: e.tensor_tensor(out=v5(t2)[:, :, :, f, :], in0=v5(x)[:, :, :, 1 - f, :],
                                                     in1=s4[:, :, f, :].unsqueeze(1).broadcast_to([npk, H, 2, 16]), op=ALU.mult),
                 reads=bx_list + [self.b_rope], writes=[bT2 or bT])
            yield
        P.op(DVE, lambda e: e.tensor_tensor(out=t1, in0=t1, in1=t2, op=ALU.add), reads=[bT, bT2 or bT], writes=[bT])
        yield

    def mixer0(self, a_M, ba_M):
        P = self.P
        self.ropec = self.sb("ropec", [128, 9, 64])
        self.ropes = self.sb("ropes", [128, 9, 64])
        self.cmask = self.sb("cmask", [128, NS_COLS])
        self.b_rope, self.b_cmask = Buf("rope"), Buf("cmask")
        P.dma(SP, self.ropec[:], self.d_ropec[:, :, :], writes=[self.b_rope], key="ropec")
        P.dma(SP, self.ropes[:], self.d_ropes[:, :, :], writes=[self.b_rope], key="ropes")
        P.dma(SP, self.cmask[:], self.d_cmask[:, :], writes=[self.b_cmask])
        P.dma(SP, self.qg[:], self.d_qg[:, :], writes=[self.b_qg])
        P.dma(SP, self.kg[:], self.d_kg[:, :], writes=[self.b_kg])
        aR = self.av(0, NJ * NR).rearrange("p (j n) -> p j n", n=NR)
        xR = self.av(NJ * NR, 2 * 8 * NR, F32).rearrange("p (c n) -> p c n", n=NR)
        hR = self.hbuf[:, :, 0:NR]
        b_aR = [Buf("aR%d" % i) for i in range(2)]
        b_xR = [Buf("xR%d" % i) for i in range(2)]
        Prog.inherit(b_aR + b_xR, ba_M)
        bhR = [Buf("hR0"), Buf("hR1")]
        Prog.inherit(bhR, self.bh_M)
        stg = [self.sq[:, 0:4, :].rearrange("p a b -> p (a b)"), self.sq[:, 4:8, :].rearrange("p a b -> p (a b)")]
        self.load_x(self.d_xs, NS_COLS, xR, [(c0, npk, c0) for (c0, npk, par) in self.tok_R], b_xR,
                    [par for (c0, npk, par) in self.tok_R], stg)
        self.norm(xR, hR, self.sub_R, b_xR, bhR, 0, 0)
        self.ffn(0, 0, xR, hR, aR, self.ffn_tiles_R, b_xR, bhR, b_aR)
        self.norm(xR, hR, self.sub_R, b_xR, bhR, 0, 1)

        kT = self.av(0, 2560)
        V = self.av(2560, 21 * 128).rearrange("p (t f) -> p t f", f=128)
        b_kT = [Buf("kT_p%d" % i) for i in range(4)] + [Buf("kT_s")]
        b_V = [Buf("V_p%d" % i) for i in range(4)] + [Buf("V_s")]
        Prog.inherit(b_kT + b_V, b_aR)
        T = {"tA": [self.sg[0][:, 0:128], self.sg[0][:, 256:384]], "tB": [self.sg[0][:, 128:256], self.sg[0][:, 384:512]],
             "kst": [self.sg[1][:, 0:256], self.sg[1][:, 256:512]], "i": 0,
             "bT": [Buf("kvT0"), Buf("kvT1")], "bkst": [Buf("kst0"), Buf("kst1")], "bsm": [Buf("smk0"), Buf("smk1")]}
        Prog.inherit(T["bT"] + T["bkst"], self.b_sg)
        Wkv, bWkv = self.wload(self.d_wkv[:, :, :], [128, 8, 256])
        self.interleave([self.kv_tile(hR, c0, npk, bhR[par], Wkv, bWkv, kT, V, b_kT[4], b_V[4], 1824 + c0, 15 + i, 3 + i, None, T)
                         for i, (c0, npk, par) in enumerate(self.tok_R)])
        Prog.inherit(self.bh_M, bhR)
        self.norm(self.xres, self.hbuf, self.sub_M, self.bx_M, self.bh_M, 0, 1)
        qT = self.av(5248, 4 * NM).rearrange("p (s n) -> p s n", n=NM)
        attnT = self.av(10496, 4 * NM).rearrange("p (s n) -> p s n", n=NM)
        convT = self.av(15744, 4 * NM).rearrange("p (s n) -> p s n", n=NM)
        pT = [self.av(20992 + 512 * k, 512) for k in range(3)]
        T1s = [self.av(22528, 1024, F32), self.av(25600, 1024, F32)]
        T2s = [self.av(23552, 1024, F32), self.av(20992, 1024, F32)]
        T3s = [self.av(24576, 1024, F32), self.av(15744, 1024, F32)]
        rd = self.av(25600, 1024, F32)
        ckst = self.av(26624, 1024, F32).rearrange("p (t f) -> p t f", f=128)
        b_qT = [Buf("qT_%d" % i) for i in range(5)]
        b_attnT = [Buf("attnT%d" % i) for i in range(3)]
        b_convT = Buf("convT")
        b_pT = [Buf("pT%d" % k) for k in range(3)]
        b_T1s, b_T3s, b_rd, b_ckst = [Buf("T1a"), Buf("T1b")], [Buf("T3a"), Buf("T3b")], Buf("rd"), Buf("ckst")
        b_smq = [Buf("smq0"), Buf("smq1")]
        allnew = b_qT + b_attnT + [b_convT] + b_pT + b_T1s + b_T3s + [b_rd, b_ckst]
        Prog.inherit(allnew, b_aR + b_xR)
        P.dma(SP, ckst[:, :, :], self.d_ck.rearrange("(t p) f -> p t f", p=128), writes=[b_ckst])
        P.dma(POOL, V[:, 8:12, :], self.d_cv.rearrange("(t p) f -> p t f", p=128), writes=[b_V[4]], key="cvload")
        for t in range(4):
            ps2, pb2 = self.psum()
            P.op(PE, lambda e, t=t, ps2=ps2: e.transpose(out=ps2[:, 0:128], in_=ckst[:, t, :], identity=self.ident[:]),
                 reads=[b_ckst, self.b_ident], writes=[pb2])
            P.op(ACT, lambda e, t=t, ps2=ps2: e.activation(out=kT[:, 1024 + 128 * t:1152 + 128 * t], in_=ps2[:, 0:128], func=AF.Copy),
                 reads=[pb2], writes=[b_kT[4]])
        Wq, bWq = self.wload(self.d_wq[:, :, :], [128, 8, 512])
        def mtile(i, c0, npk, par):
            is_s = i >= 8
            bi = 4 if is_s else i // 2
            if is_s:
                kcol, vt, rt, orow = 1536 + (c0 - 1024), 12 + (i - 8), (i - 8), None
            else:
                kcol, vt, rt, orow = c0, i, None, c0
            yield from self.kv_tile(self.hbuf, c0, npk, self.bh_M[par], Wkv, bWkv, kT, V, b_kT[bi], b_V[bi], kcol, vt, rt, orow, T, par_=i % 2)
            qp = i % 2
            T1, T2, b_T1, bsmq = T1s[qp], T2s[qp], b_T1s[qp], b_smq[qp]
            qc = slice(34 + 8 * qp, 42 + 8 * qp)
            ps, pb = self.psum()
            P.op(PE, mm_group([(ps[0:npk, 0:512], self.hbuf[:, c, c0:c0 + npk], Wq[:, c, :], c == 0, c == 7) for c in range(8)]),
                 reads=[bWq, self.bh_M[par]], writes=[pb])
            yield
            small = self.small
            v3 = lambda a: a.rearrange("p (h d) -> p h d", d=64)
            P.op(ACT, lambda e, ps=ps, npk=npk, T1=T1: e.activation(out=T1[0:npk, :], in_=ps[0:npk, :], func=AF.Square),
                 reads=[pb], writes=[b_T1])
            yield
            P.op(DVE, lambda e, npk=npk, T1=T1, qc=qc: e.tensor_reduce(out=small[0:npk, qc], in_=v3(T1[0:npk, :]), axis=AX.X, op=ALU.add),
                 reads=[b_T1], writes=[bsmq])
            yield
            P.op(ACT, lambda e, npk=npk, qc=qc: e.activation(out=small[0:npk, qc], in_=small[0:npk, qc], func=AF.Sqrt, bias=EPS,
                                                      scale=1.0 / 64), reads=[bsmq], writes=[bsmq])
            yield
            P.op(DVE, lambda e, npk=npk, qc=qc: e.reciprocal(out=small[0:npk, qc], in_=small[0:npk, qc]),
                 reads=[bsmq], writes=[bsmq])
            yield
            P.op(DVE, lambda e, ps=ps, npk=npk, T2=T2, qc=qc: e.tensor_tensor(out=v3(T2[0:npk, :]), in0=v3(ps[0:npk, :]),
                                                                in1=small[0:npk, qc].unsqueeze(2).broadcast_to([npk, 8, 64]),
                                                                op=ALU.mult), reads=[pb, bsmq], writes=[b_T1])
            yield
            P.op(DVE, lambda e, npk=npk, T1=T1, T2=T2: e.tensor_tensor(out=T1[0:npk, :], in0=T2[0:npk, :], in1=self.qg[0:npk, :], op=ALU.mult),
                 reads=[b_T1, self.b_qg], writes=[b_T1])
            yield
            qsrc = T1
            if is_s:
                yield from self.rope(T1[0:npk, :], T2[0:npk, :], T3s[qp][0:npk, :], 8, npk, rt, [b_T1], b_T1, b_T3s[qp])
                qsrc = T2
            pst, pbt = self.psum()
            def trq(e, pst=pst, qsrc=qsrc, npk=npk):
                ins = None
                for s4 in range(4):
                    ins = e.transpose(out=pst[:, s4 * 128:s4 * 128 + npk], in_=qsrc[0:npk, s4 * 128:(s4 + 1) * 128],
                                      identity=self.ident[0:npk, 0:npk])
                return ins
            P.op(PE, trq, reads=[b_T1, self.b_ident], writes=[pbt])
            yield
            P.op(ACT, lambda e, pst=pst, npk=npk, c0=c0: e.activation(
                out=qT[:, :, c0:c0 + npk], in_=pst[:, :].rearrange("p (s n) -> p s n", n=128)[:, :, 0:npk], func=AF.Copy),
                reads=[pbt], writes=[b_qT[bi]])
            yield


        self.interleave([mtile(i, c0, npk, par) for i, (c0, npk, par) in enumerate(self.tok_M)])

        Prog.inherit(b_pT + [b_rd], b_T1s)
        Prog.inherit([b_convT], b_T3s)
        s_chunks = [(1024 + 128 * t, 128, 8 + t) for t in range(4)] + [(1536, 128, 12), (1664, 128, 13), (1792, 32, 14)] + \
                   [(1824 + 128 * i, 128, 15 + i) for i in range(5)] + [(2464, 96, 20)]
        groups = []
        for bi in range(4):
            for hh in range(2):
                for sp in range(2):
                    groups.append((bi, hh, (2 * sp, 2 * sp + 2), bi * 256, 256,
                                   [(bi * 256 + 128 * kc, 128, 2 * bi + kc) for kc in range(2)], b_attnT[bi // 2]))
        for hh in range(2):
            for s4 in range(4):
                groups.append((4, hh, (s4, s4 + 1), 1024, NS_COLS, s_chunks, b_attnT[2]))
        rounds = [(g, ci) for g in range(len(groups)) for ci in range(len(groups[g][5]))]
        st = {}

        def emit_s(ri):
            g, ci = rounds[ri]
            bi, hh, (s0, s1), qc0, qn, chunks, bat = groups[g]
            kcol, npk, vt = chunks[ci]
            ps, pb = self.psum_pool("attn", (3, 4, 5, 6))
            ncol = (s1 - s0) * qn
            hs = slice(hh * 64, hh * 64 + 64)
            P.op(PE, lambda e: e.matmul(ps[0:npk, 0:ncol], lhsT=kT[hs, kcol:kcol + npk], rhs=qT[hs, s0:s1, qc0:qc0 + qn],
                                        start=True, stop=True), reads=[b_kT[bi], b_qT[bi]], writes=[pb])
            st[ri] = (ps, pb, ncol)

        def attn_gen():
            emit_s(0)
            yield
            acc = {}
            for ri in range(len(rounds)):
                if ri + 1 < len(rounds):
                    emit_s(ri + 1)
                    yield
                g, ci = rounds[ri]
                bi, hh, (s0, s1), qc0, qn, chunks, bat = groups[g]
                kcol, npk, vt = chunks[ci]
                ps, pb, ncol = st.pop(ri)
                k = ri % 3
                p, bp = pT[k], b_pT[k]
                P.op(ACT, lambda e, p=p, ps=ps, npk=npk, ncol=ncol: e.activation(out=p[0:npk, 0:ncol], in_=ps[0:npk, 0:ncol],
                                                                                 func=AF.Exp, scale=0.125), reads=[pb], writes=[bp])
                yield
                if ci == 0:
                    acc[g] = (self.psum_pool("attn", (3, 4, 5, 6), hold=True), self.psum_pool("attn", (3, 4, 5, 6), hold=True))
                (pn, bn), (pd, bd) = acc[g]
                last = ci == len(chunks) - 1
                P.op(PE, lambda e, pn=pn, p=p, npk=npk, ncol=ncol, vt=vt, ci=ci, last=last: e.matmul(
                    pn[:, 0:ncol], lhsT=V[0:npk, vt, :], rhs=p[0:npk, 0:ncol], start=(ci == 0), stop=last),
                    reads=[bp, b_V[bi]], writes=[bn])
                yield
                P.op(PE, lambda e, pd=pd, p=p, npk=npk, ncol=ncol, ci=ci, last=last: e.matmul(
                    pd[:, 0:ncol], lhsT=self.onesb[0:npk, :], rhs=p[0:npk, 0:ncol], start=(ci == 0), stop=last),
                    reads=[bp, self.b_onesb], writes=[bd])
                yield
                if last:
                    hs = slice(hh * 64, hh * 64 + 64)
                    P.op(ACT, lambda e, pd=pd, hs=hs, ncol=ncol: e.activation(out=rd[hs, 0:ncol], in_=pd[hs, 0:ncol], func=AF.Ln),
                         reads=[bd], writes=[b_rd])
                    yield
                    P.op(ACT, lambda e, hs=hs, ncol=ncol: e.activation(out=rd[hs, 0:ncol], in_=rd[hs, 0:ncol], func=AF.Exp, scale=-1.0),
                         reads=[b_rd], writes=[b_rd])
                    yield
                    P.op(DVE, lambda e, pn=pn, hs=hs, ncol=ncol, s0=s0, s1=s1, qc0=qc0, qn=qn: e.tensor_tensor(
                        out=attnT[hs, s0:s1, qc0:qc0 + qn], in0=pn[hs, 0:ncol].rearrange("p (s n) -> p s n", n=qn),
                        in1=rd[hs, 0:ncol].rearrange("p (s n) -> p s n", n=qn), op=ALU.mult),
                        reads=[bn, b_rd], writes=[bat])
                    yield
                    self.psum_release(bn)
                    self.psum_release(bd)
                    del acc[g]


        def conv_gen():
            upad = self.sq[:, :, :].rearrange("p a b -> p (a b)")
            cacc = self.sq2[:, :, :].rearrange("p a b -> p (a b)")
            bgs = self.rsall
            xcs = [self.nt[0], self.nt[1]]
            b_upad, b_cacc, b_bgs, b_xcs = self.b_sq, self.b_sq2, Buf("bgs"), self.b_nt
            Prog.inherit([b_bgs], getattr(self, "b_rs_prev", []))
            self.b_rs_prev = [b_bgs]
            P.op(DVE, lambda e: e.memset(upad[:, :], 0.0), writes=[b_upad])
            yield
            cw = self.convw
            for cc in range(4):
                Wc, bWc = self.wload(self.d_wcv[cc], [128, 8, 384])
                for ti, (c0, n, v) in enumerate(self.ffn_tiles_M):
                    pp = [self.psum_pool("conv", (0, 1, 2)) for _ in range(3)]
                    for q3 in range(3):
                        P.op(PE, mm_group([(pp[q3][0][:, 0:n], Wc[:, c, q3 * 128:(q3 + 1) * 128], self.hbuf[:, c, c0:c0 + n], c == 0, c == 7)
                                           for c in range(8)]), reads=[bWc, self.bh_M[ti]], writes=[pp[q3][1]])
                        yield
                    k = (cc * 3 + ti) % 2
                    xc_, bxc = xcs[k], b_xcs[k]
                    P.op(ACT, lambda e, xc_=xc_, n=n, px=pp[2][0]: e.activation(out=xc_[:, 0:n], in_=px[:, 0:n], func=AF.Copy),
                         reads=[pp[2][1]], writes=[bxc])
                    yield
                    if ti < 2:
                        uo = upad[:, 1 + 2 * ti * 257:1 + (2 * ti + 2) * 257].rearrange("p (b k) -> p b k", k=257)[:, :, 0:256]
                        P.op(DVE, lambda e, uo=uo, pc=pp[1][0], xc_=xc_: e.tensor_tensor(
                            out=uo, in0=pc[:, 0:512].rearrange("p (b k) -> p b k", k=256),
                            in1=xc_[:, 0:512].rearrange("p (b k) -> p b k", k=256), op=ALU.mult),
                            reads=[pp[1][1], bxc], writes=[b_upad])
                        yield
                    else:
                        P.op(DVE, lambda e, pc=pp[1][0], xc_=xc_: e.tensor_tensor(out=upad[:, 1029:1317], in0=pc[:, 0:NS_COLS],
                                                                                  in1=xc_[:, 0:NS_COLS], op=ALU.mult),
                             reads=[pp[1][1], bxc], writes=[b_upad])
                        yield
                    P.op(ACT, lambda e, n=n, c0=c0, pb_=pp[0][0]: e.activation(out=bgs[:, c0:c0 + n], in_=pb_[:, 0:n], func=AF.Copy),
                         reads=[pp[0][1]], writes=[b_bgs])
                    yield
                P.op(DVE, lambda e: e.tensor_tensor(out=upad[:, 1029:1317], in0=upad[:, 1029:1317], in1=self.cmask[:, :], op=ALU.mult),
                     reads=[b_upad, self.b_cmask], writes=[b_upad])
                yield
                P.op(DVE, lambda e, cc=cc: e.tensor_scalar(out=cacc[:, 1:1317], in0=upad[:, 1:1317], scalar1=cw[:, cc, 1:2], scalar2=None,
                                                           op0=ALU.mult), reads=[b_upad, self.b_convw], writes=[b_cacc])
                yield
                P.op(DVE, lambda e, cc=cc: e.scalar_tensor_tensor(out=cacc[:, 1:1317], in0=upad[:, 0:1316], scalar=cw[:, cc, 0:1],
                                                                  in1=cacc[:, 1:1317], op0=ALU.mult, op1=ALU.add),
                     reads=[b_upad, self.b_convw, b_cacc], writes=[b_cacc])
                yield
                P.op(DVE, lambda e, cc=cc: e.scalar_tensor_tensor(out=cacc[:, 1:1317], in0=upad[:, 2:1318], scalar=cw[:, cc, 2:3],
                                                                  in1=cacc[:, 1:1317], op0=ALU.mult, op1=ALU.add),
                     reads=[b_upad, self.b_convw, b_cacc], writes=[b_cacc])
                yield
                P.op(DVE, lambda e, cc=cc: e.tensor_tensor(
                    out=convT[:, cc, 0:1024].rearrange("p (b k) -> p b k", k=256),
                    in0=cacc[:, 1:1029].rearrange("p (b k) -> p b k", k=257)[:, :, 0:256],
                    in1=bgs[:, 0:1024].rearrange("p (b k) -> p b k", k=256), op=ALU.mult),
                    reads=[b_cacc, b_bgs], writes=[b_convT])
                yield
                P.op(DVE, lambda e, cc=cc: e.tensor_tensor(out=convT[:, cc, 1024:NM], in0=cacc[:, 1029:1317], in1=bgs[:, 1024:NM],
                                                           op=ALU.mult), reads=[b_cacc, b_bgs], writes=[b_convT])
                yield


        for _ in attn_gen():
            pass
        for _ in conv_gen():
            pass

        Woa, bWoa = self.wload(self.d_wo[0], [128, 4, 1024])
        Woc, bWoc = self.wload(self.d_wo[1], [128, 4, 1024])
        self.mod_need(0, 1, gate=True)
        gsc, bm = self.gsc[0], self.b_gate[0][1]
        for ti, (c0, n, v) in enumerate(self.ffn_tiles_M):
            for d in range(8):
                po, bpo = self.psum()
                mms = [(po[:, 0:n], Woa[:, s4, d * 128:(d + 1) * 128], attnT[:, s4, c0:c0 + n], s4 == 0, False) for s4 in range(4)]
                mms += [(po[:, 0:n], Woc[:, s4, d * 128:(d + 1) * 128], convT[:, s4, c0:c0 + n], False, s4 == 3) for s4 in range(4)]
                P.op(PE, mm_group(mms), reads=[bWoa, bWoc, b_attnT[ti], b_convT], writes=[bpo])
                P.op(DVE, lambda e, po=po, n=n, d=d, c0=c0, v=v: e.scalar_tensor_tensor(
                    out=self.xres[:, d, c0:c0 + n], in0=po[:, 0:n], scalar=gsc[:, 1, d, v:v + 1], in1=self.xres[:, d, c0:c0 + n],
                    op0=ALU.mult, op1=ALU.add), reads=[bpo, bm, self.bx_M[ti]], writes=[self.bx_M[ti]])
        self.cur_a = allnew + b_kT + b_V + b_aR + b_xR
        Prog.inherit(self.b_sg, T["bT"] + T["bkst"])

    def kv_tile_R(self, hR, c0, npk, bh, Wkv, bWkv, kT, V, bkT, bV, kcol, vt, ridx, T):
        self.kv_tile(hR, c0, npk, bh, Wkv, bWkv, kT, V, bkT, bV, kcol, vt, ("R", c0 // 128), None, T)

    def mixer1(self, ba_M):
        P = self.P
        z = self.av(0, 11 * 1024).rearrange("p (t f) -> p t f", f=1024)
        mpp = self.av(11264, 2048).rearrange("p (a g t) -> p a g t", g=4, t=256)
        mps = self.av(13312, 3456).rearrange("p (a g t) -> p a g t", g=4, t=NS_COLS)
        b_z = [Buf("z%d" % i) for i in range(5)]
        b_mpp, b_mps = Buf("mpp"), Buf("mps")
        Prog.inherit(b_z + [b_mpp, b_mps], self.cur_a)
        P.dma(POOL, mpp, self.d_mpp[:, :, :, :], writes=[b_mpp])
        P.dma(POOL, mps, self.d_mps[:, :, :, :], writes=[b_mps])
        Wp, bWp = self.wload(self.d_poolw[:, :, :, :], [128, 4, 2, 256])
        for tt, (c0, npk, par) in enumerate(self.tok_M):
            bz = b_z[4 if tt >= 8 else tt // 2]
            for half in range(2):
                ps, pb = self.psum()
                mms = []
                for g2 in range(2):
                    gi = 2 * half + g2
                    for kc in range(2):
                        mms.append((ps[0:npk, g2 * 256:(g2 + 1) * 256], self.hbuf[:, 2 * gi + kc, c0:c0 + npk], Wp[:, gi, kc, :],
                                    kc == 0, kc == 1))
                P.op(PE, mm_group(mms), reads=[bWp, self.bh_M[par]], writes=[pb])
                dst = z[0:npk, tt, half * 512:(half + 1) * 512]
                if half == 0:
                    P.op(ACT, lambda e, dst=dst, ps=ps, npk=npk: e.activation(out=dst, in_=ps[0:npk, :], func=AF.Copy),
                         reads=[pb], writes=[bz])
                else:
                    P.op(DVE, lambda e, dst=dst, ps=ps, npk=npk: e.tensor_copy(out=dst, in_=ps[0:npk, :]), reads=[pb], writes=[bz])
        self.mod_need(1, 1, gate=True)
        gsc, bm = self.gsc[1], self.b_gate[1][1]
        segs = [(bi, [(2 * bi, 128), (2 * bi + 1, 128)], 256, bi * 256, mpp, b_mpp, 0, bi // 2) for bi in range(4)]
        segs.append((4, [(8, 128), (9, 128), (10, 32)], NS_COLS, 1024, mps, b_mps, 1, 2))
        for (bi, stiles, Tn, c0, Mx, bMx, v, par) in segs:
            for fc in range(8):
                gi = fc // 2
                po, bpo = self.psum()
                mms = [(po[:, 0:Tn], z[0:npk, tt, fc * 128:(fc + 1) * 128], Mx[0:npk, sc, gi, 0:Tn], sc == 0, sc == len(stiles) - 1)
                       for sc, (tt, npk) in enumerate(stiles)]
                P.op(PE, mm_group(mms), reads=[b_z[bi], bMx], writes=[bpo])
                P.op(DVE, lambda e, po=po, Tn=Tn, fc=fc, c0=c0, v=v: e.scalar_tensor_tensor(
                    out=self.xres[:, fc, c0:c0 + Tn], in0=po[:, 0:Tn], scalar=gsc[:, 1, fc, v:v + 1], in1=self.xres[:, fc, c0:c0 + Tn],
                    op0=ALU.mult, op1=ALU.add), reads=[bpo, bm, self.bx_M[par]], writes=[self.bx_M[par]])
        self.cur_a = b_z + [b_mpp, b_mps]

    def final_out(self):
        P = self.P
        P.dma(SP, self.finalg[:], self.d_finalg[:, :], writes=[self.b_finalg])
        ot = [self.av(4096 * k, 2048, F32) for k in range(2)]
        jk = [self.av(8192 + 4096 * k, 2048, F32) for k in range(2)]
        b_ot = [Buf("ot0"), Buf("ot1")]
        b_jk = [Buf("jk0"), Buf("jk1")]
        b_sm = [Buf("smf0"), Buf("smf1")]
        Prog.inherit(b_ot + b_jk, self.cur_a)
        Prog.inherit(b_sm, [self.b_small])
        outs = [(128 * i, self.o_yp, 128 * i, i // 4) for i in range(8)] + \
               [(1024 + HALO + 128 * i, self.o_ys, 128 * i, 2) for i in range(2)]
        sm = self.small

        def otile(oi, c0, dram, r0, par):
            k = oi % 2
            o, bo, j, bj, bs = ot[k], b_ot[k], jk[k], b_jk[k], b_sm[k]
            for half in range(2):
                ps, pb = self.psum()

                def tr(e, ps=ps, half=half):
                    ins = None
                    for jj in range(4):
                        c = 4 * half + jj
                        ins = e.transpose(out=ps[:, jj * 128:(jj + 1) * 128], in_=self.xres[:, c, c0:c0 + 128],
                                          identity=self.ident[:])
                    return ins
                P.op(PE, tr, reads=[self.bx_M[par], self.b_ident], writes=[pb])
                yield
                P.op(ACT, lambda e, ps=ps, half=half: e.activation(out=o[:, half * 512:(half + 1) * 512], in_=ps[:, :],
                                                                   func=AF.Copy), reads=[pb], writes=[bo])
                yield
            P.op(ACT, lambda e: e.activation(out=j[:, :], in_=o[:, :], func=AF.Square, accum_out=sm[:, oi:oi + 1]),
                 reads=[bo], writes=[bj, bs])
            yield
            P.op(ACT, lambda e: e.activation(out=sm[:, oi:oi + 1], in_=sm[:, oi:oi + 1], func=AF.Sqrt, bias=EPS,
                                             scale=1.0 / D), reads=[bs], writes=[bs])
            yield
            P.op(DVE, lambda e: e.reciprocal(out=sm[:, oi:oi + 1], in_=sm[:, oi:oi + 1]), reads=[bs], writes=[bs])
            yield
            P.op(DVE, lambda e: e.scalar_tensor_tensor(out=o[:, :], in0=o[:, :], scalar=sm[:, oi:oi + 1],
                                                       in1=self.finalg[:, :], op0=ALU.mult, op1=ALU.mult),
                 reads=[bo, bs, self.b_finalg], writes=[bo])
            yield
            P.dma(SP, dram[r0:r0 + 128, :], o[:, :], reads=[bo], key="ot%d" % k)
            yield

        self.interleave([otile(oi, *t) for oi, t in enumerate(outs)])


def _host_inputs(inp):
    f = lambda a: np.ascontiguousarray(np.asarray(a, dtype=np.float32))
    x_prompt, x_sample, c, c_ctx = f(inp["x_prompt"]), f(inp["x_sample"]), f(inp["c"]), f(inp["c_ctx"])
    ada_w, ada_b, norm_g = f(inp["ada_w"]), f(inp["ada_b"]), f(inp["norm_g"])
    w1, w2 = f(inp["ffn_w1"]), f(inp["ffn_w2"])
    shared = {}
    shared["adaw"] = f(ada_w.reshape(2, 8, 128, 18, 512).transpose(0, 3, 2, 1, 4))
    shared["adab"] = f(ada_b.reshape(2, 72, 128).transpose(2, 0, 1))
    shared["normg"] = f(norm_g.reshape(2, 3, 8, 128).transpose(3, 0, 1, 2))
    shared["finalg"] = f(np.broadcast_to(f(inp["final_g"])[None, :], (128, 1024)))
    g = w1[..., :DFF].reshape(2, 2, 8, 128, 11, 2, 128)
    u = w1[..., DFF:].reshape(2, 2, 8, 128, 11, 2, 128)
    gu = np.stack([g, u], axis=6)
    shared["w1t"] = f(gu.transpose(0, 1, 4, 3, 2, 5, 6, 7).reshape(2, 2, 11, 128, 8, 512))
    w2r = w2.reshape(2, 2, 2, 11, 128, 4, 256)
    shared["w2t"] = f(w2r.transpose(0, 1, 5, 2, 4, 3, 6))
    return shared, x_prompt, x_sample, c, c_ctx


_NC_CACHE = {}


def _get_nc(stop_after=99):
    if stop_after not in _NC_CACHE:
        _NC_CACHE[stop_after] = Builder(stop_after).build()
    return _NC_CACHE[stop_after]


HEAD_PERM = [0, 4, 1, 5, 2, 6, 3, 7]


def _pool_matrix(gpos, S_total):
    n = len(gpos)
    out = np.zeros((4, n, n), np.float32)
    for gi, w in enumerate(POOL_WINDOWS):
        left = w // 2
        right = w - 1 - left
        for t in range(n):
            gt = gpos[t]
            if gt < 0 or gt >= S_total:
                continue
            lo = max(gt - left, 0)
            hi = min(gt + right + 1, S_total)
            inv = np.float32(1.0) / np.float32(hi - lo)
            for s in range(n):
                if lo <= gpos[s] < hi:
                    out[gi, s, t] += inv
            out[gi, t, t] -= 1.0
    return out


def _rope_tables(gtok):
    half = 32
    inv = 10000.0 ** (-np.arange(0, half, 2, dtype=np.float64) / half)
    row = (gtok // GRID_W).astype(np.float64)
    col = (gtok % GRID_W).astype(np.float64)
    ar = row[:, None] * inv[None, :]
    ac = col[:, None] * inv[None, :]
    cr, sr, cc, sc = np.cos(ar), np.sin(ar), np.cos(ac), np.sin(ac)
    cosf = np.concatenate([cr, cr, cc, cc], axis=1).astype(np.float32)
    sinf = np.concatenate([-sr, sr, -sc, sc], axis=1).astype(np.float32)
    return cosf, sinf


def make_in_maps(inp, cores=range(8)):
    f = lambda a: np.ascontiguousarray(np.asarray(a, dtype=np.float32))
    shared, x_prompt, x_sample, c, c_ctx = _host_inputs(inp)
    w_in, w_out = f(inp["mix_w_in"])[0], f(inp["mix_w_out"])[0]
    wq = w_in[:, 0:512].reshape(1024, 8, 64)[:, HEAD_PERM, :].reshape(1024, 512)
    shared["wq"] = f(wq.reshape(8, 128, 512).transpose(1, 0, 2))
    shared["wkv"] = f(w_in[:, 512:768].reshape(8, 128, 256).transpose(1, 0, 2))
    bg, cg, xc = w_in[:, 768:1280], w_in[:, 1280:1792], w_in[:, 1792:2304]
    wcv = np.stack([np.concatenate([bg[:, k * 128:(k + 1) * 128], cg[:, k * 128:(k + 1) * 128],
                                    xc[:, k * 128:(k + 1) * 128]], axis=1) for k in range(4)], axis=0)
    shared["wcv"] = f(wcv.reshape(4, 8, 128, 384).transpose(0, 2, 1, 3))
    wo_a = w_out[0:512].reshape(8, 64, 1024)
    wo_a = np.stack([np.concatenate([wo_a[s], wo_a[4 + s]], axis=0) for s in range(4)], axis=1)
    wo_c = w_out[512:1024].reshape(4, 128, 1024).transpose(1, 0, 2)
    shared["wo"] = f(np.stack([wo_a, wo_c], axis=0))
    shared["qg"] = f(np.broadcast_to(np.tile(f(inp["q_norm"])[0], 8)[None, :], (128, 512)))
    shared["kg"] = f(np.broadcast_to(np.tile(f(inp["k_norm"])[0], 2)[None, :], (128, 128)))
    shared["convw"] = f(f(inp["conv_w"])[0].T.reshape(4, 128, 3).transpose(1, 0, 2))
    shared["poolw"] = f(f(inp["pool_w"])[0].reshape(4, 2, 128, 256).transpose(2, 0, 1, 3))
    shared["pscale"] = f(f(inp["pool_scale"])[0].reshape(8, 128).T)
    mp = _pool_matrix(np.arange(256), 256)
    shared["mpp"] = f(mp.reshape(4, 2, 128, 256).transpose(2, 1, 0, 3))
    cache_k, cache_v = f(inp["cache_k"]), f(inp["cache_v"])
    maps = []
    for k in cores:
        b, r = k // 4, k % 4
        gwin = 256 * r - HALO + np.arange(1024)
        idx = gwin % 1024
        m = dict(shared)
        m["xp"] = f(x_prompt[4 * k:4 * k + 4].reshape(1024, 1024))
        m["xs"] = f(x_sample[b][idx])
        m["cvec"] = f(np.stack([c_ctx, c[b]], axis=-1).reshape(8, 128, 2).transpose(1, 0, 2))
        ms = _pool_matrix(gwin[:NS_COLS], 1024)
        msp = np.zeros((4, 384, NS_COLS), np.float32)
        msp[:, :NS_COLS] = ms
        m["mps"] = f(msp.reshape(4, 3, 128, NS_COLS).transpose(2, 1, 0, 3))
        cosf, sinf = _rope_tables(idx)
        def tab(a):
            a = np.concatenate([a, np.zeros((512, 64), np.float32)], axis=0)
            tl = [a[t * 128:(t + 1) * 128] for t in range(3)] + [a[NS_COLS + t * 128:NS_COLS + (t + 1) * 128] for t in range(6)]
            return f(np.stack(tl, axis=1))
        m["ropec"] = tab(cosf)
        m["ropes"] = tab(sinf)
        gw = gwin[:NS_COLS]
        m["cmask"] = f(np.broadcast_to(((gw >= 0) & (gw < 1024)).astype(np.float32)[None, :], (128, NS_COLS)))
        m["ck"] = f(cache_k[b, 0].reshape(512, 128))
        m["cv"] = f(cache_v[b, 0].reshape(512, 128))
        maps.append(m)
    return maps


def assemble(results, cores=range(8)):
    y_prompt = np.zeros((32, 256, 1024), np.float32)
    y_sample = np.zeros((2, 1024, 1024), np.float32)
    nk = np.zeros((32, 1, 256, 2, 64), np.float32)
    nv = np.zeros((32, 1, 256, 2, 64), np.float32)
    for res, k in zip(results, cores):
        b, r = k // 4, k % 4
        y_prompt[4 * k:4 * k + 4] = res["yp"].reshape(4, 256, 1024)
        y_sample[b, 256 * r:256 * r + 256] = res["ys"]
        nk[4 * k:4 * k + 4, 0] = res["nk"].reshape(4, 256, 2, 64)
        nv[4 * k:4 * k + 4, 0] = res["nv"].reshape(4, 256, 2, 64)
    return y_prompt, y_sample, nk, nv


def kernel(**inputs):
    nc = _get_nc()
    maps = make_in_maps(inputs)
    res = run_bass_kernel_spmd(nc, maps, core_ids=list(range(8)))
    return assemble(res.results)
```

```python
import numpy as np
from collections import deque
from contextlib import ExitStack
import concourse.bass as bass
import concourse.mybir as mybir
from concourse.bass_utils import run_bass_kernel_spmd

F32 = mybir.dt.float32
BF16 = mybir.dt.bfloat16
AF = mybir.ActivationFunctionType
ALU = mybir.AluOpType
AX = mybir.AxisListType
PE, ACT, DVE, POOL, SP = "tensor", "scalar", "vector", "gpsimd", "sync"
ENGS = (PE, ACT, DVE, POOL, SP)

D = 1024
DFF = 2816
NJ = 22
EPS = 1e-6
NP_COLS = 1024
NS_COLS = 288
HALO = 16
NM = NP_COLS + NS_COLS
NR = 1024 - NS_COLS
GRID_W = 64
POOL_WINDOWS = (2, 4, 8, 16)


class Buf:
    __slots__ = ("name", "w", "r")

    def __init__(self, name):
        self.name = name
        self.w = None
        self.r = []


class Op:
    __slots__ = ("eng", "fn", "deps", "marked", "sig", "is_dma", "key", "dval")

    def __init__(self, eng, fn, is_dma):
        self.eng = eng
        self.fn = fn
        self.deps = ()
        self.marked = False
        self.sig = 0
        self.is_dma = is_dma
        self.key = None
        self.dval = 0


class Prog:
    def __init__(self, nc):
        self.nc = nc
        self.ops = {e: [] for e in ENGS}
        self.dma_keys = {}

    def op(self, eng, fn, reads=(), writes=(), dma=False, key=None):
        o = Op(eng, fn, dma)
        deps = set()
        for b in reads:
            if b.w is not None:
                deps.add(b.w)
        for b in writes:
            if b.w is not None:
                deps.add(b.w)
            deps.update(b.r)
        if eng == PE and not dma:
            deps = {d for d in deps if not (d.eng == PE and not d.is_dma)}
        for d in deps:
            d.marked = True
        o.deps = deps
        for b in reads:
            b.r.append(o)
        for b in writes:
            b.w = o
            b.r = []
        if dma:
            if key is None:
                key = (writes[0] if writes else reads[0]).name
            o.key = key
            self.dma_keys[key] = self.dma_keys.get(key, 0) + 16
            o.dval = self.dma_keys[key]
        self.ops[eng].append(o)
        return o

    def dma(self, queue, out, in_, reads=(), writes=(), key=None):
        return self.op(queue, lambda e: e.dma_start(out=out, in_=in_), reads, writes, dma=True, key=key)

    @staticmethod
    def inherit(new_bufs, old_bufs):
        hz = []
        for b in old_bufs:
            if b.w is not None:
                hz.append(b.w)
            hz.extend(b.r)
        for nb in new_bufs:
            nb.r = list(nb.r) + hz

    def emit(self):
        nc = self.nc
        with ExitStack() as es:
            esem = {e: es.enter_context(nc.semaphore("s_" + e)) for e in ENGS}
            dsem = {k: es.enter_context(nc.semaphore("d%d" % i)) for i, k in enumerate(self.dma_keys)}
            for e in ENGS:
                c = 0
                for o in self.ops[e]:
                    if not o.is_dma and o.marked:
                        c += 1
                        o.sig = c
            block = es.enter_context(nc.Block())

            def run(e, eng):
                waited = {}
                for o in self.ops[e]:
                    need = {}
                    for d in o.deps:
                        if d.is_dma:
                            s, v = dsem[d.key], d.dval
                        else:
                            s, v = esem[d.eng], d.sig
                        if need.get(s, 0) < v:
                            need[s] = v
                    for s, v in need.items():
                        if waited.get(s, 0) < v:
                            eng.wait_ge(s, v)
                            waited[s] = v
                    ins = o.fn(eng)
                    if o.is_dma:
                        ins.then_inc(dsem[o.key], 16)
                    elif o.marked:
                        ins.then_inc(esem[e], 1)
                if e == SP:
                    for k, v in self.dma_keys.items():
                        eng.wait_ge(dsem[k], v)

            @block.tensor
            def _(eng):
                run(PE, eng)

            @block.scalar
            def _(eng):
                run(ACT, eng)

            @block.vector
            def _(eng):
                run(DVE, eng)

            @block.gpsimd
            def _(eng):
                run(POOL, eng)

            @block.sync
            def _(eng):
                run(SP, eng)


def mm_group(mms):
    def fn(e):
        ins = None
        for (o, l, r, st, sp) in mms:
            ins = e.matmul(o, lhsT=l, rhs=r, start=st, stop=sp)
        return ins
    return fn


class Builder:
    def __init__(self, stop_after=99):
        self.stop_after = stop_after
        self.nc = bass.Bass("TRN2", target_bir_lowering=False)
        self.P = Prog(self.nc)
        self.es = ExitStack()
        self.uid = 0

    def din(self, name, shape):
        return self.nc.dram_tensor(name, list(shape), F32, kind="ExternalInput").ap()

    def dout(self, name, shape):
        return self.nc.dram_tensor(name, list(shape), F32, kind="ExternalOutput").ap()

    def sb(self, name, shape, dt=F32):
        return self.es.enter_context(self.nc.sbuf_tensor("sb_" + name, list(shape), dt))

    def buf(self, name):
        self.uid += 1
        return Buf("%s#%d" % (name, self.uid))

    def av(self, off, n, dt=BF16):
        v = self.areg[:, off:off + n]
        if dt == F32:
            v = v.bitcast(F32)
        return v

    def psum(self, hold=False):
        while True:
            k = self.ps_i % 7
            self.ps_i += 1
            if k not in self.ps_hold:
                break
        if hold:
            self.ps_hold.add(k)
        return self.ps[k], self.psb[k]

    def psum_pool(self, name, banks, hold=False):
        tries = 0
        while True:
            i = self.ps_pool_i.get(name, 0)
            self.ps_pool_i[name] = i + 1
            k = banks[i % len(banks)]
            tries += 1
            if k in self.ps_hold:
                continue
            if k in self.ps_recent and tries <= len(banks):
                continue
            break
        if hold:
            self.ps_hold.add(k)
        return self.ps[k], self.psb[k]

    def psum_release(self, pb, fresh=False):
        k = self.psb.index(pb)
        self.ps_hold.discard(k)
        if fresh:
            self.ps_recent = set()
        self.ps_recent.add(k)

    def wload(self, src, shape):
        k = self.ring_i % len(self.ring)
        self.ring_i += 1
        npart = shape[0]
        n = int(np.prod(shape[1:]))
        v = self.ring[k][0:npart, 0:n]
        if len(shape) == 3:
            v = v.rearrange("p (a b) -> p a b", b=shape[2])
        elif len(shape) == 4:
            v = v.rearrange("p (a b c) -> p a b c", b=shape[2], c=shape[3])
        b = self.ringb[k]
        self.P.dma(POOL, v, src, writes=[b], key="ring%d" % k)
        return v, b

    def build(self):
        nc, P = self.nc, self.P
        self.d_xp = self.din("xp", [1024, 1024])
        self.d_xs = self.din("xs", [1024, 1024])
        self.d_cvec = self.din("cvec", [128, 8, 2])
        self.d_adaw = self.din("adaw", [2, 18, 128, 8, 512])
        self.d_adab = self.din("adab", [128, 2, 72])
        self.d_normg = self.din("normg", [128, 2, 3, 8])
        self.d_finalg = self.din("finalg", [128, 1024])
        self.d_w1 = self.din("w1t", [2, 2, 11, 128, 8, 512])
        self.d_w2 = self.din("w2t", [2, 2, 4, 2, 128, 11, 256])
        self.d_wq = self.din("wq", [128, 8, 512])
        self.d_wkv = self.din("wkv", [128, 8, 256])
        self.d_wcv = self.din("wcv", [4, 128, 8, 384])
        self.d_wo = self.din("wo", [2, 128, 4, 1024])
        self.d_qg = self.din("qg", [128, 512])
        self.d_kg = self.din("kg", [128, 128])
        self.d_convw = self.din("convw", [128, 4, 3])
        self.d_poolw = self.din("poolw", [128, 4, 2, 256])
        self.d_pscale = self.din("pscale", [128, 8])
        self.d_mpp = self.din("mpp", [128, 2, 4, 256])
        self.d_mps = self.din("mps", [128, 3, 4, 288])
        self.d_ropec = self.din("ropec", [128, 9, 64])
        self.d_ropes = self.din("ropes", [128, 9, 64])
        self.d_cmask = self.din("cmask", [128, 288])
        self.d_ck = self.din("ck", [512, 128])
        self.d_cv = self.din("cv", [512, 128])
        self.o_yp = self.dout("yp", [1024, 1024])
        self.o_ys = self.dout("ys", [256, 1024])
        self.o_nk = self.dout("nk", [1024, 128])
        self.o_nv = self.dout("nv", [1024, 128])

        self.xres = self.sb("xres", [128, 8, NM])
        self.hbuf = self.sb("hbuf", [128, 8, NM], BF16)
        self.areg = self.sb("areg", [128, NJ * NM], BF16)
        self.ring = [self.sb("ring%d" % i, [128, 4096], BF16) for i in range(5)]
        self.ringb = [Buf("ring%d" % i) for i in range(5)]
        self.ring_i = 0
        self.sq = self.sb("sq", [128, 8, 256])
        self.b_sq = Buf("sq")
        self.sq2 = self.sb("sq2", [128, 8, 256])
        self.b_sq2 = Buf("sq2")
        self.rsall = self.sb("rsall", [128, NM])
        self.nrm_i = 0
        self.sg = [self.sb("sg%d" % i, [128, 512]) for i in range(2)]
        self.b_sg = [Buf("sg%d" % i) for i in range(2)]
        self.sg_i = 0
        self.nt = [self.sb("nt%d" % i, [128, 512]) for i in range(2)]
        self.b_nt = [Buf("nt%d" % i) for i in range(2)]
        self.nt_i = 0
        self.ident = self.sb("ident", [128, 128])
        self.ones = self.sb("ones", [128, 128])
        self.onesb = self.sb("onesb", [128, 128], BF16)
        self.b_ident, self.b_ones, self.b_onesb = Buf("ident"), Buf("ones"), Buf("onesb")
        self.cvec = self.sb("cvec", [128, 8, 2])
        self.scb = self.sb("scb", [128, 8, 2], BF16)
        self.adab = self.sb("adab", [128, 2, 72])
        self.normg = self.sb("normg", [128, 2, 3, 8])
        self.finalg = self.sb("finalg", [128, 1024])
        self.qg = self.sb("qg", [128, 512])
        self.kg = self.sb("kg", [128, 128])
        self.convw = self.sb("convw", [128, 4, 3])
        self.pscale = self.sb("pscale", [128, 8])
        self.b_cvec, self.b_scb, self.b_adab, self.b_normg = Buf("cvec"), Buf("scb"), Buf("adab"), Buf("normg")
        self.b_finalg, self.b_qg, self.b_kg, self.b_convw, self.b_pscale = (
            Buf("finalg"), Buf("qg"), Buf("kg"), Buf("convw"), Buf("pscale"))
        self.modsb = [self.sb("modsb%d" % l, [128, 72, 2]) for l in range(2)]
        self.asc = [self.sb("asc%d" % l, [128, 3, 8, 2]) for l in range(2)]
        self.gsc = [self.sb("gsc%d" % l, [128, 3, 8, 2]) for l in range(2)]
        self.b_mod = [[Buf("mod%d_%d" % (l, i)) for i in range(3)] for l in range(2)]
        self.b_gate = [[Buf("gate%d_%d" % (l, i)) for i in range(3)] for l in range(2)]
        self.small = self.sb("small", [128, 64])
        self.b_small = Buf("small")
        self.ps = [self.es.enter_context(nc.psum_tensor("ps%d" % i, [128, 512], F32)) for i in range(8)]
        self.psb = [Buf("ps%d" % i) for i in range(8)]
        self.ps_i = 0
        self.ps_hold = set()
        self.ps_pool_i = {}
        self.ps_recent = set()

        self.ffn_tiles_M = [(0, 512, 0), (512, 512, 0), (1024, 288, 1)]
        self.ffn_tiles_R = [(0, 512, 1), (512, 224, 1)]
        self.sub_M = [(256 * i, 256, 0, i // 2) for i in range(4)] + [(1024, 256, 1, 2), (1280, 32, 1, 2)]
        self.sub_R = [(0, 256, 1, 0), (256, 256, 1, 0), (512, 224, 1, 1)]
        self.tok_M = [(128 * i, 128, i // 4) for i in range(8)] + [(1024, 128, 2), (1152, 128, 2), (1280, 32, 2)]
        self.tok_R = [(128 * i, 128, 0) for i in range(4)] + [(512, 128, 1), (640, 96, 1)]
        self.bx_M = [Buf("xM%d" % i) for i in range(3)]
        self.bh_M = [Buf("hM%d" % i) for i in range(3)]

        self.setup_consts()
        self.mod_q = deque()
        for l in range(2):
            for s in range(18):
                self.mod_q.append((l, s))
        self.mod_done = {}

        stg = [self.sq[:, 0:4, :].rearrange("p a b -> p (a b)"), self.sq[:, 4:8, :].rearrange("p a b -> p (a b)")]
        self.load_x(self.d_xp, 0, self.xres, [(128 * i, 128, 128 * i) for i in range(8)], self.bx_M,
                    [i // 4 for i in range(8)], stg)
        self.load_x(self.d_xs, 0, self.xres, [(0, 128, 1024), (128, 128, 1152), (256, 32, 1280)], self.bx_M,
                    [2, 2, 2], stg)

        a_M = self.av(0, NJ * NM).rearrange("p (j n) -> p j n", n=NM)
        ba_M = [Buf("aM%d" % i) for i in range(3)]
        self.cur_a = ba_M

        for l in range(2):
            if self.stop_after < 10 * l + 1:
                break
            self.norm(self.xres, self.hbuf, self.sub_M, self.bx_M, self.bh_M, l, 0)
            self.ffn(l, 0, self.xres, self.hbuf, a_M, self.ffn_tiles_M, self.bx_M, self.bh_M, ba_M)
            if self.stop_after < 10 * l + 2:
                break
            self.mod_need(l, 1)
            if l == 0:
                self.mixer0(a_M, ba_M)
            else:
                self.norm(self.xres, self.hbuf, self.sub_M, self.bx_M, self.bh_M, l, 1)
                self.mixer1(ba_M)
            if self.stop_after < 10 * l + 3:
                break
            nb = [Buf("aM%d" % i) for i in range(3)]
            Prog.inherit(nb, self.cur_a)
            ba_M = nb
            self.cur_a = nb
            self.mod_need(l, 2)
            self.norm(self.xres, self.hbuf, self.sub_M, self.bx_M, self.bh_M, l, 2)
            self.ffn(l, 1, self.xres, self.hbuf, a_M, self.ffn_tiles_M, self.bx_M, self.bh_M, ba_M)

        self.final_out()
        P.emit()
        self.es.close()
        return nc

    def setup_consts(self):
        P = self.P
        P.dma(SP, self.cvec[:], self.d_cvec[:, :, :], writes=[self.b_cvec])
        P.dma(SP, self.adab[:], self.d_adab[:, :, :], writes=[self.b_adab])
        P.dma(SP, self.normg[:], self.d_normg[:, :, :, :], writes=[self.b_normg])
        P.dma(SP, self.pscale[:], self.d_pscale[:, :], writes=[self.b_pscale])
        P.dma(SP, self.convw[:], self.d_convw[:, :, :], writes=[self.b_convw])
        ident, ones, onesb = self.ident, self.ones, self.onesb
        P.op(DVE, lambda e: e.memset(ident[:], 0.0), writes=[self.b_ident])
        P.op(POOL, lambda e: e.affine_select(out=ident[:], in_=ident[:], pattern=[[-1, 128]],
                                             compare_op=ALU.not_equal, fill=1.0, base=0, channel_multiplier=1),
             reads=[self.b_ident], writes=[self.b_ident])
        P.op(DVE, lambda e: e.memset(ones[:], 1.0), writes=[self.b_ones])
        P.op(DVE, lambda e: e.memset(onesb[:], 1.0), writes=[self.b_onesb])
        cvec, scb = self.cvec, self.scb
        P.op(ACT, lambda e: e.activation(out=scb[:], in_=cvec[:], func=AF.Silu), reads=[self.b_cvec], writes=[self.b_scb])

    def mod_emit(self, l, s):
        P = self.P
        W, bW = self.wload(self.d_adaw[l, s], [128, 8, 512])
        ps7 = self.ps[7][:, 0:144].rearrange("p (m v) -> p m v", v=2)
        mms = []
        for mt in range(4):
            m = 4 * s + mt
            for c in range(8):
                mms.append((ps7[:, m, :], W[:, c, mt * 128:(mt + 1) * 128], self.scb[:, c, :], c == 0, c == 7))
        P.op(PE, mm_group(mms), reads=[bW, self.b_scb], writes=[self.psb[7]])
        i = s // 6
        bm = self.b_mod[l][i]
        modsb, adab, asc, gsc, normg, pscale = self.modsb[l], self.adab, self.asc[l], self.gsc[l], self.normg, self.pscale
        if s % 6 == 3:
            lo, hi = 24 * i, 24 * i + 16
            P.op(DVE, lambda e: e.tensor_tensor(out=modsb[:, lo:hi, :], in0=ps7[:, lo:hi, :],
                                                in1=adab[:, l, lo:hi].unsqueeze(2).broadcast_to([128, 16, 2]), op=ALU.add),
                 reads=[self.psb[7], self.b_adab], writes=[bm])
            P.op(DVE, lambda e: e.tensor_scalar(out=asc[:, i, :, :], in0=modsb[:, lo + 8:lo + 16, :], scalar1=1.0,
                                                scalar2=None, op0=ALU.add), reads=[bm], writes=[bm])
            P.op(DVE, lambda e: e.tensor_tensor(out=asc[:, i, :, :], in0=asc[:, i, :, :],
                                                in1=normg[:, l, i, :].unsqueeze(2).broadcast_to([128, 8, 2]), op=ALU.mult),
                 reads=[bm, self.b_normg], writes=[bm])
            self.mod_done[(l, i)] = True
        if s % 6 == 5:
            bg = self.b_gate[l][i]
            lo, hi = 24 * i + 16, 24 * i + 24
            P.op(DVE, lambda e: e.tensor_tensor(out=modsb[:, lo:hi, :], in0=ps7[:, lo:hi, :],
                                                in1=adab[:, l, lo:hi].unsqueeze(2).broadcast_to([128, 8, 2]), op=ALU.add),
                 reads=[self.psb[7], self.b_adab], writes=[bg])
            if i == 1 and l == 1:
                P.op(DVE, lambda e: e.tensor_tensor(out=gsc[:, i, :, :], in0=modsb[:, lo:hi, :],
                                                    in1=pscale[:, :].unsqueeze(2).broadcast_to([128, 8, 2]), op=ALU.mult),
                     reads=[bg, self.b_pscale], writes=[bg])
            else:
                f = 1.0 if i == 1 else 0.5
                P.op(DVE, lambda e: e.tensor_scalar(out=gsc[:, i, :, :], in0=modsb[:, lo:hi, :], scalar1=f,
                                                    scalar2=None, op0=ALU.mult), reads=[bg], writes=[bg])
            self.mod_done[(l, i, "g")] = True

    def mod_pump(self, n):
        for _ in range(n):
            if not self.mod_q:
                return
            l, s = self.mod_q.popleft()
            self.mod_emit(l, s)

    def mod_need(self, l, i, gate=False):
        key = (l, i, "g") if gate else (l, i)
        while not self.mod_done.get(key):
            self.mod_pump(1)

    def load_x(self, dram, row0, xbuf, tiles, bx, parents, stg):
        P = self.P
        for ti, (r0, npk, c0) in enumerate(tiles):
            k = self.sg_i % 2
            self.sg_i += 1
            st = stg[k]
            P.dma(SP, st[0:npk, :], dram[row0 + r0:row0 + r0 + npk, :], writes=[self.b_sq], key="stg")
            for half in range(2):
                ps, pb = self.psum()
                def tr(e, ps=ps, st=st, half=half, npk=npk):
                    ins = None
                    for j in range(4):
                        c = 4 * half + j
                        ins = e.transpose(out=ps[:, j * 128:j * 128 + npk], in_=st[0:npk, c * 128:(c + 1) * 128],
                                          identity=self.ident[0:npk, 0:npk])
                    return ins
                P.op(PE, tr, reads=[self.b_sq, self.b_ident], writes=[pb])
                src = ps[:, :].rearrange("p (j n) -> p j n", n=128)[:, :, 0:npk]
                dst = xbuf[:, 4 * half:4 * half + 4, c0:c0 + npk]
                P.op(ACT if half == 0 else DVE,
                     (lambda e, dst=dst, src=src: e.activation(out=dst, in_=src, func=AF.Copy)) if half == 0 else
                     (lambda e, dst=dst, src=src: e.tensor_copy(out=dst, in_=src)),
                     reads=[pb], writes=[bx[parents[ti]]])

    def norm(self, xbuf, hbuf, subs, bx, bh, l, i, tiles=None):
        P = self.P
        if tiles is None:
            tiles = self.ffn_tiles_M if len(subs) == len(self.sub_M) else self.ffn_tiles_R
        asc, modsb, bm = self.asc[l], self.modsb[l], self.b_mod[l][i]
        rsall = self.rsall
        b_rs = [Buf("rs_t%d" % t) for t in range(len(tiles))]
        Prog.inherit(b_rs, getattr(self, "b_rs_prev", []))
        self.b_rs_prev = b_rs
        pend = None

        def fin(pd):
            ps, pb, c0, n, par = pd
            P.op(ACT, lambda e: e.activation(out=rsall[:, c0:c0 + n], in_=ps[:, 0:n], func=AF.Sqrt, bias=EPS, scale=1.0 / D),
                 reads=[pb], writes=[b_rs[par]])
            P.op(DVE, lambda e: e.reciprocal(out=rsall[:, c0:c0 + n], in_=rsall[:, c0:c0 + n]),
                 reads=[b_rs[par]], writes=[b_rs[par]])

        for (c0, n, v, par) in subs:
            k = self.nrm_i % 2
            self.nrm_i += 1
            sq, bsq = (self.sq, self.b_sq) if k == 0 else (self.sq2, self.b_sq2)
            P.op(ACT, lambda e, sq=sq, c0=c0, n=n: e.activation(out=sq[:, :, 0:n], in_=xbuf[:, :, c0:c0 + n], func=AF.Square),
                 reads=[bx[par]], writes=[bsq])
            ps, pb = self.psum()
            P.op(PE, mm_group([(ps[:, 0:n], self.ones[:], sq[:, c, 0:n], c == 0, c == 7) for c in range(8)]),
                 reads=[bsq, self.b_ones], writes=[pb])
            if pend is not None:
                fin(pend)
            pend = (ps, pb, c0, n, par)
        fin(pend)
        self.mod_need(l, i)
        for ti, (c0, n, v) in enumerate(tiles):
            for c in range(8):
                k2 = self.nt_i % 2
                self.nt_i += 1
                nt, bnt = self.nt[k2], self.b_nt[k2]
                P.op(DVE, lambda e, nt=nt, c=c, c0=c0, n=n, v=v: e.scalar_tensor_tensor(
                    out=nt[:, 0:n], in0=xbuf[:, c, c0:c0 + n], scalar=asc[:, i, c, v:v + 1], in1=rsall[:, c0:c0 + n],
                    op0=ALU.mult, op1=ALU.mult), reads=[bx[ti], b_rs[ti], bm], writes=[bnt])
                P.op(ACT, lambda e, nt=nt, c=c, c0=c0, n=n, v=v: e.activation(
                    out=hbuf[:, c, c0:c0 + n], in_=nt[:, 0:n], func=AF.Identity,
                    bias=modsb[:, 24 * i + c, v:v + 1], scale=1.0), reads=[bnt, bm], writes=[bh[ti]])

    def ffn(self, l, s, xbuf, hbuf, abuf, tiles, bx, bh, ba):
        P = self.P
        gi = 0 if s == 0 else 2
        gsc, bm = self.gsc[l], self.b_gate[l][gi]
        for sl in range(11):
            W, bW = self.wload(self.d_w1[l, s, sl], [128, 8, 512])
            for ti, (c0, n, v) in enumerate(tiles):
                for jj in range(2):
                    j = 2 * sl + jj
                    pg, bg = self.psum()
                    pu, bu = self.psum()
                    P.op(PE, mm_group([(pg[:, 0:n], W[:, c, jj * 256:jj * 256 + 128], hbuf[:, c, c0:c0 + n], c == 0, c == 7)
                                       for c in range(8)]), reads=[bW, bh[ti]], writes=[bg])
                    P.op(PE, mm_group([(pu[:, 0:n], W[:, c, jj * 256 + 128:jj * 256 + 256], hbuf[:, c, c0:c0 + n], c == 0, c == 7)
                                       for c in range(8)]), reads=[bW, bh[ti]], writes=[bu])
                    k = self.sg_i % 2
                    self.sg_i += 1
                    sg, bsg = self.sg[k], self.b_sg[k]
                    P.op(ACT, lambda e, sg=sg, pg=pg, n=n: e.activation(out=sg[:, 0:n], in_=pg[:, 0:n], func=AF.Silu),
                         reads=[bg], writes=[bsg])
                    P.op(DVE, lambda e, sg=sg, pu=pu, n=n, j=j, c0=c0: e.tensor_tensor(
                        out=abuf[:, j, c0:c0 + n], in0=pu[:, 0:n], in1=sg[:, 0:n], op=ALU.mult),
                        reads=[bu, bsg], writes=[ba[ti]])
            self.mod_pump(2)
        self.mod_need(l, gi, gate=True)
        for g in range(4):
            Wa, bWa = self.wload(self.d_w2[l, s, g, 0], [128, 11, 256])
            Wb, bWb = self.wload(self.d_w2[l, s, g, 1], [128, 11, 256])
            for ti, (c0, n, v) in enumerate(tiles):
                for dd in range(2):
                    d = 2 * g + dd
                    py, by = self.psum()
                    mms = []
                    for j in range(NJ):
                        Wx = Wa if j < 11 else Wb
                        mms.append((py[:, 0:n], Wx[:, j % 11, dd * 128:(dd + 1) * 128], abuf[:, j, c0:c0 + n], j == 0, j == NJ - 1))
                    P.op(PE, mm_group(mms), reads=[bWa, bWb, ba[ti]], writes=[by])
                    P.op(DVE, lambda e, py=py, n=n, d=d, c0=c0, v=v: e.scalar_tensor_tensor(
                        out=xbuf[:, d, c0:c0 + n], in0=py[:, 0:n], scalar=gsc[:, gi, d, v:v + 1], in1=xbuf[:, d, c0:c0 + n],
                        op0=ALU.mult, op1=ALU.add), reads=[by, bm, bx[ti]], writes=[bx[ti]])
            self.mod_pump(1)

    def kv_tile(self, hsrc, c0, npk, bh, Wkv, bWkv, kT, V, bkT, bV, kcol, vt, rope_t, out_row, T, par_=None):
        P = self.P
        if par_ is None:
            par_ = T["i"] % 2
        tA, tB, kst, small, bT, bkst = T["tA"][par_], T["tB"][par_], T["kst"][par_], self.small, T["bT"][par_], T["bkst"][par_]
        bsm = T["bsm"][par_]
        sc = slice(50 + 2 * par_, 52 + 2 * par_)
        T["i"] += 1
        ps, pb = self.psum()
        P.op(PE, mm_group([(ps[0:npk, 0:256], hsrc[:, c, c0:c0 + npk], Wkv[:, c, :], c == 0, c == 7) for c in range(8)]),
             reads=[bWkv, bh], writes=[pb])
        yield
        P.op(ACT, lambda e: e.activation(out=tA[0:npk, 0:128], in_=ps[0:npk, 0:128], func=AF.Square),
             reads=[pb], writes=[bT])
        yield
        P.op(DVE, lambda e: e.tensor_reduce(out=small[0:npk, sc], in_=tA[0:npk, 0:128].rearrange("p (h d) -> p h d", d=64),
                                            axis=AX.X, op=ALU.add), reads=[bT], writes=[bsm])
        yield
        P.op(ACT, lambda e: e.activation(out=small[0:npk, sc], in_=small[0:npk, sc], func=AF.Sqrt, bias=EPS, scale=1.0 / 64),
             reads=[bsm], writes=[bsm])
        yield
        P.op(DVE, lambda e: e.reciprocal(out=small[0:npk, sc], in_=small[0:npk, sc]), reads=[bsm], writes=[bsm])
        yield
        P.op(DVE, lambda e: e.tensor_tensor(out=tB[0:npk, 0:128].rearrange("p (h d) -> p h d", d=64),
                                            in0=ps[0:npk, 0:128].rearrange("p (h d) -> p h d", d=64),
                                            in1=small[0:npk, sc].unsqueeze(2).broadcast_to([npk, 2, 64]), op=ALU.mult),
             reads=[pb, bsm], writes=[bT])
        yield
        P.op(DVE, lambda e: e.tensor_tensor(out=kst[0:npk, 0:128], in0=tB[0:npk, 0:128], in1=self.kg[0:npk, :], op=ALU.mult),
             reads=[bT, self.b_kg], writes=[bkst])
        yield
        P.op(ACT, lambda e: e.activation(out=V[0:npk, vt, :], in_=ps[0:npk, 128:256], func=AF.Copy), reads=[pb], writes=[bV])
        yield
        ksrc = kst
        if out_row is not None:
            P.op(ACT, lambda e: e.activation(out=kst[0:npk, 128:256], in_=ps[0:npk, 128:256], func=AF.Copy), reads=[pb], writes=[bkst])
            yield
            P.dma(SP, self.o_nk[out_row:out_row + npk, :], kst[0:npk, 0:128], reads=[bkst], key=bkst.name + "k")
            yield
            P.dma(SP, self.o_nv[out_row:out_row + npk, :], kst[0:npk, 128:256], reads=[bkst], key=bkst.name + "v")
            yield
        if rope_t is not None:
            yield from self.rope(kst[0:npk, 0:128], tA[0:npk, 0:128], tB[0:npk, 0:128], 2, npk, rope_t, [bkst], bT)
            ksrc = tA
        ps2, pb2 = self.psum()
        P.op(PE, lambda e: e.transpose(out=ps2[:, 0:npk], in_=ksrc[0:npk, 0:128], identity=self.ident[0:npk, 0:npk]),
             reads=[bkst, bT, self.b_ident], writes=[pb2])
        yield
        P.op(ACT, lambda e: e.activation(out=kT[:, kcol:kcol + npk], in_=ps2[:, 0:npk], func=AF.Copy), reads=[pb2], writes=[bkT])
        yield

    def interleave(self, gens, width=2):
        pending = deque(gens)
        active = []
        while pending or active:
            while pending and len(active) < width:
                active.append(pending.popleft())
            for g in list(active):
                try:
                    next(g)
                except StopIteration:
                    active.remove(g)

    def interleave_w(self, gens_w):
        active = [[g, w] for g, w in gens_w]
        while active:
            for gw in list(active):
                g, w = gw
                for _ in range(w):
                    try:
                        next(g)
                    except StopIteration:
                        active.remove(gw)
                        break

    def rope(self, x, t1, t2, H, npk, rt, bx_list, bT, bT2=None):
        P = self.P
        cosf, sinf = self.ropec[0:npk, rt, :], self.ropes[0:npk, rt, :]
        v5 = lambda a: a.rearrange("p (h r f s) -> p h r f s", h=H, r=2, f=2, s=16)
        c4 = cosf.rearrange("p (r f s) -> p r f s", r=2, f=2, s=16)
        s4 = sinf.rearrange("p (r f s) -> p r f s", r=2, f=2, s=16)
        P.op(DVE, lambda e: e.tensor_tensor(out=v5(t1), in0=v5(x), in1=c4.unsqueeze(1).broadcast_to([npk, H, 2, 2, 16]), op=ALU.mult),
             reads=bx_list + [self.b_rope], writes=[bT])
        yield
        for f in range(2):
            P.op(DVE, lambda e, f=f: e.tensor_tensor(out=v5(t2)[:, :, :, f, :], in0=v5(x)[:, :, :, 1 - f, :],
                                                     in1=s4[:, :, f, :].unsqueeze(1).broadcast_to([npk, H, 2, 16]), op=ALU.mult),
                 reads=bx_list + [self.b_rope], writes=[bT2 or bT])
            yield
        P.op(DVE, lambda e: e.tensor_tensor(out=t1, in0=t1, in1=t2, op=ALU.add), reads=[bT, bT2 or bT], writes=[bT])
        yield

    def mixer0(self, a_M, ba_M):
        P = self.P
        self.ropec = self.sb("ropec", [128, 9, 64])
        self.ropes = self.sb("ropes", [128, 9, 64])
        self.cmask = self.sb("cmask", [128, NS_COLS])
        self.b_rope, self.b_cmask = Buf("rope"), Buf("cmask")
        P.dma(SP, self.ropec[:], self.d_ropec[:, :, :], writes=[self.b_rope], key="ropec")
        P.dma(SP, self.ropes[:], self.d_ropes[:, :, :], writes=[self.b_rope], key="ropes")
        P.dma(SP, self.cmask[:], self.d_cmask[:, :], writes=[self.b_cmask])
        P.dma(SP, self.qg[:], self.d_qg[:, :], writes=[self.b_qg])
        P.dma(SP, self.kg[:], self.d_kg[:, :], writes=[self.b_kg])
        aR = self.av(0, NJ * NR).rearrange("p (j n) -> p j n", n=NR)
        xR = self.av(NJ * NR, 2 * 8 * NR, F32).rearrange("p (c n) -> p c n", n=NR)
        hR = self.hbuf[:, :, 0:NR]
        b_aR = [Buf("aR%d" % i) for i in range(2)]
        b_xR = [Buf("xR%d" % i) for i in range(2)]
        Prog.inherit(b_aR + b_xR, ba_M)
        bhR = [Buf("hR0"), Buf("hR1")]
        Prog.inherit(bhR, self.bh_M)
        stg = [self.sq[:, 0:4, :].rearrange("p a b -> p (a b)"), self.sq[:, 4:8, :].rearrange("p a b -> p (a b)")]
        self.load_x(self.d_xs, NS_COLS, xR, [(c0, npk, c0) for (c0, npk, par) in self.tok_R], b_xR,
                    [par for (c0, npk, par) in self.tok_R], stg)
        self.norm(xR, hR, self.sub_R, b_xR, bhR, 0, 0)
        self.ffn(0, 0, xR, hR, aR, self.ffn_tiles_R, b_xR, bhR, b_aR)
        self.norm(xR, hR, self.sub_R, b_xR, bhR, 0, 1)

        kT = self.av(0, 2560)
        V = self.av(2560, 21 * 128).rearrange("p (t f) -> p t f", f=128)
        b_kT = [Buf("kT_p%d" % i) for i in range(4)] + [Buf("kT_s")]
        b_V = [Buf("V_p%d" % i) for i in range(4)] + [Buf("V_s")]
        Prog.inherit(b_kT + b_V, b_aR)
        T = {"tA": [self.sg[0][:, 0:128], self.sg[0][:, 256:384]], "tB": [self.sg[0][:, 128:256], self.sg[0][:, 384:512]],
             "kst": [self.sg[1][:, 0:256], self.sg[1][:, 256:512]], "i": 0,
             "bT": [Buf("kvT0"), Buf("kvT1")], "bkst": [Buf("kst0"), Buf("kst1")], "bsm": [Buf("smk0"), Buf("smk1")]}
        Prog.inherit(T["bT"] + T["bkst"], self.b_sg)
        Wkv, bWkv = self.wload(self.d_wkv[:, :, :], [128, 8, 256])
        self.interleave([self.kv_tile(hR, c0, npk, bhR[par], Wkv, bWkv, kT, V, b_kT[4], b_V[4], 1824 + c0, 15 + i, 3 + i, None, T)
                         for i, (c0, npk, par) in enumerate(self.tok_R)])
        Prog.inherit(self.bh_M, bhR)
        self.norm(self.xres, self.hbuf, self.sub_M, self.bx_M, self.bh_M, 0, 1)
        qT = self.av(5248, 4 * NM).rearrange("p (s n) -> p s n", n=NM)
        attnT = self.av(10496, 4 * NM).rearrange("p (s n) -> p s n", n=NM)
        convT = self.av(15744, 4 * NM).rearrange("p (s n) -> p s n", n=NM)
        pT = [self.av(20992 + 512 * k, 512) for k in range(3)]
        T1s = [self.av(22528, 1024, F32), self.av(25600, 1024, F32)]
        T2s = [self.av(23552, 1024, F32), self.av(20992, 1024, F32)]
        T3s = [self.av(24576, 1024, F32), self.av(15744, 1024, F32)]
        rd = self.av(25600, 1024, F32)
        ckst = self.av(26624, 1024, F32).rearrange("p (t f) -> p t f", f=128)
        b_qT = [Buf("qT_%d" % i) for i in range(5)]
        b_attnT = [Buf("attnT%d" % i) for i in range(3)]
        b_convT = Buf("convT")
        b_pT = [Buf("pT%d" % k) for k in range(3)]
        b_T1s, b_T3s, b_rd, b_ckst = [Buf("T1a"), Buf("T1b")], [Buf("T3a"), Buf("T3b")], Buf("rd"), Buf("ckst")
        b_smq = [Buf("smq0"), Buf("smq1")]
        allnew = b_qT + b_attnT + [b_convT] + b_pT + b_T1s + b_T3s + [b_rd, b_ckst]
        Prog.inherit(allnew, b_aR + b_xR)
        P.dma(SP, ckst[:, :, :], self.d_ck.rearrange("(t p) f -> p t f", p=128), writes=[b_ckst])
        P.dma(POOL, V[:, 8:12, :], self.d_cv.rearrange("(t p) f -> p t f", p=128), writes=[b_V[4]], key="cvload")
        for t in range(4):
            ps2, pb2 = self.psum()
            P.op(PE, lambda e, t=t, ps2=ps2: e.transpose(out=ps2[:, 0:128], in_=ckst[:, t, :], identity=self.ident[:]),
                 reads=[b_ckst, self.b_ident], writes=[pb2])
            P.op(ACT, lambda e, t=t, ps2=ps2: e.activation(out=kT[:, 1024 + 128 * t:1152 + 128 * t], in_=ps2[:, 0:128], func=AF.Copy),
                 reads=[pb2], writes=[b_kT[4]])
        Wq, bWq = self.wload(self.d_wq[:, :, :], [128, 8, 512])
        def mtile(i, c0, npk, par):
            is_s = i >= 8
            bi = 4 if is_s else i // 2
            if is_s:
                kcol, vt, rt, orow = 1536 + (c0 - 1024), 12 + (i - 8), (i - 8), None
            else:
                kcol, vt, rt, orow = c0, i, None, c0
            yield from self.kv_tile(self.hbuf, c0, npk, self.bh_M[par], Wkv, bWkv, kT, V, b_kT[bi], b_V[bi], kcol, vt, rt, orow, T, par_=i % 2)
            qp = i % 2
            T1, T2, b_T1, bsmq = T1s[qp], T2s[qp], b_T1s[qp], b_smq[qp]
            qc = slice(34 + 8 * qp, 42 + 8 * qp)
            ps, pb = self.psum()
            P.op(PE, mm_group([(ps[0:npk, 0:512], self.hbuf[:, c, c0:c0 + npk], Wq[:, c, :], c == 0, c == 7) for c in range(8)]),
                 reads=[bWq, self.bh_M[par]], writes=[pb])
            yield
            small = self.small
            v3 = lambda a: a.rearrange("p (h d) -> p h d", d=64)
            P.op(ACT, lambda e, ps=ps, npk=npk, T1=T1: e.activation(out=T1[0:npk, :], in_=ps[0:npk, :], func=AF.Square),
                 reads=[pb], writes=[b_T1])
            yield
            P.op(DVE, lambda e, npk=npk, T1=T1, qc=qc: e.tensor_reduce(out=small[0:npk, qc], in_=v3(T1[0:npk, :]), axis=AX.X, op=ALU.add),
                 reads=[b_T1], writes=[bsmq])
            yield
            P.op(ACT, lambda e, npk=npk, qc=qc: e.activation(out=small[0:npk, qc], in_=small[0:npk, qc], func=AF.Sqrt, bias=EPS,
                                                      scale=1.0 / 64), reads=[bsmq], writes=[bsmq])
            yield
            P.op(DVE, lambda e, npk=npk, qc=qc: e.reciprocal(out=small[0:npk, qc], in_=small[0:npk, qc]),
                 reads=[bsmq], writes=[bsmq])
            yield
            P.op(DVE, lambda e, ps=ps, npk=npk, T2=T2, qc=qc: e.tensor_tensor(out=v3(T2[0:npk, :]), in0=v3(ps[0:npk, :]),
                                                                in1=small[0:npk, qc].unsqueeze(2).broadcast_to([npk, 8, 64]),
                                                                op=ALU.mult), reads=[pb, bsmq], writes=[b_T1])
            yield
            P.op(DVE, lambda e, npk=npk, T1=T1, T2=T2: e.tensor_tensor(out=T1[0:npk, :], in0=T2[0:npk, :], in1=self.qg[0:npk, :], op=ALU.mult),
                 reads=[b_T1, self.b_qg], writes=[b_T1])
            yield
            qsrc = T1
            if is_s:
                yield from self.rope(T1[0:npk, :], T2[0:npk, :], T3s[qp][0:npk, :], 8, npk, rt, [b_T1], b_T1, b_T3s[qp])
                qsrc = T2
            pst, pbt = self.psum()
            def trq(e, pst=pst, qsrc=qsrc, npk=npk):
                ins = None
                for s4 in range(4):
                    ins = e.transpose(out=pst[:, s4 * 128:s4 * 128 + npk], in_=qsrc[0:npk, s4 * 128:(s4 + 1) * 128],
                                      identity=self.ident[0:npk, 0:npk])
                return ins
            P.op(PE, trq, reads=[b_T1, self.b_ident], writes=[pbt])
            yield
            P.op(ACT, lambda e, pst=pst, npk=npk, c0=c0: e.activation(
                out=qT[:, :, c0:c0 + npk], in_=pst[:, :].rearrange("p (s n) -> p s n", n=128)[:, :, 0:npk], func=AF.Copy),
                reads=[pbt], writes=[b_qT[bi]])
            yield


        self.interleave([mtile(i, c0, npk, par) for i, (c0, npk, par) in enumerate(self.tok_M)])

        assert not self.mod_q and self.mod_done.get((1, 2, "g")), "PSUM bank 7 still holds adaLN accumulators"
        Prog.inherit(b_pT + [b_rd], b_T1s)
        Prog.inherit([b_convT], b_T3s)
        s_chunks = [(1024 + 128 * t, 128, 8 + t) for t in range(4)] + [(1536, 128, 12), (1664, 128, 13), (1792, 32, 14)] + \
                   [(1824 + 128 * i, 128, 15 + i) for i in range(5)] + [(2464, 96, 20)]
        groups = []
        for bi in range(4):
            for hh in range(2):
                for sp in range(2):
                    groups.append((bi, hh, (2 * sp, 2 * sp + 2), bi * 256, 256,
                                   [(bi * 256 + 128 * kc, 128, 2 * bi + kc) for kc in range(2)], b_attnT[bi // 2]))
        for hh in range(2):
            for s4 in range(4):
                groups.append((4, hh, (s4, s4 + 1), 1024, NS_COLS, s_chunks, b_attnT[2]))
        rounds = [(g, ci) for g in range(len(groups)) for ci in range(len(groups[g][5]))]
        st = {}

        def emit_s(ri):
            g, ci = rounds[ri]
            bi, hh, (s0, s1), qc0, qn, chunks, bat = groups[g]
            kcol, npk, vt = chunks[ci]
            ps, pb = self.psum_pool("attn", (2, 3, 4, 5, 6, 7))
            ncol = (s1 - s0) * qn
            hs = slice(hh * 64, hh * 64 + 64)
            P.op(PE, lambda e: e.matmul(ps[0:npk, 0:ncol], lhsT=kT[hs, kcol:kcol + npk], rhs=qT[hs, s0:s1, qc0:qc0 + qn],
                                        start=True, stop=True), reads=[b_kT[bi], b_qT[bi]], writes=[pb])
            st[ri] = (ps, pb, ncol)

        def attn_gen():
            emit_s(0)
            yield
            acc = {}
            for ri in range(len(rounds)):
                if ri + 1 < len(rounds):
                    emit_s(ri + 1)
                    yield
                g, ci = rounds[ri]
                bi, hh, (s0, s1), qc0, qn, chunks, bat = groups[g]
                kcol, npk, vt = chunks[ci]
                ps, pb, ncol = st.pop(ri)
                k = ri % 3
                p, bp = pT[k], b_pT[k]
                P.op(ACT, lambda e, p=p, ps=ps, npk=npk, ncol=ncol: e.activation(out=p[0:npk, 0:ncol], in_=ps[0:npk, 0:ncol],
                                                                                 func=AF.Exp, scale=0.125), reads=[pb], writes=[bp])
                yield
                if ci == 0:
                    acc[g] = (self.psum_pool("attn", (2, 3, 4, 5, 6, 7), hold=True), self.psum_pool("attn", (2, 3, 4, 5, 6, 7), hold=True))
                (pn, bn), (pd, bd) = acc[g]
                last = ci == len(chunks) - 1
                P.op(PE, lambda e, pn=pn, p=p, npk=npk, ncol=ncol, vt=vt, ci=ci, last=last: e.matmul(
                    pn[:, 0:ncol], lhsT=V[0:npk, vt, :], rhs=p[0:npk, 0:ncol], start=(ci == 0), stop=last),
                    reads=[bp, b_V[bi]], writes=[bn])
                yield
                P.op(PE, lambda e, pd=pd, p=p, npk=npk, ncol=ncol, ci=ci, last=last: e.matmul(
                    pd[:, 0:ncol], lhsT=self.onesb[0:npk, :], rhs=p[0:npk, 0:ncol], start=(ci == 0), stop=last),
                    reads=[bp, self.b_onesb], writes=[bd])
                yield
                if last:
                    hs = slice(hh * 64, hh * 64 + 64)
                    P.op(ACT, lambda e, pd=pd, hs=hs, ncol=ncol: e.activation(out=rd[hs, 0:ncol], in_=pd[hs, 0:ncol], func=AF.Ln),
                         reads=[bd], writes=[b_rd])
                    yield
                    P.op(ACT, lambda e, hs=hs, ncol=ncol: e.activation(out=rd[hs, 0:ncol], in_=rd[hs, 0:ncol], func=AF.Exp, scale=-1.0),
                         reads=[b_rd], writes=[b_rd])
                    yield
                    P.op(DVE, lambda e, pn=pn, hs=hs, ncol=ncol, s0=s0, s1=s1, qc0=qc0, qn=qn: e.tensor_tensor(
                        out=attnT[hs, s0:s1, qc0:qc0 + qn], in0=pn[hs, 0:ncol].rearrange("p (s n) -> p s n", n=qn),
                        in1=rd[hs, 0:ncol].rearrange("p (s n) -> p s n", n=qn), op=ALU.mult),
                        reads=[bn, b_rd], writes=[bat])
                    yield
                    self.psum_release(bn, fresh=True)
                    self.psum_release(bd)
                    del acc[g]


        def conv_gen():
            upad = self.sq[:, :, :].rearrange("p a b -> p (a b)")
            cacc = self.sq2[:, :, :].rearrange("p a b -> p (a b)")
            bgs = self.rsall
            xcs = [self.nt[0], self.nt[1]]
            b_upad, b_cacc, b_bgs, b_xcs = self.b_sq, self.b_sq2, Buf("bgs"), self.b_nt
            Prog.inherit([b_bgs], getattr(self, "b_rs_prev", []))
            self.b_rs_prev = [b_bgs]
            P.op(DVE, lambda e: e.memset(upad[:, :], 0.0), writes=[b_upad])
            yield
            cw = self.convw
            for cc in range(4):
                Wc, bWc = self.wload(self.d_wcv[cc], [128, 8, 384])
                for ti, (c0, n, v) in enumerate(self.ffn_tiles_M):
                    def proj(q3, pq, bq):
                        return P.op(PE, mm_group([(pq[:, 0:n], Wc[:, c, q3 * 128:(q3 + 1) * 128], self.hbuf[:, c, c0:c0 + n],
                                                   c == 0, c == 7) for c in range(8)]), reads=[bWc, self.bh_M[ti]], writes=[bq])
                    pxc, bpxc = self.psum_pool("conv", (0, 1))
                    proj(2, pxc, bpxc)
                    yield
                    pcg, bpcg = self.psum_pool("conv", (0, 1))
                    proj(1, pcg, bpcg)
                    yield
                    k = (cc * 3 + ti) % 2
                    xc_, bxc = xcs[k], b_xcs[k]
                    P.op(ACT, lambda e, xc_=xc_, n=n, px=pxc: e.activation(out=xc_[:, 0:n], in_=px[:, 0:n], func=AF.Copy),
                         reads=[bpxc], writes=[bxc])
                    yield
                    if ti < 2:
                        uo = upad[:, 1 + 2 * ti * 257:1 + (2 * ti + 2) * 257].rearrange("p (b k) -> p b k", k=257)[:, :, 0:256]
                        P.op(DVE, lambda e, uo=uo, pc=pcg, xc_=xc_: e.tensor_tensor(
                            out=uo, in0=pc[:, 0:512].rearrange("p (b k) -> p b k", k=256),
                            in1=xc_[:, 0:512].rearrange("p (b k) -> p b k", k=256), op=ALU.mult),
                            reads=[bpcg, bxc], writes=[b_upad])
                        yield
                    else:
                        P.op(DVE, lambda e, pc=pcg, xc_=xc_: e.tensor_tensor(out=upad[:, 1029:1317], in0=pc[:, 0:NS_COLS],
                                                                             in1=xc_[:, 0:NS_COLS], op=ALU.mult),
                             reads=[bpcg, bxc], writes=[b_upad])
                        yield
                    pbg, bpbg = self.psum_pool("conv", (0, 1))
                    proj(0, pbg, bpbg)
                    yield
                    P.op(ACT, lambda e, n=n, c0=c0, pb_=pbg: e.activation(out=bgs[:, c0:c0 + n], in_=pb_[:, 0:n], func=AF.Copy),
                         reads=[bpbg], writes=[b_bgs])
                    yield
                P.op(DVE, lambda e: e.tensor_tensor(out=upad[:, 1029:1317], in0=upad[:, 1029:1317], in1=self.cmask[:, :], op=ALU.mult),
                     reads=[b_upad, self.b_cmask], writes=[b_upad])
                yield
                P.op(DVE, lambda e, cc=cc: e.tensor_scalar(out=cacc[:, 1:1317], in0=upad[:, 1:1317], scalar1=cw[:, cc, 1:2], scalar2=None,
                                                           op0=ALU.mult), reads=[b_upad, self.b_convw], writes=[b_cacc])
                yield
                P.op(DVE, lambda e, cc=cc: e.scalar_tensor_tensor(out=cacc[:, 1:1317], in0=upad[:, 0:1316], scalar=cw[:, cc, 0:1],
                                                                  in1=cacc[:, 1:1317], op0=ALU.mult, op1=ALU.add),
                     reads=[b_upad, self.b_convw, b_cacc], writes=[b_cacc])
                yield
                P.op(DVE, lambda e, cc=cc: e.scalar_tensor_tensor(out=cacc[:, 1:1317], in0=upad[:, 2:1318], scalar=cw[:, cc, 2:3],
                                                                  in1=cacc[:, 1:1317], op0=ALU.mult, op1=ALU.add),
                     reads=[b_upad, self.b_convw, b_cacc], writes=[b_cacc])
                yield
                P.op(DVE, lambda e, cc=cc: e.tensor_tensor(
                    out=convT[:, cc, 0:1024].rearrange("p (b k) -> p b k", k=256),
                    in0=cacc[:, 1:1029].rearrange("p (b k) -> p b k", k=257)[:, :, 0:256],
                    in1=bgs[:, 0:1024].rearrange("p (b k) -> p b k", k=256), op=ALU.mult),
                    reads=[b_cacc, b_bgs], writes=[b_convT])
                yield
                P.op(DVE, lambda e, cc=cc: e.tensor_tensor(out=convT[:, cc, 1024:NM], in0=cacc[:, 1029:1317], in1=bgs[:, 1024:NM],
                                                           op=ALU.mult), reads=[b_cacc, b_bgs], writes=[b_convT])
                yield


        self.interleave_w([(attn_gen(), 6), (conv_gen(), 1)])

        Woa, bWoa = self.wload(self.d_wo[0], [128, 4, 1024])
        Woc, bWoc = self.wload(self.d_wo[1], [128, 4, 1024])
        self.mod_need(0, 1, gate=True)
        gsc, bm = self.gsc[0], self.b_gate[0][1]
        for ti, (c0, n, v) in enumerate(self.ffn_tiles_M):
            for d in range(8):
                po, bpo = self.psum()
                mms = [(po[:, 0:n], Woa[:, s4, d * 128:(d + 1) * 128], attnT[:, s4, c0:c0 + n], s4 == 0, False) for s4 in range(4)]
                mms += [(po[:, 0:n], Woc[:, s4, d * 128:(d + 1) * 128], convT[:, s4, c0:c0 + n], False, s4 == 3) for s4 in range(4)]
                P.op(PE, mm_group(mms), reads=[bWoa, bWoc, b_attnT[ti], b_convT], writes=[bpo])
                P.op(DVE, lambda e, po=po, n=n, d=d, c0=c0, v=v: e.scalar_tensor_tensor(
                    out=self.xres[:, d, c0:c0 + n], in0=po[:, 0:n], scalar=gsc[:, 1, d, v:v + 1], in1=self.xres[:, d, c0:c0 + n],
                    op0=ALU.mult, op1=ALU.add), reads=[bpo, bm, self.bx_M[ti]], writes=[self.bx_M[ti]])
        self.cur_a = allnew + b_kT + b_V + b_aR + b_xR
        Prog.inherit(self.b_sg, T["bT"] + T["bkst"])

    def kv_tile_R(self, hR, c0, npk, bh, Wkv, bWkv, kT, V, bkT, bV, kcol, vt, ridx, T):
        self.kv_tile(hR, c0, npk, bh, Wkv, bWkv, kT, V, bkT, bV, kcol, vt, ("R", c0 // 128), None, T)

    def mixer1(self, ba_M):
        P = self.P
        z = self.av(0, 11 * 1024).rearrange("p (t f) -> p t f", f=1024)
        mpp = self.av(11264, 2048).rearrange("p (a g t) -> p a g t", g=4, t=256)
        mps = self.av(13312, 3456).rearrange("p (a g t) -> p a g t", g=4, t=NS_COLS)
        b_z = [Buf("z%d" % i) for i in range(5)]
        b_mpp, b_mps = Buf("mpp"), Buf("mps")
        Prog.inherit(b_z + [b_mpp, b_mps], self.cur_a)
        P.dma(POOL, mpp, self.d_mpp[:, :, :, :], writes=[b_mpp])
        P.dma(POOL, mps, self.d_mps[:, :, :, :], writes=[b_mps])
        Wp, bWp = self.wload(self.d_poolw[:, :, :, :], [128, 4, 2, 256])
        for tt, (c0, npk, par) in enumerate(self.tok_M):
            bz = b_z[4 if tt >= 8 else tt // 2]
            for half in range(2):
                ps, pb = self.psum()
                mms = []
                for g2 in range(2):
                    gi = 2 * half + g2
                    for kc in range(2):
                        mms.append((ps[0:npk, g2 * 256:(g2 + 1) * 256], self.hbuf[:, 2 * gi + kc, c0:c0 + npk], Wp[:, gi, kc, :],
                                    kc == 0, kc == 1))
                P.op(PE, mm_group(mms), reads=[bWp, self.bh_M[par]], writes=[pb])
                dst = z[0:npk, tt, half * 512:(half + 1) * 512]
                if half == 0:
                    P.op(ACT, lambda e, dst=dst, ps=ps, npk=npk: e.activation(out=dst, in_=ps[0:npk, :], func=AF.Copy),
                         reads=[pb], writes=[bz])
                else:
                    P.op(DVE, lambda e, dst=dst, ps=ps, npk=npk: e.tensor_copy(out=dst, in_=ps[0:npk, :]), reads=[pb], writes=[bz])
        self.mod_need(1, 1, gate=True)
        gsc, bm = self.gsc[1], self.b_gate[1][1]
        segs = [(bi, [(2 * bi, 128), (2 * bi + 1, 128)], 256, bi * 256, mpp, b_mpp, 0, bi // 2) for bi in range(4)]
        segs.append((4, [(8, 128), (9, 128), (10, 32)], NS_COLS, 1024, mps, b_mps, 1, 2))
        for (bi, stiles, Tn, c0, Mx, bMx, v, par) in segs:
            for fc in range(8):
                gi = fc // 2
                po, bpo = self.psum()
                mms = [(po[:, 0:Tn], z[0:npk, tt, fc * 128:(fc + 1) * 128], Mx[0:npk, sc, gi, 0:Tn], sc == 0, sc == len(stiles) - 1)
                       for sc, (tt, npk) in enumerate(stiles)]
                P.op(PE, mm_group(mms), reads=[b_z[bi], bMx], writes=[bpo])
                P.op(DVE, lambda e, po=po, Tn=Tn, fc=fc, c0=c0, v=v: e.scalar_tensor_tensor(
                    out=self.xres[:, fc, c0:c0 + Tn], in0=po[:, 0:Tn], scalar=gsc[:, 1, fc, v:v + 1], in1=self.xres[:, fc, c0:c0 + Tn],
                    op0=ALU.mult, op1=ALU.add), reads=[bpo, bm, self.bx_M[par]], writes=[self.bx_M[par]])
        self.cur_a = b_z + [b_mpp, b_mps]

    def final_out(self):
        P = self.P
        P.dma(SP, self.finalg[:], self.d_finalg[:, :], writes=[self.b_finalg])
        ot = [self.av(4096 * k, 2048, F32) for k in range(2)]
        jk = [self.av(8192 + 4096 * k, 2048, F32) for k in range(2)]
        b_ot = [Buf("ot0"), Buf("ot1")]
        b_jk = [Buf("jk0"), Buf("jk1")]
        b_sm = [Buf("smf0"), Buf("smf1")]
        Prog.inherit(b_ot + b_jk, self.cur_a)
        Prog.inherit(b_sm, [self.b_small])
        outs = [(128 * i, self.o_yp, 128 * i, i // 4) for i in range(8)] + \
               [(1024 + HALO + 128 * i, self.o_ys, 128 * i, 2) for i in range(2)]
        sm = self.small

        def otile(oi, c0, dram, r0, par):
            k = oi % 2
            o, bo, j, bj, bs = ot[k], b_ot[k], jk[k], b_jk[k], b_sm[k]
            for half in range(2):
                ps, pb = self.psum()

                def tr(e, ps=ps, half=half):
                    ins = None
                    for jj in range(4):
                        c = 4 * half + jj
                        ins = e.transpose(out=ps[:, jj * 128:(jj + 1) * 128], in_=self.xres[:, c, c0:c0 + 128],
                                          identity=self.ident[:])
                    return ins
                P.op(PE, tr, reads=[self.bx_M[par], self.b_ident], writes=[pb])
                yield
                P.op(ACT, lambda e, ps=ps, half=half: e.activation(out=o[:, half * 512:(half + 1) * 512], in_=ps[:, :],
                                                                   func=AF.Copy), reads=[pb], writes=[bo])
                yield
            P.op(ACT, lambda e: e.activation(out=j[:, :], in_=o[:, :], func=AF.Square, accum_out=sm[:, oi:oi + 1]),
                 reads=[bo], writes=[bj, bs])
            yield
            P.op(ACT, lambda e: e.activation(out=sm[:, oi:oi + 1], in_=sm[:, oi:oi + 1], func=AF.Sqrt, bias=EPS,
                                             scale=1.0 / D), reads=[bs], writes=[bs])
            yield
            P.op(DVE, lambda e: e.reciprocal(out=sm[:, oi:oi + 1], in_=sm[:, oi:oi + 1]), reads=[bs], writes=[bs])
            yield
            P.op(DVE, lambda e: e.scalar_tensor_tensor(out=o[:, :], in0=o[:, :], scalar=sm[:, oi:oi + 1],
                                                       in1=self.finalg[:, :], op0=ALU.mult, op1=ALU.mult),
                 reads=[bo, bs, self.b_finalg], writes=[bo])
            yield
            P.dma(SP, dram[r0:r0 + 128, :], o[:, :], reads=[bo], key="ot%d" % k)
            yield

        self.interleave([otile(oi, *t) for oi, t in enumerate(outs)])


def _host_inputs(inp):
    f = lambda a: np.ascontiguousarray(np.asarray(a, dtype=np.float32))
    x_prompt, x_sample, c, c_ctx = f(inp["x_prompt"]), f(inp["x_sample"]), f(inp["c"]), f(inp["c_ctx"])
    ada_w, ada_b, norm_g = f(inp["ada_w"]), f(inp["ada_b"]), f(inp["norm_g"])
    w1, w2 = f(inp["ffn_w1"]), f(inp["ffn_w2"])
    shared = {}
    shared["adaw"] = f(ada_w.reshape(2, 8, 128, 18, 512).transpose(0, 3, 2, 1, 4))
    shared["adab"] = f(ada_b.reshape(2, 72, 128).transpose(2, 0, 1))
    shared["normg"] = f(norm_g.reshape(2, 3, 8, 128).transpose(3, 0, 1, 2))
    shared["finalg"] = f(np.broadcast_to(f(inp["final_g"])[None, :], (128, 1024)))
    g = w1[..., :DFF].reshape(2, 2, 8, 128, 11, 2, 128)
    u = w1[..., DFF:].reshape(2, 2, 8, 128, 11, 2, 128)
    gu = np.stack([g, u], axis=6)
    shared["w1t"] = f(gu.transpose(0, 1, 4, 3, 2, 5, 6, 7).reshape(2, 2, 11, 128, 8, 512))
    w2r = w2.reshape(2, 2, 2, 11, 128, 4, 256)
    shared["w2t"] = f(w2r.transpose(0, 1, 5, 2, 4, 3, 6))
    return shared, x_prompt, x_sample, c, c_ctx


_NC_CACHE = {}


def _get_nc(stop_after=99):
    if stop_after not in _NC_CACHE:
        _NC_CACHE[stop_after] = Builder(stop_after).build()
    return _NC_CACHE[stop_after]


HEAD_PERM = [0, 4, 1, 5, 2, 6, 3, 7]


def _pool_matrix(gpos, S_total):
    n = len(gpos)
    out = np.zeros((4, n, n), np.float32)
    for gi, w in enumerate(POOL_WINDOWS):
        left = w // 2
        right = w - 1 - left
        for t in range(n):
            gt = gpos[t]
            if gt < 0 or gt >= S_total:
                continue
            lo = max(gt - left, 0)
            hi = min(gt + right + 1, S_total)
            inv = np.float32(1.0) / np.float32(hi - lo)
            for s in range(n):
                if lo <= gpos[s] < hi:
                    out[gi, s, t] += inv
            out[gi, t, t] -= 1.0
    return out


def _rope_tables(gtok):
    half = 32
    inv = 10000.0 ** (-np.arange(0, half, 2, dtype=np.float64) / half)
    row = (gtok // GRID_W).astype(np.float64)
    col = (gtok % GRID_W).astype(np.float64)
    ar = row[:, None] * inv[None, :]
    ac = col[:, None] * inv[None, :]
    cr, sr, cc, sc = np.cos(ar), np.sin(ar), np.cos(ac), np.sin(ac)
    cosf = np.concatenate([cr, cr, cc, cc], axis=1).astype(np.float32)
    sinf = np.concatenate([-sr, sr, -sc, sc], axis=1).astype(np.float32)
    return cosf, sinf


def make_in_maps(inp, cores=range(8)):
    f = lambda a: np.ascontiguousarray(np.asarray(a, dtype=np.float32))
    shared, x_prompt, x_sample, c, c_ctx = _host_inputs(inp)
    w_in, w_out = f(inp["mix_w_in"])[0], f(inp["mix_w_out"])[0]
    wq = w_in[:, 0:512].reshape(1024, 8, 64)[:, HEAD_PERM, :].reshape(1024, 512)
    shared["wq"] = f(wq.reshape(8, 128, 512).transpose(1, 0, 2))
    shared["wkv"] = f(w_in[:, 512:768].reshape(8, 128, 256).transpose(1, 0, 2))
    bg, cg, xc = w_in[:, 768:1280], w_in[:, 1280:1792], w_in[:, 1792:2304]
    wcv = np.stack([np.concatenate([bg[:, k * 128:(k + 1) * 128], cg[:, k * 128:(k + 1) * 128],
                                    xc[:, k * 128:(k + 1) * 128]], axis=1) for k in range(4)], axis=0)
    shared["wcv"] = f(wcv.reshape(4, 8, 128, 384).transpose(0, 2, 1, 3))
    wo_a = w_out[0:512].reshape(8, 64, 1024)
    wo_a = np.stack([np.concatenate([wo_a[s], wo_a[4 + s]], axis=0) for s in range(4)], axis=1)
    wo_c = w_out[512:1024].reshape(4, 128, 1024).transpose(1, 0, 2)
    shared["wo"] = f(np.stack([wo_a, wo_c], axis=0))
    shared["qg"] = f(np.broadcast_to(np.tile(f(inp["q_norm"])[0], 8)[None, :], (128, 512)))
    shared["kg"] = f(np.broadcast_to(np.tile(f(inp["k_norm"])[0], 2)[None, :], (128, 128)))
    shared["convw"] = f(f(inp["conv_w"])[0].T.reshape(4, 128, 3).transpose(1, 0, 2))
    shared["poolw"] = f(f(inp["pool_w"])[0].reshape(4, 2, 128, 256).transpose(2, 0, 1, 3))
    shared["pscale"] = f(f(inp["pool_scale"])[0].reshape(8, 128).T)
    mp = _pool_matrix(np.arange(256), 256)
    shared["mpp"] = f(mp.reshape(4, 2, 128, 256).transpose(2, 1, 0, 3))
    cache_k, cache_v = f(inp["cache_k"]), f(inp["cache_v"])
    maps = []
    for k in cores:
        b, r = k // 4, k % 4
        gwin = 256 * r - HALO + np.arange(1024)
        idx = gwin % 1024
        m = dict(shared)
        m["xp"] = f(x_prompt[4 * k:4 * k + 4].reshape(1024, 1024))
        m["xs"] = f(x_sample[b][idx])
        m["cvec"] = f(np.stack([c_ctx, c[b]], axis=-1).reshape(8, 128, 2).transpose(1, 0, 2))
        ms = _pool_matrix(gwin[:NS_COLS], 1024)
        msp = np.zeros((4, 384, NS_COLS), np.float32)
        msp[:, :NS_COLS] = ms
        m["mps"] = f(msp.reshape(4, 3, 128, NS_COLS).transpose(2, 1, 0, 3))
        cosf, sinf = _rope_tables(idx)
        def tab(a):
            a = np.concatenate([a, np.zeros((512, 64), np.float32)], axis=0)
            tl = [a[t * 128:(t + 1) * 128] for t in range(3)] + [a[NS_COLS + t * 128:NS_COLS + (t + 1) * 128] for t in range(6)]
            return f(np.stack(tl, axis=1))
        m["ropec"] = tab(cosf)
        m["ropes"] = tab(sinf)
        gw = gwin[:NS_COLS]
        m["cmask"] = f(np.broadcast_to(((gw >= 0) & (gw < 1024)).astype(np.float32)[None, :], (128, NS_COLS)))
        m["ck"] = f(cache_k[b, 0].reshape(512, 128))
        m["cv"] = f(cache_v[b, 0].reshape(512, 128))
        maps.append(m)
    return maps


def assemble(results, cores=range(8)):
    y_prompt = np.zeros((32, 256, 1024), np.float32)
    y_sample = np.zeros((2, 1024, 1024), np.float32)
    nk = np.zeros((32, 1, 256, 2, 64), np.float32)
    nv = np.zeros((32, 1, 256, 2, 64), np.float32)
    for res, k in zip(results, cores):
        b, r = k // 4, k % 4
        y_prompt[4 * k:4 * k + 4] = res["yp"].reshape(4, 256, 1024)
        y_sample[b, 256 * r:256 * r + 256] = res["ys"]
        nk[4 * k:4 * k + 4, 0] = res["nk"].reshape(4, 256, 2, 64)
        nv[4 * k:4 * k + 4, 0] = res["nv"].reshape(4, 256, 2, 64)
    return y_prompt, y_sample, nk, nv


def kernel(**inputs):
    nc = _get_nc()
    maps = make_in_maps(inputs)
    res = run_bass_kernel_spmd(nc, maps, core_ids=list(range(8)))
    return assemble(res.results)
```

```python
import numpy as np
from collections import deque
from contextlib import ExitStack
import concourse.bass as bass
import concourse.mybir as mybir
from concourse.bass_utils import run_bass_kernel_spmd

F32 = mybir.dt.float32
BF16 = mybir.dt.bfloat16
AF = mybir.ActivationFunctionType
ALU = mybir.AluOpType
AX = mybir.AxisListType
PE, ACT, DVE, POOL, SP = "tensor", "scalar", "vector", "gpsimd", "sync"
ENGS = (PE, ACT, DVE, POOL, SP)

D = 1024
DFF = 2816
NJ = 22
EPS = 1e-6
NP_COLS = 1024
NS_COLS = 288
HALO = 16
NM = NP_COLS + NS_COLS
NR = 1024 - NS_COLS
GRID_W = 64
POOL_WINDOWS = (2, 4, 8, 16)


class Buf:
    __slots__ = ("name", "w", "r")

    def __init__(self, name):
        self.name = name
        self.w = None
        self.r = []


class Op:
    __slots__ = ("eng", "fn", "deps", "marked", "sig", "is_dma", "key", "dval")

    def __init__(self, eng, fn, is_dma):
        self.eng = eng
        self.fn = fn
        self.deps = ()
        self.marked = False
        self.sig = 0
        self.is_dma = is_dma
        self.key = None
        self.dval = 0


class Prog:
    def __init__(self, nc):
        self.nc = nc
        self.ops = {e: [] for e in ENGS}
        self.dma_keys = {}

    def op(self, eng, fn, reads=(), writes=(), dma=False, key=None):
        o = Op(eng, fn, dma)
        deps = set()
        for b in reads:
            if b.w is not None:
                deps.add(b.w)
        for b in writes:
            if b.w is not None:
                deps.add(b.w)
            deps.update(b.r)
        if eng == PE and not dma:
            deps = {d for d in deps if not (d.eng == PE and not d.is_dma)}
        for d in deps:
            d.marked = True
        o.deps = deps
        for b in reads:
            b.r.append(o)
        for b in writes:
            b.w = o
            b.r = []
        if dma:
            if key is None:
                key = (writes[0] if writes else reads[0]).name
            o.key = key
            self.dma_keys[key] = self.dma_keys.get(key, 0) + 16
            o.dval = self.dma_keys[key]
        self.ops[eng].append(o)
        return o

    def dma(self, queue, out, in_, reads=(), writes=(), key=None):
        return self.op(queue, lambda e: e.dma_start(out=out, in_=in_), reads, writes, dma=True, key=key)

    @staticmethod
    def inherit(new_bufs, old_bufs):
        hz = []
        for b in old_bufs:
            if b.w is not None:
                hz.append(b.w)
            hz.extend(b.r)
        for nb in new_bufs:
            nb.r = list(nb.r) + hz

    def emit(self):
        nc = self.nc
        with ExitStack() as es:
            esem = {e: es.enter_context(nc.semaphore("s_" + e)) for e in ENGS}
            dsem = {k: es.enter_context(nc.semaphore("d%d" % i)) for i, k in enumerate(self.dma_keys)}
            for e in ENGS:
                c = 0
                for o in self.ops[e]:
                    if not o.is_dma and o.marked:
                        c += 1
                        o.sig = c
            block = es.enter_context(nc.Block())

            def run(e, eng):
                waited = {}
                for o in self.ops[e]:
                    need = {}
                    for d in o.deps:
                        if d.is_dma:
                            s, v = dsem[d.key], d.dval
                        else:
                            s, v = esem[d.eng], d.sig
                        if need.get(s, 0) < v:
                            need[s] = v
                    for s, v in need.items():
                        if waited.get(s, 0) < v:
                            eng.wait_ge(s, v)
                            waited[s] = v
                    ins = o.fn(eng)
                    if o.is_dma:
                        ins.then_inc(dsem[o.key], 16)
                    elif o.marked:
                        ins.then_inc(esem[e], 1)
                if e == SP:
                    for k, v in self.dma_keys.items():
                        eng.wait_ge(dsem[k], v)

            @block.tensor
            def _(eng):
                run(PE, eng)

            @block.scalar
            def _(eng):
                run(ACT, eng)

            @block.vector
            def _(eng):
                run(DVE, eng)

            @block.gpsimd
            def _(eng):
                run(POOL, eng)

            @block.sync
            def _(eng):
                run(SP, eng)


def mm_group(mms):
    def fn(e):
        ins = None
        for (o, l, r, st, sp) in mms:
            ins = e.matmul(o, lhsT=l, rhs=r, start=st, stop=sp)
        return ins
    return fn


class Builder:
    def __init__(self, stop_after=99):
        self.stop_after = stop_after
        self.nc = bass.Bass("TRN2", target_bir_lowering=False)
        self.P = Prog(self.nc)
        self.es = ExitStack()
        self.uid = 0

    def din(self, name, shape):
        return self.nc.dram_tensor(name, list(shape), F32, kind="ExternalInput").ap()

    def dout(self, name, shape):
        return self.nc.dram_tensor(name, list(shape), F32, kind="ExternalOutput").ap()

    def sb(self, name, shape, dt=F32):
        return self.es.enter_context(self.nc.sbuf_tensor("sb_" + name, list(shape), dt))

    def buf(self, name):
        self.uid += 1
        return Buf("%s#%d" % (name, self.uid))

    def av(self, off, n, dt=BF16):
        v = self.areg[:, off:off + n]
        if dt == F32:
            v = v.bitcast(F32)
        return v

    def psum(self, hold=False):
        while True:
            k = self.ps_i % 7
            self.ps_i += 1
            if k not in self.ps_hold:
                break
        if hold:
            self.ps_hold.add(k)
        return self.ps[k], self.psb[k]

    def psum_pool(self, name, banks, hold=False):
        tries = 0
        while True:
            i = self.ps_pool_i.get(name, 0)
            self.ps_pool_i[name] = i + 1
            k = banks[i % len(banks)]
            tries += 1
            if k in self.ps_hold:
                continue
            if k in self.ps_recent and tries <= len(banks):
                continue
            break
        if hold:
            self.ps_hold.add(k)
        return self.ps[k], self.psb[k]

    def psum_release(self, pb, fresh=False):
        k = self.psb.index(pb)
        self.ps_hold.discard(k)
        if fresh:
            self.ps_recent = set()
        self.ps_recent.add(k)

    def wload(self, src, shape):
        k = self.ring_i % len(self.ring)
        self.ring_i += 1
        npart = shape[0]
        n = int(np.prod(shape[1:]))
        v = self.ring[k][0:npart, 0:n]
        if len(shape) == 3:
            v = v.rearrange("p (a b) -> p a b", b=shape[2])
        elif len(shape) == 4:
            v = v.rearrange("p (a b c) -> p a b c", b=shape[2], c=shape[3])
        b = self.ringb[k]
        self.P.dma(POOL, v, src, writes=[b], key="ring%d" % k)
        return v, b

    def build(self):
        nc, P = self.nc, self.P
        self.d_xp = self.din("xp", [1024, 1024])
        self.d_xs = self.din("xs", [1024, 1024])
        self.d_cvec = self.din("cvec", [128, 8, 2])
        self.d_adaw = self.din("adaw", [2, 18, 128, 8, 512])
        self.d_adab = self.din("adab", [128, 2, 72])
        self.d_normg = self.din("normg", [128, 2, 3, 8])
        self.d_finalg = self.din("finalg", [128, 1024])
        self.d_w1 = self.din("w1t", [2, 2, 11, 128, 8, 512])
        self.d_w2 = self.din("w2t", [2, 2, 4, 2, 128, 11, 256])
        self.d_wq = self.din("wq", [128, 8, 512])
        self.d_wkv = self.din("wkv", [128, 8, 256])
        self.d_wcv = self.din("wcv", [4, 128, 8, 384])
        self.d_wo = self.din("wo", [2, 128, 4, 1024])
        self.d_qg = self.din("qg", [128, 512])
        self.d_kg = self.din("kg", [128, 128])
        self.d_convw = self.din("convw", [128, 4, 3])
        self.d_poolw = self.din("poolw", [128, 4, 2, 256])
        self.d_pscale = self.din("pscale", [128, 8])
        self.d_mpp = self.din("mpp", [128, 2, 4, 256])
        self.d_mps = self.din("mps", [128, 3, 4, 288])
        self.d_ropec = self.din("ropec", [128, 9, 64])
        self.d_ropes = self.din("ropes", [128, 9, 64])
        self.d_cmask = self.din("cmask", [128, 288])
        self.d_ck = self.din("ck", [512, 128])
        self.d_cv = self.din("cv", [512, 128])
        self.o_yp = self.dout("yp", [1024, 1024])
        self.o_ys = self.dout("ys", [256, 1024])
        self.o_nk = self.dout("nk", [1024, 128])
        self.o_nv = self.dout("nv", [1024, 128])

        self.xres = self.sb("xres", [128, 8, NM])
        self.hbuf = self.sb("hbuf", [128, 8, NM], BF16)
        self.areg = self.sb("areg", [128, NJ * NM], BF16)
        self.ring = [self.sb("ring%d" % i, [128, 4096], BF16) for i in range(5)]
        self.ringb = [Buf("ring%d" % i) for i in range(5)]
        self.ring_i = 0
        self.sq = self.sb("sq", [128, 8, 256])
        self.b_sq = Buf("sq")
        self.sq2 = self.sb("sq2", [128, 8, 256])
        self.b_sq2 = Buf("sq2")
        self.rsall = self.sb("rsall", [128, NM])
        self.nrm_i = 0
        self.stg_i = 0
        self.sg = [self.sb("sg%d" % i, [128, 512]) for i in range(2)]
        self.b_sg = [Buf("sg%d" % i) for i in range(2)]
        self.sg_i = 0
        self.nt = [self.sb("nt%d" % i, [128, 512]) for i in range(2)]
        self.b_nt = [Buf("nt%d" % i) for i in range(2)]
        self.nt_i = 0
        self.ident = self.sb("ident", [128, 128])
        self.ones = self.sb("ones", [128, 128])
        self.onesb = self.sb("onesb", [128, 128], BF16)
        self.b_ident, self.b_ones, self.b_onesb = Buf("ident"), Buf("ones"), Buf("onesb")
        self.cvec = self.sb("cvec", [128, 8, 2])
        self.scb = self.sb("scb", [128, 8, 2], BF16)
        self.adab = self.sb("adab", [128, 2, 72])
        self.normg = self.sb("normg", [128, 2, 3, 8])
        self.finalg = self.sb("finalg", [128, 1024])
        self.qg = self.sb("qg", [128, 512])
        self.kg = self.sb("kg", [128, 128])
        self.convw = self.sb("convw", [128, 4, 3])
        self.pscale = self.sb("pscale", [128, 8])
        self.b_cvec, self.b_scb, self.b_adab, self.b_normg = Buf("cvec"), Buf("scb"), Buf("adab"), Buf("normg")
        self.b_finalg, self.b_qg, self.b_kg, self.b_convw, self.b_pscale = (
            Buf("finalg"), Buf("qg"), Buf("kg"), Buf("convw"), Buf("pscale"))
        self.modsb = [self.sb("modsb%d" % l, [128, 72, 2]) for l in range(2)]
        self.asc = [self.sb("asc%d" % l, [128, 3, 8, 2]) for l in range(2)]
        self.gsc = [self.sb("gsc%d" % l, [128, 3, 8, 2]) for l in range(2)]
        self.b_mod = [[Buf("mod%d_%d" % (l, i)) for i in range(3)] for l in range(2)]
        self.b_gate = [[Buf("gate%d_%d" % (l, i)) for i in range(3)] for l in range(2)]
        self.small = self.sb("small", [128, 64])
        self.b_small = Buf("small")
        self.ps = [self.es.enter_context(nc.psum_tensor("ps%d" % i, [128, 512], F32)) for i in range(8)]
        self.psb = [Buf("ps%d" % i) for i in range(8)]
        self.ps_i = 0
        self.ps_hold = set()
        self.ps_pool_i = {}
        self.ps_recent = set()

        self.ffn_tiles_M = [(0, 512, 0), (512, 512, 0), (1024, 288, 1)]
        self.ffn_tiles_R = [(0, 512, 1), (512, 224, 1)]
        self.sub_M = [(256 * i, 256, 0, i // 2) for i in range(4)] + [(1024, 256, 1, 2), (1280, 32, 1, 2)]
        self.sub_R = [(0, 256, 1, 0), (256, 256, 1, 0), (512, 224, 1, 1)]
        self.tok_M = [(128 * i, 128, i // 4) for i in range(8)] + [(1024, 128, 2), (1152, 128, 2), (1280, 32, 2)]
        self.tok_R = [(128 * i, 128, 0) for i in range(4)] + [(512, 128, 1), (640, 96, 1)]
        self.bx_M = [Buf("xM%d" % i) for i in range(3)]
        self.bh_M = [Buf("hM%d" % i) for i in range(3)]

        self.setup_consts()
        self.mod_q = deque()
        for l in range(2):
            for s in range(18):
                self.mod_q.append((l, s))
        self.mod_done = {}

        stg = [self.sq[:, 0:4, :].rearrange("p a b -> p (a b)"), self.sq[:, 4:8, :].rearrange("p a b -> p (a b)")]
        self.load_x(self.d_xp, 0, self.xres, [(128 * i, 128, 128 * i) for i in range(8)], self.bx_M,
                    [i // 4 for i in range(8)], stg)
        self.load_x(self.d_xs, 0, self.xres, [(0, 128, 1024), (128, 128, 1152), (256, 32, 1280)], self.bx_M,
                    [2, 2, 2], stg)

        a_M = self.av(0, NJ * NM).rearrange("p (j n) -> p j n", n=NM)
        ba_M = [Buf("aM%d" % i) for i in range(3)]
        self.cur_a = ba_M

        for l in range(2):
            if self.stop_after < 10 * l + 1:
                break
            self.norm(self.xres, self.hbuf, self.sub_M, self.bx_M, self.bh_M, l, 0)
            self.ffn(l, 0, self.xres, self.hbuf, a_M, self.ffn_tiles_M, self.bx_M, self.bh_M, ba_M)
            if self.stop_after < 10 * l + 2:
                break
            self.mod_need(l, 1)
            if l == 0:
                self.mixer0(a_M, ba_M)
            else:
                self.norm(self.xres, self.hbuf, self.sub_M, self.bx_M, self.bh_M, l, 1)
                self.mixer1(ba_M)
            if self.stop_after < 10 * l + 3:
                break
            nb = [Buf("aM%d" % i) for i in range(3)]
            Prog.inherit(nb, self.cur_a)
            ba_M = nb
            self.cur_a = nb
            self.mod_need(l, 2)
            self.norm(self.xres, self.hbuf, self.sub_M, self.bx_M, self.bh_M, l, 2)
            self.ffn(l, 1, self.xres, self.hbuf, a_M, self.ffn_tiles_M, self.bx_M, self.bh_M, ba_M)

        self.final_out()
        P.emit()
        self.es.close()
        return nc

    def setup_consts(self):
        P = self.P
        P.dma(SP, self.cvec[:], self.d_cvec[:, :, :], writes=[self.b_cvec])
        P.dma(SP, self.adab[:], self.d_adab[:, :, :], writes=[self.b_adab])
        P.dma(SP, self.normg[:], self.d_normg[:, :, :, :], writes=[self.b_normg])
        P.dma(SP, self.pscale[:], self.d_pscale[:, :], writes=[self.b_pscale])
        P.dma(SP, self.convw[:], self.d_convw[:, :, :], writes=[self.b_convw])
        ident, ones, onesb = self.ident, self.ones, self.onesb
        P.op(DVE, lambda e: e.memset(ident[:], 0.0), writes=[self.b_ident])
        P.op(POOL, lambda e: e.affine_select(out=ident[:], in_=ident[:], pattern=[[-1, 128]],
                                             compare_op=ALU.not_equal, fill=1.0, base=0, channel_multiplier=1),
             reads=[self.b_ident], writes=[self.b_ident])
        P.op(DVE, lambda e: e.memset(ones[:], 1.0), writes=[self.b_ones])
        P.op(DVE, lambda e: e.memset(onesb[:], 1.0), writes=[self.b_onesb])
        cvec, scb = self.cvec, self.scb
        P.op(ACT, lambda e: e.activation(out=scb[:], in_=cvec[:], func=AF.Silu), reads=[self.b_cvec], writes=[self.b_scb])

    def mod_emit(self, l, s):
        P = self.P
        W, bW = self.wload(self.d_adaw[l, s], [128, 8, 512])
        ps7 = self.ps[7][:, 0:144].rearrange("p (m v) -> p m v", v=2)
        mms = []
        for mt in range(4):
            m = 4 * s + mt
            for c in range(8):
                mms.append((ps7[:, m, :], W[:, c, mt * 128:(mt + 1) * 128], self.scb[:, c, :], c == 0, c == 7))
        P.op(PE, mm_group(mms), reads=[bW, self.b_scb], writes=[self.psb[7]])
        i = s // 6
        bm = self.b_mod[l][i]
        modsb, adab, asc, gsc, normg, pscale = self.modsb[l], self.adab, self.asc[l], self.gsc[l], self.normg, self.pscale
        if s % 6 == 3:
            lo, hi = 24 * i, 24 * i + 16
            P.op(DVE, lambda e: e.tensor_tensor(out=modsb[:, lo:hi, :], in0=ps7[:, lo:hi, :],
                                                in1=adab[:, l, lo:hi].unsqueeze(2).broadcast_to([128, 16, 2]), op=ALU.add),
                 reads=[self.psb[7], self.b_adab], writes=[bm])
            P.op(DVE, lambda e: e.tensor_scalar(out=asc[:, i, :, :], in0=modsb[:, lo + 8:lo + 16, :], scalar1=1.0,
                                                scalar2=None, op0=ALU.add), reads=[bm], writes=[bm])
            P.op(DVE, lambda e: e.tensor_tensor(out=asc[:, i, :, :], in0=asc[:, i, :, :],
                                                in1=normg[:, l, i, :].unsqueeze(2).broadcast_to([128, 8, 2]), op=ALU.mult),
                 reads=[bm, self.b_normg], writes=[bm])
            self.mod_done[(l, i)] = True
        if s % 6 == 5:
            bg = self.b_gate[l][i]
            lo, hi = 24 * i + 16, 24 * i + 24
            P.op(DVE, lambda e: e.tensor_tensor(out=modsb[:, lo:hi, :], in0=ps7[:, lo:hi, :],
                                                in1=adab[:, l, lo:hi].unsqueeze(2).broadcast_to([128, 8, 2]), op=ALU.add),
                 reads=[self.psb[7], self.b_adab], writes=[bg])
            if i == 1 and l == 1:
                P.op(DVE, lambda e: e.tensor_tensor(out=gsc[:, i, :, :], in0=modsb[:, lo:hi, :],
                                                    in1=pscale[:, :].unsqueeze(2).broadcast_to([128, 8, 2]), op=ALU.mult),
                     reads=[bg, self.b_pscale], writes=[bg])
            else:
                f = 1.0 if i == 1 else 0.5
                P.op(DVE, lambda e: e.tensor_scalar(out=gsc[:, i, :, :], in0=modsb[:, lo:hi, :], scalar1=f,
                                                    scalar2=None, op0=ALU.mult), reads=[bg], writes=[bg])
            self.mod_done[(l, i, "g")] = True

    def mod_pump(self, n):
        for _ in range(n):
            if not self.mod_q:
                return
            l, s = self.mod_q.popleft()
            self.mod_emit(l, s)

    def mod_need(self, l, i, gate=False):
        key = (l, i, "g") if gate else (l, i)
        while not self.mod_done.get(key):
            self.mod_pump(1)

    def load_x(self, dram, row0, xbuf, tiles, bx, parents, stg=None):
        P = self.P
        stg = [self.sq[:, 0:4, :].rearrange("p a b -> p (a b)"), self.sq[:, 4:8, :].rearrange("p a b -> p (a b)"),
               self.sq2[:, 0:4, :].rearrange("p a b -> p (a b)"), self.sq2[:, 4:8, :].rearrange("p a b -> p (a b)")]
        b_stg = [Buf("stg%d" % i) for i in range(4)]
        Prog.inherit(b_stg, [self.b_sq, self.b_sq2])
        for ti, (r0, npk, c0) in enumerate(tiles):
            k = self.stg_i % 4
            self.stg_i += 1
            st, bst = stg[k], b_stg[k]
            P.dma(SP, st[0:npk, :], dram[row0 + r0:row0 + r0 + npk, :], writes=[bst], key="stg%d" % k)
            for half in range(2):
                ps, pb = self.psum()
                def tr(e, ps=ps, st=st, half=half, npk=npk):
                    ins = None
                    for j in range(4):
                        c = 4 * half + j
                        ins = e.transpose(out=ps[:, j * 128:j * 128 + npk], in_=st[0:npk, c * 128:(c + 1) * 128],
                                          identity=self.ident[0:npk, 0:npk])
                    return ins
                P.op(PE, tr, reads=[bst, self.b_ident], writes=[pb])
                src = ps[:, :].rearrange("p (j n) -> p j n", n=128)[:, :, 0:npk]
                dst = xbuf[:, 4 * half:4 * half + 4, c0:c0 + npk]
                P.op(ACT if half == 0 else DVE,
                     (lambda e, dst=dst, src=src: e.activation(out=dst, in_=src, func=AF.Copy)) if half == 0 else
                     (lambda e, dst=dst, src=src: e.tensor_copy(out=dst, in_=src)),
                     reads=[pb], writes=[bx[parents[ti]]])
        Prog.inherit([self.b_sq, self.b_sq2], b_stg)

    def norm(self, xbuf, hbuf, subs, bx, bh, l, i, tiles=None):
        P = self.P
        if tiles is None:
            tiles = self.ffn_tiles_M if len(subs) == len(self.sub_M) else self.ffn_tiles_R
        asc, modsb, bm = self.asc[l], self.modsb[l], self.b_mod[l][i]
        rsall = self.rsall
        b_rs = [Buf("rs_t%d" % t) for t in range(len(tiles))]
        Prog.inherit(b_rs, getattr(self, "b_rs_prev", []))
        self.b_rs_prev = b_rs
        pend = None

        def fin(pd):
            ps, pb, c0, n, par = pd
            P.op(ACT, lambda e: e.activation(out=rsall[:, c0:c0 + n], in_=ps[:, 0:n], func=AF.Sqrt, bias=EPS, scale=1.0 / D),
                 reads=[pb], writes=[b_rs[par]])
            P.op(DVE, lambda e: e.reciprocal(out=rsall[:, c0:c0 + n], in_=rsall[:, c0:c0 + n]),
                 reads=[b_rs[par]], writes=[b_rs[par]])

        for (c0, n, v, par) in subs:
            k = self.nrm_i % 2
            self.nrm_i += 1
            sq, bsq = (self.sq, self.b_sq) if k == 0 else (self.sq2, self.b_sq2)
            P.op(ACT, lambda e, sq=sq, c0=c0, n=n: e.activation(out=sq[:, :, 0:n], in_=xbuf[:, :, c0:c0 + n], func=AF.Square),
                 reads=[bx[par]], writes=[bsq])
            ps, pb = self.psum()
            P.op(PE, mm_group([(ps[:, 0:n], self.ones[:], sq[:, c, 0:n], c == 0, c == 7) for c in range(8)]),
                 reads=[bsq, self.b_ones], writes=[pb])
            if pend is not None:
                fin(pend)
            pend = (ps, pb, c0, n, par)
        fin(pend)
        self.mod_need(l, i)
        for ti, (c0, n, v) in enumerate(tiles):
            for c in range(8):
                k2 = self.nt_i % 2
                self.nt_i += 1
                nt, bnt = self.nt[k2], self.b_nt[k2]
                P.op(DVE, lambda e, nt=nt, c=c, c0=c0, n=n, v=v: e.scalar_tensor_tensor(
                    out=nt[:, 0:n], in0=xbuf[:, c, c0:c0 + n], scalar=asc[:, i, c, v:v + 1], in1=rsall[:, c0:c0 + n],
                    op0=ALU.mult, op1=ALU.mult), reads=[bx[ti], b_rs[ti], bm], writes=[bnt])
                P.op(ACT, lambda e, nt=nt, c=c, c0=c0, n=n, v=v: e.activation(
                    out=hbuf[:, c, c0:c0 + n], in_=nt[:, 0:n], func=AF.Identity,
                    bias=modsb[:, 24 * i + c, v:v + 1], scale=1.0), reads=[bnt, bm], writes=[bh[ti]])

    def ffn(self, l, s, xbuf, hbuf, abuf, tiles, bx, bh, ba):
        P = self.P
        gi = 0 if s == 0 else 2
        gsc, bm = self.gsc[l], self.b_gate[l][gi]
        for sl in range(11):
            W, bW = self.wload(self.d_w1[l, s, sl], [128, 8, 512])
            for ti, (c0, n, v) in enumerate(tiles):
                for jj in range(2):
                    j = 2 * sl + jj
                    pg, bg = self.psum()
                    pu, bu = self.psum()
                    P.op(PE, mm_group([(pg[:, 0:n], W[:, c, jj * 256:jj * 256 + 128], hbuf[:, c, c0:c0 + n], c == 0, c == 7)
                                       for c in range(8)]), reads=[bW, bh[ti]], writes=[bg])
                    P.op(PE, mm_group([(pu[:, 0:n], W[:, c, jj * 256 + 128:jj * 256 + 256], hbuf[:, c, c0:c0 + n], c == 0, c == 7)
                                       for c in range(8)]), reads=[bW, bh[ti]], writes=[bu])
                    k = self.sg_i % 2
                    self.sg_i += 1
                    sg, bsg = self.sg[k], self.b_sg[k]
                    P.op(ACT, lambda e, sg=sg, pg=pg, n=n: e.activation(out=sg[:, 0:n], in_=pg[:, 0:n], func=AF.Silu),
                         reads=[bg], writes=[bsg])
                    P.op(DVE, lambda e, sg=sg, pu=pu, n=n, j=j, c0=c0: e.tensor_tensor(
                        out=abuf[:, j, c0:c0 + n], in0=pu[:, 0:n], in1=sg[:, 0:n], op=ALU.mult),
                        reads=[bu, bsg], writes=[ba[ti]])
            self.mod_pump(2)
        self.mod_need(l, gi, gate=True)
        for g in range(4):
            Wa, bWa = self.wload(self.d_w2[l, s, g, 0], [128, 11, 256])
            Wb, bWb = self.wload(self.d_w2[l, s, g, 1], [128, 11, 256])
            for ti, (c0, n, v) in enumerate(tiles):
                for dd in range(2):
                    d = 2 * g + dd
                    py, by = self.psum()
                    mms = []
                    for j in range(NJ):
                        Wx = Wa if j < 11 else Wb
                        mms.append((py[:, 0:n], Wx[:, j % 11, dd * 128:(dd + 1) * 128], abuf[:, j, c0:c0 + n], j == 0, j == NJ - 1))
                    P.op(PE, mm_group(mms), reads=[bWa, bWb, ba[ti]], writes=[by])
                    P.op(DVE, lambda e, py=py, n=n, d=d, c0=c0, v=v: e.scalar_tensor_tensor(
                        out=xbuf[:, d, c0:c0 + n], in0=py[:, 0:n], scalar=gsc[:, gi, d, v:v + 1], in1=xbuf[:, d, c0:c0 + n],
                        op0=ALU.mult, op1=ALU.add), reads=[by, bm, bx[ti]], writes=[bx[ti]])
            self.mod_pump(1)

    def kv_tile(self, hsrc, c0, npk, bh, Wkv, bWkv, kT, V, bkT, bV, kcol, vt, rope_t, out_row, T, par_=None):
        P = self.P
        if par_ is None:
            par_ = T["i"] % 2
        tA, tB, kst, small, bT, bkst = T["tA"][par_], T["tB"][par_], T["kst"][par_], self.small, T["bT"][par_], T["bkst"][par_]
        bsm = T["bsm"][par_]
        sc = slice(50 + 2 * par_, 52 + 2 * par_)
        T["i"] += 1
        ps, pb = self.psum()
        P.op(PE, mm_group([(ps[0:npk, 0:256], hsrc[:, c, c0:c0 + npk], Wkv[:, c, :], c == 0, c == 7) for c in range(8)]),
             reads=[bWkv, bh], writes=[pb])
        yield
        P.op(ACT, lambda e: e.activation(out=tA[0:npk, 0:128], in_=ps[0:npk, 0:128], func=AF.Square),
             reads=[pb], writes=[bT])
        yield
        P.op(DVE, lambda e: e.tensor_reduce(out=small[0:npk, sc], in_=tA[0:npk, 0:128].rearrange("p (h d) -> p h d", d=64),
                                            axis=AX.X, op=ALU.add), reads=[bT], writes=[bsm])
        yield
        P.op(ACT, lambda e: e.activation(out=small[0:npk, sc], in_=small[0:npk, sc], func=AF.Sqrt, bias=EPS, scale=1.0 / 64),
             reads=[bsm], writes=[bsm])
        yield
        P.op(DVE, lambda e: e.reciprocal(out=small[0:npk, sc], in_=small[0:npk, sc]), reads=[bsm], writes=[bsm])
        yield
        P.op(DVE, lambda e: e.tensor_tensor(out=tB[0:npk, 0:128].rearrange("p (h d) -> p h d", d=64),
                                            in0=ps[0:npk, 0:128].rearrange("p (h d) -> p h d", d=64),
                                            in1=small[0:npk, sc].unsqueeze(2).broadcast_to([npk, 2, 64]), op=ALU.mult),
             reads=[pb, bsm], writes=[bT])
        yield
        P.op(DVE, lambda e: e.tensor_tensor(out=kst[0:npk, 0:128], in0=tB[0:npk, 0:128], in1=self.kg[0:npk, :], op=ALU.mult),
             reads=[bT, self.b_kg], writes=[bkst])
        yield
        P.op(ACT, lambda e: e.activation(out=V[0:npk, vt, :], in_=ps[0:npk, 128:256], func=AF.Copy), reads=[pb], writes=[bV])
        yield
        ksrc = kst
        if out_row is not None:
            P.op(ACT, lambda e: e.activation(out=kst[0:npk, 128:256], in_=ps[0:npk, 128:256], func=AF.Copy), reads=[pb], writes=[bkst])
            yield
            P.dma(SP, self.o_nk[out_row:out_row + npk, :], kst[0:npk, 0:128], reads=[bkst], key=bkst.name + "k")
            yield
            P.dma(SP, self.o_nv[out_row:out_row + npk, :], kst[0:npk, 128:256], reads=[bkst], key=bkst.name + "v")
            yield
        if rope_t is not None:
            yield from self.rope(kst[0:npk, 0:128], tA[0:npk, 0:128], tB[0:npk, 0:128], 2, npk, rope_t, [bkst], bT)
            ksrc = tA
        ps2, pb2 = self.psum()
        P.op(PE, lambda e: e.transpose(out=ps2[:, 0:npk], in_=ksrc[0:npk, 0:128], identity=self.ident[0:npk, 0:npk]),
             reads=[bkst, bT, self.b_ident], writes=[pb2])
        yield
        P.op(ACT, lambda e: e.activation(out=kT[:, kcol:kcol + npk], in_=ps2[:, 0:npk], func=AF.Copy), reads=[pb2], writes=[bkT])
        yield

    def interleave(self, gens, width=2):
        pending = deque(gens)
        active = []
        while pending or active:
            while pending and len(active) < width:
                active.append(pending.popleft())
            for g in list(active):
                try:
                    next(g)
                except StopIteration:
                    active.remove(g)

    def interleave_w(self, gens_w):
        active = [[g, w] for g, w in gens_w]
        while active:
            for gw in list(active):
                g, w = gw
                for _ in range(w):
                    try:
                        next(g)
                    except StopIteration:
                        active.remove(gw)
                        break

    def rope(self, x, t1, t2, H, npk, rt, bx_list, bT, bT2=None):
        P = self.P
        cosf, sinf = self.ropec[0:npk, rt, :], self.ropes[0:npk, rt, :]
        v5 = lambda a: a.rearrange("p (h r f s) -> p h r f s", h=H, r=2, f=2, s=16)
        c4 = cosf.rearrange("p (r f s) -> p r f s", r=2, f=2, s=16)
        s4 = sinf.rearrange("p (r f s) -> p r f s", r=2, f=2, s=16)
        P.op(DVE, lambda e: e.tensor_tensor(out=v5(t1), in0=v5(x), in1=c4.unsqueeze(1).broadcast_to([npk, H, 2, 2, 16]), op=ALU.mult),
             reads=bx_list + [self.b_rope], writes=[bT])
        yield
        for f in range(2):
            P.op(DVE, lambda e, f=f: e.tensor_tensor(out=v5(t2)[:, :, :, f, :], in0=v5(x)[:, :, :, 1 - f, :],
                                                     in1=s4[:, :, f, :].unsqueeze(1).broadcast_to([npk, H, 2, 16]), op=ALU.mult),
                 reads=bx_list + [self.b_rope], writes=[bT2 or bT])
            yield
        P.op(DVE, lambda e: e.tensor_tensor(out=t1, in0=t1, in1=t2, op=ALU.add), reads=[bT, bT2 or bT], writes=[bT])
        yield

    def mixer0(self, a_M, ba_M):
        P = self.P
        self.ropec = self.sb("ropec", [128, 9, 64])
        self.ropes = self.sb("ropes", [128, 9, 64])
        self.cmask = self.sb("cmask", [128, NS_COLS])
        self.b_rope, self.b_cmask = Buf("rope"), Buf("cmask")
        P.dma(SP, self.ropec[:], self.d_ropec[:, :, :], writes=[self.b_rope], key="ropec")
        P.dma(SP, self.ropes[:], self.d_ropes[:, :, :], writes=[self.b_rope], key="ropes")
        P.dma(SP, self.cmask[:], self.d_cmask[:, :], writes=[self.b_cmask])
        P.dma(SP, self.qg[:], self.d_qg[:, :], writes=[self.b_qg])
        P.dma(SP, self.kg[:], self.d_kg[:, :], writes=[self.b_kg])
        aR = self.av(0, NJ * NR).rearrange("p (j n) -> p j n", n=NR)
        xR = self.av(NJ * NR, 2 * 8 * NR, F32).rearrange("p (c n) -> p c n", n=NR)
        hR = self.hbuf[:, :, 0:NR]
        b_aR = [Buf("aR%d" % i) for i in range(2)]
        b_xR = [Buf("xR%d" % i) for i in range(2)]
        Prog.inherit(b_aR + b_xR, ba_M)
        bhR = [Buf("hR0"), Buf("hR1")]
        Prog.inherit(bhR, self.bh_M)
        stg = [self.sq[:, 0:4, :].rearrange("p a b -> p (a b)"), self.sq[:, 4:8, :].rearrange("p a b -> p (a b)")]
        self.load_x(self.d_xs, NS_COLS, xR, [(c0, npk, c0) for (c0, npk, par) in self.tok_R], b_xR,
                    [par for (c0, npk, par) in self.tok_R], stg)
        self.norm(xR, hR, self.sub_R, b_xR, bhR, 0, 0)
        self.ffn(0, 0, xR, hR, aR, self.ffn_tiles_R, b_xR, bhR, b_aR)
        self.norm(xR, hR, self.sub_R, b_xR, bhR, 0, 1)

        kT = self.av(0, 2560)
        V = self.av(2560, 21 * 128).rearrange("p (t f) -> p t f", f=128)
        b_kT = [Buf("kT_p%d" % i) for i in range(4)] + [Buf("kT_s")]
        b_V = [Buf("V_p%d" % i) for i in range(4)] + [Buf("V_s")]
        Prog.inherit(b_kT + b_V, b_aR)
        T = {"tA": [self.sg[0][:, 0:128], self.sg[0][:, 256:384]], "tB": [self.sg[0][:, 128:256], self.sg[0][:, 384:512]],
             "kst": [self.sg[1][:, 0:256], self.sg[1][:, 256:512]], "i": 0,
             "bT": [Buf("kvT0"), Buf("kvT1")], "bkst": [Buf("kst0"), Buf("kst1")], "bsm": [Buf("smk0"), Buf("smk1")]}
        Prog.inherit(T["bT"] + T["bkst"], self.b_sg)
        Wkv, bWkv = self.wload(self.d_wkv[:, :, :], [128, 8, 256])
        self.interleave([self.kv_tile(hR, c0, npk, bhR[par], Wkv, bWkv, kT, V, b_kT[4], b_V[4], 1824 + c0, 15 + i, 3 + i, None, T)
                         for i, (c0, npk, par) in enumerate(self.tok_R)])
        Prog.inherit(self.bh_M, bhR)
        self.norm(self.xres, self.hbuf, self.sub_M, self.bx_M, self.bh_M, 0, 1)
        qT = self.av(5248, 4 * NM).rearrange("p (s n) -> p s n", n=NM)
        attnT = self.av(10496, 4 * NM).rearrange("p (s n) -> p s n", n=NM)
        convT = self.av(15744, 4 * NM).rearrange("p (s n) -> p s n", n=NM)
        pT = [self.av(20992 + 512 * k, 512) for k in range(3)]
        T1s = [self.av(22528, 1024, F32), self.av(25600, 1024, F32)]
        T2s = [self.av(23552, 1024, F32), self.av(20992, 1024, F32)]
        T3s = [self.av(24576, 1024, F32), self.av(15744, 1024, F32)]
        rd = self.av(25600, 1024, F32)
        ckst = self.av(26624, 1024, F32).rearrange("p (t f) -> p t f", f=128)
        b_qT = [Buf("qT_%d" % i) for i in range(5)]
        b_attnT = [Buf("attnT%d" % i) for i in range(3)]
        b_convT = Buf("convT")
        b_pT = [Buf("pT%d" % k) for k in range(3)]
        b_T1s, b_T3s, b_rd, b_ckst = [Buf("T1a"), Buf("T1b")], [Buf("T3a"), Buf("T3b")], Buf("rd"), Buf("ckst")
        b_smq = [Buf("smq0"), Buf("smq1")]
        allnew = b_qT + b_attnT + [b_convT] + b_pT + b_T1s + b_T3s + [b_rd, b_ckst]
        Prog.inherit(allnew, b_aR + b_xR)
        P.dma(SP, ckst[:, :, :], self.d_ck.rearrange("(t p) f -> p t f", p=128), writes=[b_ckst])
        P.dma(POOL, V[:, 8:12, :], self.d_cv.rearrange("(t p) f -> p t f", p=128), writes=[b_V[4]], key="cvload")
        for t in range(4):
            ps2, pb2 = self.psum()
            P.op(PE, lambda e, t=t, ps2=ps2: e.transpose(out=ps2[:, 0:128], in_=ckst[:, t, :], identity=self.ident[:]),
                 reads=[b_ckst, self.b_ident], writes=[pb2])
            P.op(ACT, lambda e, t=t, ps2=ps2: e.activation(out=kT[:, 1024 + 128 * t:1152 + 128 * t], in_=ps2[:, 0:128], func=AF.Copy),
                 reads=[pb2], writes=[b_kT[4]])
        Wq, bWq = self.wload(self.d_wq[:, :, :], [128, 8, 512])
        def mtile(i, c0, npk, par):
            is_s = i >= 8
            bi = 4 if is_s else i // 2
            if is_s:
                kcol, vt, rt, orow = 1536 + (c0 - 1024), 12 + (i - 8), (i - 8), None
            else:
                kcol, vt, rt, orow = c0, i, None, c0
            yield from self.kv_tile(self.hbuf, c0, npk, self.bh_M[par], Wkv, bWkv, kT, V, b_kT[bi], b_V[bi], kcol, vt, rt, orow, T, par_=i % 2)
            qp = i % 2
            T1, T2, b_T1, bsmq = T1s[qp], T2s[qp], b_T1s[qp], b_smq[qp]
            qc = slice(34 + 8 * qp, 42 + 8 * qp)
            ps, pb = self.psum()
            P.op(PE, mm_group([(ps[0:npk, 0:512], self.hbuf[:, c, c0:c0 + npk], Wq[:, c, :], c == 0, c == 7) for c in range(8)]),
                 reads=[bWq, self.bh_M[par]], writes=[pb])
            yield
            small = self.small
            v3 = lambda a: a.rearrange("p (h d) -> p h d", d=64)
            P.op(ACT, lambda e, ps=ps, npk=npk, T1=T1: e.activation(out=T1[0:npk, :], in_=ps[0:npk, :], func=AF.Square),
                 reads=[pb], writes=[b_T1])
            yield
            P.op(DVE, lambda e, npk=npk, T1=T1, qc=qc: e.tensor_reduce(out=small[0:npk, qc], in_=v3(T1[0:npk, :]), axis=AX.X, op=ALU.add),
                 reads=[b_T1], writes=[bsmq])
            yield
            P.op(ACT, lambda e, npk=npk, qc=qc: e.activation(out=small[0:npk, qc], in_=small[0:npk, qc], func=AF.Sqrt, bias=EPS,
                                                      scale=1.0 / 64), reads=[bsmq], writes=[bsmq])
            yield
            P.op(DVE, lambda e, npk=npk, qc=qc: e.reciprocal(out=small[0:npk, qc], in_=small[0:npk, qc]),
                 reads=[bsmq], writes=[bsmq])
            yield
            P.op(DVE, lambda e, ps=ps, npk=npk, T2=T2, qc=qc: e.tensor_tensor(out=v3(T2[0:npk, :]), in0=v3(ps[0:npk, :]),
                                                                in1=small[0:npk, qc].unsqueeze(2).broadcast_to([npk, 8, 64]),
                                                                op=ALU.mult), reads=[pb, bsmq], writes=[b_T1])
            yield
            P.op(DVE, lambda e, npk=npk, T1=T1, T2=T2: e.tensor_tensor(out=T1[0:npk, :], in0=T2[0:npk, :], in1=self.qg[0:npk, :], op=ALU.mult),
                 reads=[b_T1, self.b_qg], writes=[b_T1])
            yield
            qsrc = T1
            if is_s:
                yield from self.rope(T1[0:npk, :], T2[0:npk, :], T3s[qp][0:npk, :], 8, npk, rt, [b_T1], b_T1, b_T3s[qp])
                qsrc = T2
            pst, pbt = self.psum()
            def trq(e, pst=pst, qsrc=qsrc, npk=npk):
                ins = None
                for s4 in range(4):
                    ins = e.transpose(out=pst[:, s4 * 128:s4 * 128 + npk], in_=qsrc[0:npk, s4 * 128:(s4 + 1) * 128],
                                      identity=self.ident[0:npk, 0:npk])
                return ins
            P.op(PE, trq, reads=[b_T1, self.b_ident], writes=[pbt])
            yield
            P.op(ACT, lambda e, pst=pst, npk=npk, c0=c0: e.activation(
                out=qT[:, :, c0:c0 + npk], in_=pst[:, :].rearrange("p (s n) -> p s n", n=128)[:, :, 0:npk], func=AF.Copy),
                reads=[pbt], writes=[b_qT[bi]])
            yield


        self.interleave([mtile(i, c0, npk, par) for i, (c0, npk, par) in enumerate(self.tok_M)])

        assert not self.mod_q and self.mod_done.get((1, 2, "g")), "PSUM bank 7 still holds adaLN accumulators"
        Prog.inherit(b_pT + [b_rd], b_T1s)
        Prog.inherit([b_convT], b_T3s)
        s_chunks = [(1024 + 128 * t, 128, 8 + t) for t in range(4)] + [(1536, 128, 12), (1664, 128, 13), (1792, 32, 14)] + \
                   [(1824 + 128 * i, 128, 15 + i) for i in range(5)] + [(2464, 96, 20)]
        groups = []
        for bi in range(4):
            for hh in range(2):
                for sp in range(2):
                    groups.append((bi, hh, (2 * sp, 2 * sp + 2), bi * 256, 256,
                                   [(bi * 256 + 128 * kc, 128, 2 * bi + kc) for kc in range(2)], b_attnT[bi // 2]))
        for hh in range(2):
            for s4 in range(4):
                groups.append((4, hh, (s4, s4 + 1), 1024, NS_COLS, s_chunks, b_attnT[2]))
        rounds = [(g, ci) for g in range(len(groups)) for ci in range(len(groups[g][5]))]
        st = {}

        def emit_s(ri):
            g, ci = rounds[ri]
            bi, hh, (s0, s1), qc0, qn, chunks, bat = groups[g]
            kcol, npk, vt = chunks[ci]
            ps, pb = self.psum_pool("attn", (2, 3, 4, 5, 6, 7))
            ncol = (s1 - s0) * qn
            hs = slice(hh * 64, hh * 64 + 64)
            P.op(PE, lambda e: e.matmul(ps[0:npk, 0:ncol], lhsT=kT[hs, kcol:kcol + npk], rhs=qT[hs, s0:s1, qc0:qc0 + qn],
                                        start=True, stop=True), reads=[b_kT[bi], b_qT[bi]], writes=[pb])
            st[ri] = (ps, pb, ncol)

        def attn_gen():
            emit_s(0)
            yield
            acc = {}
            for ri in range(len(rounds)):
                if ri + 1 < len(rounds):
                    emit_s(ri + 1)
                    yield
                g, ci = rounds[ri]
                bi, hh, (s0, s1), qc0, qn, chunks, bat = groups[g]
                kcol, npk, vt = chunks[ci]
                ps, pb, ncol = st.pop(ri)
                k = ri % 3
                p, bp = pT[k], b_pT[k]
                P.op(ACT, lambda e, p=p, ps=ps, npk=npk, ncol=ncol: e.activation(out=p[0:npk, 0:ncol], in_=ps[0:npk, 0:ncol],
                                                                                 func=AF.Exp, scale=0.125), reads=[pb], writes=[bp])
                yield
                if ci == 0:
                    acc[g] = (self.psum_pool("attn", (2, 3, 4, 5, 6, 7), hold=True), self.psum_pool("attn", (2, 3, 4, 5, 6, 7), hold=True))
                (pn, bn), (pd, bd) = acc[g]
                last = ci == len(chunks) - 1
                P.op(PE, lambda e, pn=pn, p=p, npk=npk, ncol=ncol, vt=vt, ci=ci, last=last: e.matmul(
                    pn[:, 0:ncol], lhsT=V[0:npk, vt, :], rhs=p[0:npk, 0:ncol], start=(ci == 0), stop=last),
                    reads=[bp, b_V[bi]], writes=[bn])
                yield
                P.op(PE, lambda e, pd=pd, p=p, npk=npk, ncol=ncol, ci=ci, last=last: e.matmul(
                    pd[:, 0:ncol], lhsT=self.onesb[0:npk, :], rhs=p[0:npk, 0:ncol], start=(ci == 0), stop=last),
                    reads=[bp, self.b_onesb], writes=[bd])
                yield
                if last:
                    hs = slice(hh * 64, hh * 64 + 64)
                    P.op(ACT, lambda e, pd=pd, hs=hs, ncol=ncol: e.activation(out=rd[hs, 0:ncol], in_=pd[hs, 0:ncol], func=AF.Ln),
                         reads=[bd], writes=[b_rd])
                    yield
                    P.op(ACT, lambda e, hs=hs, ncol=ncol: e.activation(out=rd[hs, 0:ncol], in_=rd[hs, 0:ncol], func=AF.Exp, scale=-1.0),
                         reads=[b_rd], writes=[b_rd])
                    yield
                    P.op(DVE, lambda e, pn=pn, hs=hs, ncol=ncol, s0=s0, s1=s1, qc0=qc0, qn=qn: e.tensor_tensor(
                        out=attnT[hs, s0:s1, qc0:qc0 + qn], in0=pn[hs, 0:ncol].rearrange("p (s n) -> p s n", n=qn),
                        in1=rd[hs, 0:ncol].rearrange("p (s n) -> p s n", n=qn), op=ALU.mult),
                        reads=[bn, b_rd], writes=[bat])
                    yield
                    self.psum_release(bn, fresh=True)
                    self.psum_release(bd)
                    del acc[g]


        def conv_gen():
            upad = self.sq[:, :, :].rearrange("p a b -> p (a b)")
            cacc = self.sq2[:, :, :].rearrange("p a b -> p (a b)")
            bgs = self.rsall
            xcs = [self.nt[0], self.nt[1]]
            b_upad, b_cacc, b_bgs, b_xcs = self.b_sq, self.b_sq2, Buf("bgs"), self.b_nt
            Prog.inherit([b_bgs], getattr(self, "b_rs_prev", []))
            self.b_rs_prev = [b_bgs]
            P.op(DVE, lambda e: e.memset(upad[:, :], 0.0), writes=[b_upad])
            yield
            cw = self.convw
            for cc in range(4):
                Wc, bWc = self.wload(self.d_wcv[cc], [128, 8, 384])
                for ti, (c0, n, v) in enumerate(self.ffn_tiles_M):
                    def proj(q3, pq, bq):
                        return P.op(PE, mm_group([(pq[:, 0:n], Wc[:, c, q3 * 128:(q3 + 1) * 128], self.hbuf[:, c, c0:c0 + n],
                                                   c == 0, c == 7) for c in range(8)]), reads=[bWc, self.bh_M[ti]], writes=[bq])
                    pxc, bpxc = self.psum_pool("conv", (0, 1))
                    proj(2, pxc, bpxc)
                    yield
                    pcg, bpcg = self.psum_pool("conv", (0, 1))
                    proj(1, pcg, bpcg)
                    yield
                    k = (cc * 3 + ti) % 2
                    xc_, bxc = xcs[k], b_xcs[k]
                    P.op(ACT, lambda e, xc_=xc_, n=n, px=pxc: e.activation(out=xc_[:, 0:n], in_=px[:, 0:n], func=AF.Copy),
                         reads=[bpxc], writes=[bxc])
                    yield
                    if ti < 2:
                        uo = upad[:, 1 + 2 * ti * 257:1 + (2 * ti + 2) * 257].rearrange("p (b k) -> p b k", k=257)[:, :, 0:256]
                        P.op(DVE, lambda e, uo=uo, pc=pcg, xc_=xc_: e.tensor_tensor(
                            out=uo, in0=pc[:, 0:512].rearrange("p (b k) -> p b k", k=256),
                            in1=xc_[:, 0:512].rearrange("p (b k) -> p b k", k=256), op=ALU.mult),
                            reads=[bpcg, bxc], writes=[b_upad])
                        yield
                    else:
                        P.op(DVE, lambda e, pc=pcg, xc_=xc_: e.tensor_tensor(out=upad[:, 1029:1317], in0=pc[:, 0:NS_COLS],
                                                                             in1=xc_[:, 0:NS_COLS], op=ALU.mult),
                             reads=[bpcg, bxc], writes=[b_upad])
                        yield
                    pbg, bpbg = self.psum_pool("conv", (0, 1))
                    proj(0, pbg, bpbg)
                    yield
                    P.op(ACT, lambda e, n=n, c0=c0, pb_=pbg: e.activation(out=bgs[:, c0:c0 + n], in_=pb_[:, 0:n], func=AF.Copy),
                         reads=[bpbg], writes=[b_bgs])
                    yield
                P.op(DVE, lambda e: e.tensor_tensor(out=upad[:, 1029:1317], in0=upad[:, 1029:1317], in1=self.cmask[:, :], op=ALU.mult),
                     reads=[b_upad, self.b_cmask], writes=[b_upad])
                yield
                P.op(DVE, lambda e, cc=cc: e.tensor_scalar(out=cacc[:, 1:1317], in0=upad[:, 1:1317], scalar1=cw[:, cc, 1:2], scalar2=None,
                                                           op0=ALU.mult), reads=[b_upad, self.b_convw], writes=[b_cacc])
                yield
                P.op(DVE, lambda e, cc=cc: e.scalar_tensor_tensor(out=cacc[:, 1:1317], in0=upad[:, 0:1316], scalar=cw[:, cc, 0:1],
                                                                  in1=cacc[:, 1:1317], op0=ALU.mult, op1=ALU.add),
                     reads=[b_upad, self.b_convw, b_cacc], writes=[b_cacc])
                yield
                P.op(DVE, lambda e, cc=cc: e.scalar_tensor_tensor(out=cacc[:, 1:1317], in0=upad[:, 2:1318], scalar=cw[:, cc, 2:3],
                                                                  in1=cacc[:, 1:1317], op0=ALU.mult, op1=ALU.add),
                     reads=[b_upad, self.b_convw, b_cacc], writes=[b_cacc])
                yield
                P.op(DVE, lambda e, cc=cc: e.tensor_tensor(
                    out=convT[:, cc, 0:1024].rearrange("p (b k) -> p b k", k=256),
                    in0=cacc[:, 1:1029].rearrange("p (b k) -> p b k", k=257)[:, :, 0:256],
                    in1=bgs[:, 0:1024].rearrange("p (b k) -> p b k", k=256), op=ALU.mult),
                    reads=[b_cacc, b_bgs], writes=[b_convT])
                yield
                P.op(DVE, lambda e, cc=cc: e.tensor_tensor(out=convT[:, cc, 1024:NM], in0=cacc[:, 1029:1317], in1=bgs[:, 1024:NM],
                                                           op=ALU.mult), reads=[b_cacc, b_bgs], writes=[b_convT])
                yield


        self.interleave_w([(attn_gen(), 6), (conv_gen(), 1)])

        Woa, bWoa = self.wload(self.d_wo[0], [128, 4, 1024])
        Woc, bWoc = self.wload(self.d_wo[1], [128, 4, 1024])
        self.mod_need(0, 1, gate=True)
        gsc, bm = self.gsc[0], self.b_gate[0][1]
        for ti, (c0, n, v) in enumerate(self.ffn_tiles_M):
            for d in range(8):
                po, bpo = self.psum()
                mms = [(po[:, 0:n], Woa[:, s4, d * 128:(d + 1) * 128], attnT[:, s4, c0:c0 + n], s4 == 0, False) for s4 in range(4)]
                mms += [(po[:, 0:n], Woc[:, s4, d * 128:(d + 1) * 128], convT[:, s4, c0:c0 + n], False, s4 == 3) for s4 in range(4)]
                P.op(PE, mm_group(mms), reads=[bWoa, bWoc, b_attnT[ti], b_convT], writes=[bpo])
                P.op(DVE, lambda e, po=po, n=n, d=d, c0=c0, v=v: e.scalar_tensor_tensor(
                    out=self.xres[:, d, c0:c0 + n], in0=po[:, 0:n], scalar=gsc[:, 1, d, v:v + 1], in1=self.xres[:, d, c0:c0 + n],
                    op0=ALU.mult, op1=ALU.add), reads=[bpo, bm, self.bx_M[ti]], writes=[self.bx_M[ti]])
        self.cur_a = allnew + b_kT + b_V + b_aR + b_xR
        Prog.inherit(self.b_sg, T["bT"] + T["bkst"])

    def kv_tile_R(self, hR, c0, npk, bh, Wkv, bWkv, kT, V, bkT, bV, kcol, vt, ridx, T):
        self.kv_tile(hR, c0, npk, bh, Wkv, bWkv, kT, V, bkT, bV, kcol, vt, ("R", c0 // 128), None, T)

    def mixer1(self, ba_M):
        P = self.P
        z = self.av(0, 11 * 1024).rearrange("p (t f) -> p t f", f=1024)
        mpp = self.av(11264, 2048).rearrange("p (a g t) -> p a g t", g=4, t=256)
        mps = self.av(13312, 3456).rearrange("p (a g t) -> p a g t", g=4, t=NS_COLS)
        b_z = [Buf("z%d" % i) for i in range(5)]
        b_mpp, b_mps = Buf("mpp"), Buf("mps")
        Prog.inherit(b_z + [b_mpp, b_mps], self.cur_a)
        P.dma(POOL, mpp, self.d_mpp[:, :, :, :], writes=[b_mpp])
        P.dma(POOL, mps, self.d_mps[:, :, :, :], writes=[b_mps])
        Wp, bWp = self.wload(self.d_poolw[:, :, :, :], [128, 4, 2, 256])
        for tt, (c0, npk, par) in enumerate(self.tok_M):
            bz = b_z[4 if tt >= 8 else tt // 2]
            for half in range(2):
                ps, pb = self.psum()
                mms = []
                for g2 in range(2):
                    gi = 2 * half + g2
                    for kc in range(2):
                        mms.append((ps[0:npk, g2 * 256:(g2 + 1) * 256], self.hbuf[:, 2 * gi + kc, c0:c0 + npk], Wp[:, gi, kc, :],
                                    kc == 0, kc == 1))
                P.op(PE, mm_group(mms), reads=[bWp, self.bh_M[par]], writes=[pb])
                dst = z[0:npk, tt, half * 512:(half + 1) * 512]
                if half == 0:
                    P.op(ACT, lambda e, dst=dst, ps=ps, npk=npk: e.activation(out=dst, in_=ps[0:npk, :], func=AF.Copy),
                         reads=[pb], writes=[bz])
                else:
                    P.op(DVE, lambda e, dst=dst, ps=ps, npk=npk: e.tensor_copy(out=dst, in_=ps[0:npk, :]), reads=[pb], writes=[bz])
        self.mod_need(1, 1, gate=True)
        gsc, bm = self.gsc[1], self.b_gate[1][1]
        segs = [(bi, [(2 * bi, 128), (2 * bi + 1, 128)], 256, bi * 256, mpp, b_mpp, 0, bi // 2) for bi in range(4)]
        segs.append((4, [(8, 128), (9, 128), (10, 32)], NS_COLS, 1024, mps, b_mps, 1, 2))
        for (bi, stiles, Tn, c0, Mx, bMx, v, par) in segs:
            for fc in range(8):
                gi = fc // 2
                po, bpo = self.psum()
                mms = [(po[:, 0:Tn], z[0:npk, tt, fc * 128:(fc + 1) * 128], Mx[0:npk, sc, gi, 0:Tn], sc == 0, sc == len(stiles) - 1)
                       for sc, (tt, npk) in enumerate(stiles)]
                P.op(PE, mm_group(mms), reads=[b_z[bi], bMx], writes=[bpo])
                P.op(DVE, lambda e, po=po, Tn=Tn, fc=fc, c0=c0, v=v: e.scalar_tensor_tensor(
                    out=self.xres[:, fc, c0:c0 + Tn], in0=po[:, 0:Tn], scalar=gsc[:, 1, fc, v:v + 1], in1=self.xres[:, fc, c0:c0 + Tn],
                    op0=ALU.mult, op1=ALU.add), reads=[bpo, bm, self.bx_M[par]], writes=[self.bx_M[par]])
        self.cur_a = b_z + [b_mpp, b_mps]

    def final_out(self):
        P = self.P
        P.dma(SP, self.finalg[:], self.d_finalg[:, :], writes=[self.b_finalg])
        ot = [self.av(4096 * k, 2048, F32) for k in range(2)]
        jk = [self.av(8192 + 4096 * k, 2048, F32) for k in range(2)]
        b_ot = [Buf("ot0"), Buf("ot1")]
        b_jk = [Buf("jk0"), Buf("jk1")]
        b_sm = [Buf("smf0"), Buf("smf1")]
        Prog.inherit(b_ot + b_jk, self.cur_a)
        Prog.inherit(b_sm, [self.b_small])
        outs = [(128 * i, self.o_yp, 128 * i, i // 4) for i in range(8)] + \
               [(1024 + HALO + 128 * i, self.o_ys, 128 * i, 2) for i in range(2)]
        sm = self.small

        def otile(oi, c0, dram, r0, par):
            k = oi % 2
            o, bo, j, bj, bs = ot[k], b_ot[k], jk[k], b_jk[k], b_sm[k]
            for half in range(2):
                ps, pb = self.psum()

                def tr(e, ps=ps, half=half):
                    ins = None
                    for jj in range(4):
                        c = 4 * half + jj
                        ins = e.transpose(out=ps[:, jj * 128:(jj + 1) * 128], in_=self.xres[:, c, c0:c0 + 128],
                                          identity=self.ident[:])
                    return ins
                P.op(PE, tr, reads=[self.bx_M[par], self.b_ident], writes=[pb])
                yield
                P.op(ACT, lambda e, ps=ps, half=half: e.activation(out=o[:, half * 512:(half + 1) * 512], in_=ps[:, :],
                                                                   func=AF.Copy), reads=[pb], writes=[bo])
                yield
            P.op(ACT, lambda e: e.activation(out=j[:, :], in_=o[:, :], func=AF.Square, accum_out=sm[:, oi:oi + 1]),
                 reads=[bo], writes=[bj, bs])
            yield
            P.op(ACT, lambda e: e.activation(out=sm[:, oi:oi + 1], in_=sm[:, oi:oi + 1], func=AF.Sqrt, bias=EPS,
                                             scale=1.0 / D), reads=[bs], writes=[bs])
            yield
            P.op(DVE, lambda e: e.reciprocal(out=sm[:, oi:oi + 1], in_=sm[:, oi:oi + 1]), reads=[bs], writes=[bs])
            yield
            P.op(DVE, lambda e: e.scalar_tensor_tensor(out=o[:, :], in0=o[:, :], scalar=sm[:, oi:oi + 1],
                                                       in1=self.finalg[:, :], op0=ALU.mult, op1=ALU.mult),
                 reads=[bo, bs, self.b_finalg], writes=[bo])
            yield
            P.dma(SP, dram[r0:r0 + 128, :], o[:, :], reads=[bo], key="ot%d" % k)
            yield

        self.interleave([otile(oi, *t) for oi, t in enumerate(outs)])


def _host_inputs(inp):
    f = lambda a: np.ascontiguousarray(np.asarray(a, dtype=np.float32))
    x_prompt, x_sample, c, c_ctx = f(inp["x_prompt"]), f(inp["x_sample"]), f(inp["c"]), f(inp["c_ctx"])
    ada_w, ada_b, norm_g = f(inp["ada_w"]), f(inp["ada_b"]), f(inp["norm_g"])
    w1, w2 = f(inp["ffn_w1"]), f(inp["ffn_w2"])
    shared = {}
    shared["adaw"] = f(ada_w.reshape(2, 8, 128, 18, 512).transpose(0, 3, 2, 1, 4))
    shared["adab"] = f(ada_b.reshape(2, 72, 128).transpose(2, 0, 1))
    shared["normg"] = f(norm_g.reshape(2, 3, 8, 128).transpose(3, 0, 1, 2))
    shared["finalg"] = f(np.broadcast_to(f(inp["final_g"])[None, :], (128, 1024)))
    g = w1[..., :DFF].reshape(2, 2, 8, 128, 11, 2, 128)
    u = w1[..., DFF:].reshape(2, 2, 8, 128, 11, 2, 128)
    gu = np.stack([g, u], axis=6)
    shared["w1t"] = f(gu.transpose(0, 1, 4, 3, 2, 5, 6, 7).reshape(2, 2, 11, 128, 8, 512))
    w2r = w2.reshape(2, 2, 2, 11, 128, 4, 256)
    shared["w2t"] = f(w2r.transpose(0, 1, 5, 2, 4, 3, 6))
    return shared, x_prompt, x_sample, c, c_ctx


_NC_CACHE = {}


def _get_nc(stop_after=99):
    if stop_after not in _NC_CACHE:
        _NC_CACHE[stop_after] = Builder(stop_after).build()
    return _NC_CACHE[stop_after]


HEAD_PERM = [0, 4, 1, 5, 2, 6, 3, 7]


def _pool_matrix(gpos, S_total):
    n = len(gpos)
    out = np.zeros((4, n, n), np.float32)
    for gi, w in enumerate(POOL_WINDOWS):
        left = w // 2
        right = w - 1 - left
        for t in range(n):
            gt = gpos[t]
            if gt < 0 or gt >= S_total:
                continue
            lo = max(gt - left, 0)
            hi = min(gt + right + 1, S_total)
            inv = np.float32(1.0) / np.float32(hi - lo)
            for s in range(n):
                if lo <= gpos[s] < hi:
                    out[gi, s, t] += inv
            out[gi, t, t] -= 1.0
    return out


def _rope_tables(gtok):
    half = 32
    inv = 10000.0 ** (-np.arange(0, half, 2, dtype=np.float64) / half)
    row = (gtok // GRID_W).astype(np.float64)
    col = (gtok % GRID_W).astype(np.float64)
    ar = row[:, None] * inv[None, :]
    ac = col[:, None] * inv[None, :]
    cr, sr, cc, sc = np.cos(ar), np.sin(ar), np.cos(ac), np.sin(ac)
    cosf = np.concatenate([cr, cr, cc, cc], axis=1).astype(np.float32)
    sinf = np.concatenate([-sr, sr, -sc, sc], axis=1).astype(np.float32)
    return cosf, sinf


def make_in_maps(inp, cores=range(8)):
    f = lambda a: np.ascontiguousarray(np.asarray(a, dtype=np.float32))
    shared, x_prompt, x_sample, c, c_ctx = _host_inputs(inp)
    w_in, w_out = f(inp["mix_w_in"])[0], f(inp["mix_w_out"])[0]
    wq = w_in[:, 0:512].reshape(1024, 8, 64)[:, HEAD_PERM, :].reshape(1024, 512)
    shared["wq"] = f(wq.reshape(8, 128, 512).transpose(1, 0, 2))
    shared["wkv"] = f(w_in[:, 512:768].reshape(8, 128, 256).transpose(1, 0, 2))
    bg, cg, xc = w_in[:, 768:1280], w_in[:, 1280:1792], w_in[:, 1792:2304]
    wcv = np.stack([np.concatenate([bg[:, k * 128:(k + 1) * 128], cg[:, k * 128:(k + 1) * 128],
                                    xc[:, k * 128:(k + 1) * 128]], axis=1) for k in range(4)], axis=0)
    shared["wcv"] = f(wcv.reshape(4, 8, 128, 384).transpose(0, 2, 1, 3))
    wo_a = w_out[0:512].reshape(8, 64, 1024)
    wo_a = np.stack([np.concatenate([wo_a[s], wo_a[4 + s]], axis=0) for s in range(4)], axis=1)
    wo_c = w_out[512:1024].reshape(4, 128, 1024).transpose(1, 0, 2)
    shared["wo"] = f(np.stack([wo_a, wo_c], axis=0))
    shared["qg"] = f(np.broadcast_to(np.tile(f(inp["q_norm"])[0], 8)[None, :], (128, 512)))
    shared["kg"] = f(np.broadcast_to(np.tile(f(inp["k_norm"])[0], 2)[None, :], (128, 128)))
    shared["convw"] = f(f(inp["conv_w"])[0].T.reshape(4, 128, 3).transpose(1, 0, 2))
    shared["poolw"] = f(f(inp["pool_w"])[0].reshape(4, 2, 128, 256).transpose(2, 0, 1, 3))
    shared["pscale"] = f(f(inp["pool_scale"])[0].reshape(8, 128).T)
    mp = _pool_matrix(np.arange(256), 256)
    shared["mpp"] = f(mp.reshape(4, 2, 128, 256).transpose(2, 1, 0, 3))
    cache_k, cache_v = f(inp["cache_k"]), f(inp["cache_v"])
    maps = []
    for k in cores:
        b, r = k // 4, k % 4
        gwin = 256 * r - HALO + np.arange(1024)
        idx = gwin % 1024
        m = dict(shared)
        m["xp"] = f(x_prompt[4 * k:4 * k + 4].reshape(1024, 1024))
        m["xs"] = f(x_sample[b][idx])
        m["cvec"] = f(np.stack([c_ctx, c[b]], axis=-1).reshape(8, 128, 2).transpose(1, 0, 2))
        ms = _pool_matrix(gwin[:NS_COLS], 1024)
        msp = np.zeros((4, 384, NS_COLS), np.float32)
        msp[:, :NS_COLS] = ms
        m["mps"] = f(msp.reshape(4, 3, 128, NS_COLS).transpose(2, 1, 0, 3))
        cosf, sinf = _rope_tables(idx)
        def tab(a):
            a = np.concatenate([a, np.zeros((512, 64), np.float32)], axis=0)
            tl = [a[t * 128:(t + 1) * 128] for t in range(3)] + [a[NS_COLS + t * 128:NS_COLS + (t + 1) * 128] for t in range(6)]
            return f(np.stack(tl, axis=1))
        m["ropec"] = tab(cosf)
        m["ropes"] = tab(sinf)
        gw = gwin[:NS_COLS]
        m["cmask"] = f(np.broadcast_to(((gw >= 0) & (gw < 1024)).astype(np.float32)[None, :], (128, NS_COLS)))
        m["ck"] = f(cache_k[b, 0].reshape(512, 128))
        m["cv"] = f(cache_v[b, 0].reshape(512, 128))
        maps.append(m)
    return maps


def assemble(results, cores=range(8)):
    y_prompt = np.zeros((32, 256, 1024), np.float32)
    y_sample = np.zeros((2, 1024, 1024), np.float32)
    nk = np.zeros((32, 1, 256, 2, 64), np.float32)
    nv = np.zeros((32, 1, 256, 2, 64), np.float32)
    for res, k in zip(results, cores):
        b, r = k // 4, k % 4
        y_prompt[4 * k:4 * k + 4] = res["yp"].reshape(4, 256, 1024)
        y_sample[b, 256 * r:256 * r + 256] = res["ys"]
        nk[4 * k:4 * k + 4, 0] = res["nk"].reshape(4, 256, 2, 64)
        nv[4 * k:4 * k + 4, 0] = res["nv"].reshape(4, 256, 2, 64)
    return y_prompt, y_sample, nk, nv


def kernel(**inputs):
    nc = _get_nc()
    maps = make_in_maps(inputs)
    res = run_bass_kernel_spmd(nc, maps, core_ids=list(range(8)))
    return assemble(res.results)
```

```python
import numpy as np
from collections import deque
from contextlib import ExitStack
import concourse.bass as bass
import concourse.mybir as mybir
from concourse.bass_utils import run_bass_kernel_spmd

F32 = mybir.dt.float32
BF16 = mybir.dt.bfloat16
AF = mybir.ActivationFunctionType
ALU = mybir.AluOpType
AX = mybir.AxisListType
PE, ACT, DVE, POOL, SP = "tensor", "scalar", "vector", "gpsimd", "sync"
ENGS = (PE, ACT, DVE, POOL, SP)

D = 1024
DFF = 2816
NJ = 22
EPS = 1e-6
NP_COLS = 1024
NS_COLS = 288
HALO = 16
NM = NP_COLS + NS_COLS
NR = 1024 - NS_COLS
GRID_W = 64
POOL_WINDOWS = (2, 4, 8, 16)


class Buf:
    __slots__ = ("name", "w", "r")

    def __init__(self, name):
        self.name = name
        self.w = None
        self.r = []


class Op:
    __slots__ = ("eng", "fn", "deps", "marked", "sig", "is_dma", "key", "dval")

    def __init__(self, eng, fn, is_dma):
        self.eng = eng
        self.fn = fn
        self.deps = ()
        self.marked = False
        self.sig = 0
        self.is_dma = is_dma
        self.key = None
        self.dval = 0


class Prog:
    def __init__(self, nc):
        self.nc = nc
        self.ops = {e: [] for e in ENGS}
        self.dma_keys = {}

    def op(self, eng, fn, reads=(), writes=(), dma=False, key=None):
        o = Op(eng, fn, dma)
        deps = set()
        for b in reads:
            if b.w is not None:
                deps.add(b.w)
        for b in writes:
            if b.w is not None:
                deps.add(b.w)
            deps.update(b.r)
        if eng == PE and not dma:
            deps = {d for d in deps if not (d.eng == PE and not d.is_dma)}
        for d in deps:
            d.marked = True
        o.deps = deps
        for b in reads:
            b.r.append(o)
        for b in writes:
            b.w = o
            b.r = []
        if dma:
            if key is None:
                key = (writes[0] if writes else reads[0]).name
            o.key = key
            self.dma_keys[key] = self.dma_keys.get(key, 0) + 16
            o.dval = self.dma_keys[key]
        self.ops[eng].append(o)
        return o

    def dma(self, queue, out, in_, reads=(), writes=(), key=None):
        return self.op(queue, lambda e: e.dma_start(out=out, in_=in_), reads, writes, dma=True, key=key)

    @staticmethod
    def inherit(new_bufs, old_bufs):
        hz = []
        for b in old_bufs:
            if b.w is not None:
                hz.append(b.w)
            hz.extend(b.r)
        for nb in new_bufs:
            nb.r = list(nb.r) + hz

    def emit(self):
        nc = self.nc
        with ExitStack() as es:
            esem = {e: es.enter_context(nc.semaphore("s_" + e)) for e in ENGS}
            dsem = {k: es.enter_context(nc.semaphore("d%d" % i)) for i, k in enumerate(self.dma_keys)}
            for e in ENGS:
                c = 0
                for o in self.ops[e]:
                    if not o.is_dma and o.marked:
                        c += 1
                        o.sig = c
            block = es.enter_context(nc.Block())

            def run(e, eng):
                waited = {}
                for o in self.ops[e]:
                    need = {}
                    for d in o.deps:
                        if d.is_dma:
                            s, v = dsem[d.key], d.dval
                        else:
                            s, v = esem[d.eng], d.sig
                        if need.get(s, 0) < v:
                            need[s] = v
                    for s, v in need.items():
                        if waited.get(s, 0) < v:
                            eng.wait_ge(s, v)
                            waited[s] = v
                    ins = o.fn(eng)
                    if o.is_dma:
                        ins.then_inc(dsem[o.key], 16)
                    elif o.marked:
                        ins.then_inc(esem[e], 1)
                if e == SP:
                    for k, v in self.dma_keys.items():
                        eng.wait_ge(dsem[k], v)

            @block.tensor
            def _(eng):
                run(PE, eng)

            @block.scalar
            def _(eng):
                run(ACT, eng)

            @block.vector
            def _(eng):
                run(DVE, eng)

            @block.gpsimd
            def _(eng):
                run(POOL, eng)

            @block.sync
            def _(eng):
                run(SP, eng)


def mm_group(mms):
    def fn(e):
        ins = None
        for (o, l, r, st, sp) in mms:
            ins = e.matmul(o, lhsT=l, rhs=r, start=st, stop=sp)
        return ins
    return fn


class Builder:
    def __init__(self, stop_after=99):
        self.stop_after = stop_after
        self.nc = bass.Bass("TRN2", target_bir_lowering=False)
        self.P = Prog(self.nc)
        self.es = ExitStack()
        self.uid = 0

    def din(self, name, shape):
        return self.nc.dram_tensor(name, list(shape), F32, kind="ExternalInput").ap()

    def dout(self, name, shape):
        return self.nc.dram_tensor(name, list(shape), F32, kind="ExternalOutput").ap()

    def sb(self, name, shape, dt=F32):
        return self.es.enter_context(self.nc.sbuf_tensor("sb_" + name, list(shape), dt))

    def buf(self, name):
        self.uid += 1
        return Buf("%s#%d" % (name, self.uid))

    def av(self, off, n, dt=BF16):
        v = self.areg[:, off:off + n]
        if dt == F32:
            v = v.bitcast(F32)
        return v

    def psum(self, hold=False):
        while True:
            k = self.ps_i % 7
            self.ps_i += 1
            if k not in self.ps_hold:
                break
        if hold:
            self.ps_hold.add(k)
        return self.ps[k], self.psb[k]

    def psum_pool(self, name, banks, hold=False):
        tries = 0
        while True:
            i = self.ps_pool_i.get(name, 0)
            self.ps_pool_i[name] = i + 1
            k = banks[i % len(banks)]
            tries += 1
            if k in self.ps_hold:
                continue
            if k in self.ps_recent and tries <= len(banks):
                continue
            break
        if hold:
            self.ps_hold.add(k)
        return self.ps[k], self.psb[k]

    def psum_release(self, pb, fresh=False):
        k = self.psb.index(pb)
        self.ps_hold.discard(k)
        if fresh:
            self.ps_recent = set()
        self.ps_recent.add(k)

    def wload(self, src, shape):
        k = self.ring_i % len(self.ring)
        self.ring_i += 1
        npart = shape[0]
        n = int(np.prod(shape[1:]))
        v = self.ring[k][0:npart, 0:n]
        if len(shape) == 3:
            v = v.rearrange("p (a b) -> p a b", b=shape[2])
        elif len(shape) == 4:
            v = v.rearrange("p (a b c) -> p a b c", b=shape[2], c=shape[3])
        b = self.ringb[k]
        self.P.dma(POOL, v, src, writes=[b], key="ring%d" % k)
        return v, b

    def build(self):
        nc, P = self.nc, self.P
        self.d_xp = self.din("xp", [1024, 1024])
        self.d_xs = self.din("xs", [1024, 1024])
        self.d_cvec = self.din("cvec", [128, 8, 2])
        self.d_adaw = self.din("adaw", [2, 18, 128, 8, 512])
        self.d_adab = self.din("adab", [128, 2, 72])
        self.d_normg = self.din("normg", [128, 2, 3, 8])
        self.d_finalg = self.din("finalg", [128, 1024])
        self.d_w1 = self.din("w1t", [2, 2, 11, 128, 8, 512])
        self.d_w2 = self.din("w2t", [2, 2, 4, 2, 128, 11, 256])
        self.d_wq = self.din("wq", [128, 8, 512])
        self.d_wkv = self.din("wkv", [128, 8, 256])
        self.d_wcv = self.din("wcv", [4, 128, 8, 384])
        self.d_wo = self.din("wo", [2, 128, 4, 1024])
        self.d_qg = self.din("qg", [128, 512])
        self.d_kg = self.din("kg", [128, 128])
        self.d_qgc = self.din("qgc", [128, 1])
        self.d_convw = self.din("convw", [128, 4, 3])
        self.d_poolw = self.din("poolw", [128, 4, 2, 256])
        self.d_pscale = self.din("pscale", [128, 8])
        self.d_mpp = self.din("mpp", [128, 2, 4, 256])
        self.d_mps = self.din("mps", [128, 3, 4, 288])
        self.d_ropec = self.din("ropec", [128, 9, 64])
        self.d_ropes = self.din("ropes", [128, 9, 64])
        self.d_cmask = self.din("cmask", [128, 288])
        self.d_ck = self.din("ck", [512, 128])
        self.d_cv = self.din("cv", [512, 128])
        self.o_yp = self.dout("yp", [1024, 1024])
        self.o_ys = self.dout("ys", [256, 1024])
        self.o_nk = self.dout("nk", [1024, 128])
        self.o_nv = self.dout("nv", [1024, 128])

        self.xres = self.sb("xres", [128, 8, NM])
        self.hbuf = self.sb("hbuf", [128, 8, NM], BF16)
        self.areg = self.sb("areg", [128, NJ * NM], BF16)
        self.ring = [self.sb("ring%d" % i, [128, 4096], BF16) for i in range(5)]
        self.ringb = [Buf("ring%d" % i) for i in range(5)]
        self.ring_i = 0
        self.sq = self.sb("sq", [128, 8, 256])
        self.b_sq = Buf("sq")
        self.sq2 = self.sb("sq2", [128, 8, 256])
        self.b_sq2 = Buf("sq2")
        self.rsall = self.sb("rsall", [128, NM])
        self.nrm_i = 0
        self.stg_i = 0
        self.sg = [self.sb("sg%d" % i, [128, 512]) for i in range(2)]
        self.b_sg = [Buf("sg%d" % i) for i in range(2)]
        self.sg_i = 0
        self.nt = [self.sb("nt%d" % i, [128, 512]) for i in range(2)]
        self.b_nt = [Buf("nt%d" % i) for i in range(2)]
        self.nt_i = 0
        self.ident = self.sb("ident", [128, 128])
        self.ones = self.sb("ones", [128, 128])
        self.onesb = self.sb("onesb", [128, 128], BF16)
        self.b_ident, self.b_ones, self.b_onesb = Buf("ident"), Buf("ones"), Buf("onesb")
        self.cvec = self.sb("cvec", [128, 8, 2])
        self.scb = self.sb("scb", [128, 8, 2], BF16)
        self.adab = self.sb("adab", [128, 2, 72])
        self.normg = self.sb("normg", [128, 2, 3, 8])
        self.finalg = self.sb("finalg", [128, 1024])
        self.qg = self.sb("qg", [128, 512])
        self.kg = self.sb("kg", [128, 128])
        self.convw = self.sb("convw", [128, 4, 3])
        self.pscale = self.sb("pscale", [128, 8])
        self.b_cvec, self.b_scb, self.b_adab, self.b_normg = Buf("cvec"), Buf("scb"), Buf("adab"), Buf("normg")
        self.b_finalg, self.b_qg, self.b_kg, self.b_convw, self.b_pscale = (
            Buf("finalg"), Buf("qg"), Buf("kg"), Buf("convw"), Buf("pscale"))
        self.modsb = [self.sb("modsb%d" % l, [128, 72, 2]) for l in range(2)]
        self.asc = [self.sb("asc%d" % l, [128, 3, 8, 2]) for l in range(2)]
        self.gsc = [self.sb("gsc%d" % l, [128, 3, 8, 2]) for l in range(2)]
        self.b_mod = [[Buf("mod%d_%d" % (l, i)) for i in range(3)] for l in range(2)]
        self.b_gate = [[Buf("gate%d_%d" % (l, i)) for i in range(3)] for l in range(2)]
        self.small = self.sb("small", [128, 64])
        self.dummy = self.sb("dummy", [128, 8])
        self.b_dummy = Buf("dummy")
        self.qgc = self.sb("qgc", [128, 1])
        self.b_qgc = Buf("qgc")
        self.b_small = Buf("small")
        self.ps = [self.es.enter_context(nc.psum_tensor("ps%d" % i, [128, 512], F32)) for i in range(8)]
        self.psb = [Buf("ps%d" % i) for i in range(8)]
        self.ps_i = 0
        self.ps_hold = set()
        self.ps_pool_i = {}
        self.ps_recent = set()

        self.ffn_tiles_M = [(0, 512, 0), (512, 512, 0), (1024, 288, 1)]
        self.ffn_tiles_R = [(0, 512, 1), (512, 224, 1)]
        self.sub_M = [(256 * i, 256, 0, i // 2) for i in range(4)] + [(1024, 256, 1, 2), (1280, 32, 1, 2)]
        self.sub_R = [(0, 256, 1, 0), (256, 256, 1, 0), (512, 224, 1, 1)]
        self.tok_M = [(128 * i, 128, i // 4) for i in range(8)] + [(1024, 128, 2), (1152, 128, 2), (1280, 32, 2)]
        self.tok_R = [(128 * i, 128, 0) for i in range(4)] + [(512, 128, 1), (640, 96, 1)]
        self.bx_M = [Buf("xM%d" % i) for i in range(3)]
        self.bh_M = [Buf("hM%d" % i) for i in range(3)]

        self.setup_consts()
        self.mod_q = deque()
        for l in range(2):
            for s in range(18):
                self.mod_q.append((l, s))
        self.mod_done = {}

        stg = [self.sq[:, 0:4, :].rearrange("p a b -> p (a b)"), self.sq[:, 4:8, :].rearrange("p a b -> p (a b)")]
        self.load_x(self.d_xp, 0, self.xres, [(128 * i, 128, 128 * i) for i in range(8)], self.bx_M,
                    [i // 4 for i in range(8)], stg)
        self.load_x(self.d_xs, 0, self.xres, [(0, 128, 1024), (128, 128, 1152), (256, 32, 1280)], self.bx_M,
                    [2, 2, 2], stg)

        a_M = self.av(0, NJ * NM).rearrange("p (j n) -> p j n", n=NM)
        ba_M = [Buf("aM%d" % i) for i in range(3)]
        self.cur_a = ba_M

        for l in range(2):
            if self.stop_after < 10 * l + 1:
                break
            self.norm(self.xres, self.hbuf, self.sub_M, self.bx_M, self.bh_M, l, 0)
            self.ffn(l, 0, self.xres, self.hbuf, a_M, self.ffn_tiles_M, self.bx_M, self.bh_M, ba_M)
            if self.stop_after < 10 * l + 2:
                break
            self.mod_need(l, 1)
            if l == 0:
                self.mixer0(a_M, ba_M)
            else:
                self.norm(self.xres, self.hbuf, self.sub_M, self.bx_M, self.bh_M, l, 1)
                self.mixer1(ba_M)
            if self.stop_after < 10 * l + 3:
                break
            nb = [Buf("aM%d" % i) for i in range(3)]
            Prog.inherit(nb, self.cur_a)
            ba_M = nb
            self.cur_a = nb
            self.mod_need(l, 2)
            self.norm(self.xres, self.hbuf, self.sub_M, self.bx_M, self.bh_M, l, 2)
            self.ffn(l, 1, self.xres, self.hbuf, a_M, self.ffn_tiles_M, self.bx_M, self.bh_M, ba_M)

        self.final_out()
        P.emit()
        self.es.close()
        return nc

    def setup_consts(self):
        P = self.P
        P.dma(SP, self.cvec[:], self.d_cvec[:, :, :], writes=[self.b_cvec])
        P.dma(SP, self.adab[:], self.d_adab[:, :, :], writes=[self.b_adab])
        P.dma(SP, self.normg[:], self.d_normg[:, :, :, :], writes=[self.b_normg])
        P.dma(SP, self.pscale[:], self.d_pscale[:, :], writes=[self.b_pscale])
        P.dma(SP, self.convw[:], self.d_convw[:, :, :], writes=[self.b_convw])
        ident, ones, onesb = self.ident, self.ones, self.onesb
        P.op(DVE, lambda e: e.memset(ident[:], 0.0), writes=[self.b_ident])
        P.op(POOL, lambda e: e.affine_select(out=ident[:], in_=ident[:], pattern=[[-1, 128]],
                                             compare_op=ALU.not_equal, fill=1.0, base=0, channel_multiplier=1),
             reads=[self.b_ident], writes=[self.b_ident])
        P.op(DVE, lambda e: e.memset(ones[:], 1.0), writes=[self.b_ones])
        dummy = self.dummy
        P.op(DVE, lambda e: e.memset(dummy[:], 1.0), writes=[self.b_dummy])
        P.dma(SP, self.qgc[:], self.d_qgc[:, :], writes=[self.b_qgc])
        P.op(DVE, lambda e: e.memset(onesb[:], 1.0), writes=[self.b_onesb])
        cvec, scb = self.cvec, self.scb
        P.op(ACT, lambda e: e.activation(out=scb[:], in_=cvec[:], func=AF.Silu), reads=[self.b_cvec], writes=[self.b_scb])

    def mod_emit(self, l, s):
        P = self.P
        W, bW = self.wload(self.d_adaw[l, s], [128, 8, 512])
        ps7 = self.ps[7][:, 0:144].rearrange("p (m v) -> p m v", v=2)
        mms = []
        for mt in range(4):
            m = 4 * s + mt
            for c in range(8):
                mms.append((ps7[:, m, :], W[:, c, mt * 128:(mt + 1) * 128], self.scb[:, c, :], c == 0, c == 7))
        P.op(PE, mm_group(mms), reads=[bW, self.b_scb], writes=[self.psb[7]])
        i = s // 6
        bm = self.b_mod[l][i]
        modsb, adab, asc, gsc, normg, pscale = self.modsb[l], self.adab, self.asc[l], self.gsc[l], self.normg, self.pscale
        if s % 6 == 3:
            lo, hi = 24 * i, 24 * i + 16
            P.op(DVE, lambda e: e.tensor_tensor(out=modsb[:, lo:hi, :], in0=ps7[:, lo:hi, :],
                                                in1=adab[:, l, lo:hi].unsqueeze(2).broadcast_to([128, 16, 2]), op=ALU.add),
                 reads=[self.psb[7], self.b_adab], writes=[bm])
            P.op(DVE, lambda e: e.tensor_scalar(out=asc[:, i, :, :], in0=modsb[:, lo + 8:lo + 16, :], scalar1=1.0,
                                                scalar2=None, op0=ALU.add), reads=[bm], writes=[bm])
            P.op(DVE, lambda e: e.tensor_tensor(out=asc[:, i, :, :], in0=asc[:, i, :, :],
                                                in1=normg[:, l, i, :].unsqueeze(2).broadcast_to([128, 8, 2]), op=ALU.mult),
                 reads=[bm, self.b_normg], writes=[bm])
            self.mod_done[(l, i)] = True
        if s % 6 == 5:
            bg = self.b_gate[l][i]
            lo, hi = 24 * i + 16, 24 * i + 24
            P.op(DVE, lambda e: e.tensor_tensor(out=modsb[:, lo:hi, :], in0=ps7[:, lo:hi, :],
                                                in1=adab[:, l, lo:hi].unsqueeze(2).broadcast_to([128, 8, 2]), op=ALU.add),
                 reads=[self.psb[7], self.b_adab], writes=[bg])
            if i == 1 and l == 1:
                P.op(DVE, lambda e: e.tensor_tensor(out=gsc[:, i, :, :], in0=modsb[:, lo:hi, :],
                                                    in1=pscale[:, :].unsqueeze(2).broadcast_to([128, 8, 2]), op=ALU.mult),
                     reads=[bg, self.b_pscale], writes=[bg])
            else:
                f = 1.0 if i == 1 else 0.5
                P.op(DVE, lambda e: e.tensor_scalar(out=gsc[:, i, :, :], in0=modsb[:, lo:hi, :], scalar1=f,
                                                    scalar2=None, op0=ALU.mult), reads=[bg], writes=[bg])
            self.mod_done[(l, i, "g")] = True

    def act_prefetch(self, func):
        d = self.dummy
        self.P.op(ACT, lambda e: e.activation(out=d[0:1, 0:1], in_=d[0:1, 1:2], func=func), reads=[self.b_dummy], writes=[self.b_dummy])

    def mod_pump(self, n):
        for _ in range(n):
            if not self.mod_q:
                return
            l, s = self.mod_q.popleft()
            self.mod_emit(l, s)

    def mod_need(self, l, i, gate=False):
        key = (l, i, "g") if gate else (l, i)
        while not self.mod_done.get(key):
            self.mod_pump(1)

    def load_x(self, dram, row0, xbuf, tiles, bx, parents, stg=None):
        P = self.P
        stg = [self.sq[:, 0:4, :].rearrange("p a b -> p (a b)"), self.sq[:, 4:8, :].rearrange("p a b -> p (a b)"),
               self.sq2[:, 0:4, :].rearrange("p a b -> p (a b)"), self.sq2[:, 4:8, :].rearrange("p a b -> p (a b)")]
        b_stg = [Buf("stg%d" % i) for i in range(4)]
        Prog.inherit(b_stg, [self.b_sq, self.b_sq2])
        for ti, (r0, npk, c0) in enumerate(tiles):
            k = self.stg_i % 4
            self.stg_i += 1
            st, bst = stg[k], b_stg[k]
            P.dma(SP, st[0:npk, :], dram[row0 + r0:row0 + r0 + npk, :], writes=[bst], key="stg%d" % k)
            for half in range(2):
                ps, pb = self.psum()
                def tr(e, ps=ps, st=st, half=half, npk=npk):
                    ins = None
                    for j in range(4):
                        c = 4 * half + j
                        ins = e.transpose(out=ps[:, j * 128:j * 128 + npk], in_=st[0:npk, c * 128:(c + 1) * 128],
                                          identity=self.ident[0:npk, 0:npk])
                    return ins
                P.op(PE, tr, reads=[bst, self.b_ident], writes=[pb])
                src = ps[:, :].rearrange("p (j n) -> p j n", n=128)[:, :, 0:npk]
                dst = xbuf[:, 4 * half:4 * half + 4, c0:c0 + npk]
                P.op(ACT if half == 0 else DVE,
                     (lambda e, dst=dst, src=src: e.activation(out=dst, in_=src, func=AF.Copy)) if half == 0 else
                     (lambda e, dst=dst, src=src: e.tensor_copy(out=dst, in_=src)),
                     reads=[pb], writes=[bx[parents[ti]]])
        Prog.inherit([self.b_sq, self.b_sq2], b_stg)

    def norm(self, xbuf, hbuf, subs, bx, bh, l, i, tiles=None):
        P = self.P
        if tiles is None:
            tiles = self.ffn_tiles_M if len(subs) == len(self.sub_M) else self.ffn_tiles_R
        asc, modsb, bm = self.asc[l], self.modsb[l], self.b_mod[l][i]
        rsall = self.rsall
        b_rs = [Buf("rs_t%d" % t) for t in range(len(tiles))]
        Prog.inherit(b_rs, getattr(self, "b_rs_prev", []))
        self.b_rs_prev = b_rs
        pend = None

        def fin(pd):
            ps, pb, c0, n, par = pd
            P.op(ACT, lambda e: e.activation(out=rsall[:, c0:c0 + n], in_=ps[:, 0:n], func=AF.Sqrt, bias=EPS, scale=1.0 / D),
                 reads=[pb], writes=[b_rs[par]])
            P.op(DVE, lambda e: e.reciprocal(out=rsall[:, c0:c0 + n], in_=rsall[:, c0:c0 + n]),
                 reads=[b_rs[par]], writes=[b_rs[par]])

        for (c0, n, v, par) in subs:
            k = self.nrm_i % 2
            self.nrm_i += 1
            sq, bsq = (self.sq, self.b_sq) if k == 0 else (self.sq2, self.b_sq2)
            P.op(ACT, lambda e, sq=sq, c0=c0, n=n: e.activation(out=sq[:, :, 0:n], in_=xbuf[:, :, c0:c0 + n], func=AF.Square),
                 reads=[bx[par]], writes=[bsq])
            ps, pb = self.psum()
            P.op(PE, mm_group([(ps[:, 0:n], self.ones[:], sq[:, c, 0:n], c == 0, c == 7) for c in range(8)]),
                 reads=[bsq, self.b_ones], writes=[pb])
            if pend is not None:
                fin(pend)
            pend = (ps, pb, c0, n, par)
        fin(pend)
        if i != 1:
            self.act_prefetch(AF.Silu)
        self.mod_need(l, i)
        for ti, (c0, n, v) in enumerate(tiles):
            for c in range(8):
                k2 = self.nt_i % 2
                self.nt_i += 1
                nt, bnt = self.nt[k2], self.b_nt[k2]
                P.op(DVE, lambda e, nt=nt, c=c, c0=c0, n=n, v=v: e.scalar_tensor_tensor(
                    out=nt[:, 0:n], in0=xbuf[:, c, c0:c0 + n], scalar=asc[:, i, c, v:v + 1], in1=rsall[:, c0:c0 + n],
                    op0=ALU.mult, op1=ALU.mult), reads=[bx[ti], b_rs[ti], bm], writes=[bnt])
                P.op(ACT, lambda e, nt=nt, c=c, c0=c0, n=n, v=v: e.activation(
                    out=hbuf[:, c, c0:c0 + n], in_=nt[:, 0:n], func=AF.Identity,
                    bias=modsb[:, 24 * i + c, v:v + 1], scale=1.0), reads=[bnt, bm], writes=[bh[ti]])

    def ffn(self, l, s, xbuf, hbuf, abuf, tiles, bx, bh, ba):
        P = self.P
        gi = 0 if s == 0 else 2
        gsc, bm = self.gsc[l], self.b_gate[l][gi]
        for sl in range(11):
            W, bW = self.wload(self.d_w1[l, s, sl], [128, 8, 512])
            for ti, (c0, n, v) in enumerate(tiles):
                for jj in range(2):
                    j = 2 * sl + jj
                    pg, bg = self.psum()
                    pu, bu = self.psum()
                    P.op(PE, mm_group([(pg[:, 0:n], W[:, c, jj * 256:jj * 256 + 128], hbuf[:, c, c0:c0 + n], c == 0, c == 7)
                                       for c in range(8)]), reads=[bW, bh[ti]], writes=[bg])
                    P.op(PE, mm_group([(pu[:, 0:n], W[:, c, jj * 256 + 128:jj * 256 + 256], hbuf[:, c, c0:c0 + n], c == 0, c == 7)
                                       for c in range(8)]), reads=[bW, bh[ti]], writes=[bu])
                    k = self.sg_i % 2
                    self.sg_i += 1
                    sg, bsg = self.sg[k], self.b_sg[k]
                    P.op(ACT, lambda e, sg=sg, pg=pg, n=n: e.activation(out=sg[:, 0:n], in_=pg[:, 0:n], func=AF.Silu),
                         reads=[bg], writes=[bsg])
                    P.op(DVE, lambda e, sg=sg, pu=pu, n=n, j=j, c0=c0: e.tensor_tensor(
                        out=abuf[:, j, c0:c0 + n], in0=pu[:, 0:n], in1=sg[:, 0:n], op=ALU.mult),
                        reads=[bu, bsg], writes=[ba[ti]])
            self.mod_pump(2)
        self.act_prefetch(AF.Sqrt)
        self.mod_need(l, gi, gate=True)
        for g in range(4):
            Wa, bWa = self.wload(self.d_w2[l, s, g, 0], [128, 11, 256])
            Wb, bWb = self.wload(self.d_w2[l, s, g, 1], [128, 11, 256])
            for ti, (c0, n, v) in enumerate(tiles):
                for dd in range(2):
                    d = 2 * g + dd
                    py, by = self.psum()
                    mms = []
                    for j in range(NJ):
                        Wx = Wa if j < 11 else Wb
                        mms.append((py[:, 0:n], Wx[:, j % 11, dd * 128:(dd + 1) * 128], abuf[:, j, c0:c0 + n], j == 0, j == NJ - 1))
                    P.op(PE, mm_group(mms), reads=[bWa, bWb, ba[ti]], writes=[by])
                    P.op(DVE, lambda e, py=py, n=n, d=d, c0=c0, v=v: e.scalar_tensor_tensor(
                        out=xbuf[:, d, c0:c0 + n], in0=py[:, 0:n], scalar=gsc[:, gi, d, v:v + 1], in1=xbuf[:, d, c0:c0 + n],
                        op0=ALU.mult, op1=ALU.add), reads=[by, bm, bx[ti]], writes=[bx[ti]])
            self.mod_pump(1)

    def kv_tile(self, hsrc, c0, npk, bh, Wkv, bWkv, kT, V, bkT, bV, kcol, vt, rope_t, out_row, T, par_=None):
        P = self.P
        if par_ is None:
            par_ = T["i"] % 2
        tA, tB, kst, small, bT, bkst = T["tA"][par_], T["tB"][par_], T["kst"][par_], self.small, T["bT"][par_], T["bkst"][par_]
        bsm = T["bsm"][par_]
        sc = slice(50 + 2 * par_, 52 + 2 * par_)
        T["i"] += 1
        ps, pb = self.psum()
        P.op(PE, mm_group([(ps[0:npk, 0:256], hsrc[:, c, c0:c0 + npk], Wkv[:, c, :], c == 0, c == 7) for c in range(8)]),
             reads=[bWkv, bh], writes=[pb])
        yield
        P.op(ACT, lambda e: e.activation(out=tA[0:npk, 0:128], in_=ps[0:npk, 0:128], func=AF.Square),
             reads=[pb], writes=[bT])
        yield
        P.op(DVE, lambda e: e.tensor_reduce(out=small[0:npk, sc], in_=tA[0:npk, 0:128].rearrange("p (h d) -> p h d", d=64),
                                            axis=AX.X, op=ALU.add), reads=[bT], writes=[bsm])
        yield
        P.op(ACT, lambda e: e.activation(out=small[0:npk, sc], in_=small[0:npk, sc], func=AF.Sqrt, bias=EPS, scale=1.0 / 64),
             reads=[bsm], writes=[bsm])
        yield
        P.op(DVE, lambda e: e.reciprocal(out=small[0:npk, sc], in_=small[0:npk, sc]), reads=[bsm], writes=[bsm])
        yield
        P.op(DVE, lambda e: e.tensor_tensor(out=tB[0:npk, 0:128].rearrange("p (h d) -> p h d", d=64),
                                            in0=ps[0:npk, 0:128].rearrange("p (h d) -> p h d", d=64),
                                            in1=small[0:npk, sc].unsqueeze(2).broadcast_to([npk, 2, 64]), op=ALU.mult),
             reads=[pb, bsm], writes=[bT])
        yield
        P.op(DVE, lambda e: e.tensor_tensor(out=kst[0:npk, 0:128], in0=tB[0:npk, 0:128], in1=self.kg[0:npk, :], op=ALU.mult),
             reads=[bT, self.b_kg], writes=[bkst])
        yield
        P.op(ACT, lambda e: e.activation(out=V[0:npk, vt, :], in_=ps[0:npk, 128:256], func=AF.Copy), reads=[pb], writes=[bV])
        yield
        ksrc = kst
        if out_row is not None:
            P.op(ACT, lambda e: e.activation(out=kst[0:npk, 128:256], in_=ps[0:npk, 128:256], func=AF.Copy), reads=[pb], writes=[bkst])
            yield
            P.dma(SP, self.o_nk[out_row:out_row + npk, :], kst[0:npk, 0:128], reads=[bkst], key=bkst.name + "k")
            yield
            P.dma(SP, self.o_nv[out_row:out_row + npk, :], kst[0:npk, 128:256], reads=[bkst], key=bkst.name + "v")
            yield
        if rope_t is not None:
            yield from self.rope(kst[0:npk, 0:128], tA[0:npk, 0:128], tB[0:npk, 0:128], 2, npk, rope_t, [bkst], bT)
            ksrc = tA
        ps2, pb2 = self.psum()
        P.op(PE, lambda e: e.transpose(out=ps2[:, 0:npk], in_=ksrc[0:npk, 0:128], identity=self.ident[0:npk, 0:npk]),
             reads=[bkst, bT, self.b_ident], writes=[pb2])
        yield
        P.op(ACT, lambda e: e.activation(out=kT[:, kcol:kcol + npk], in_=ps2[:, 0:npk], func=AF.Copy), reads=[pb2], writes=[bkT])
        yield

    def interleave(self, gens, width=2):
        pending = deque(gens)
        active = []
        while pending or active:
            while pending and len(active) < width:
                active.append(pending.popleft())
            for g in list(active):
                try:
                    next(g)
                except StopIteration:
                    active.remove(g)

    def interleave_w(self, gens_w):
        active = [[g, w] for g, w in gens_w]
        while active:
            for gw in list(active):
                g, w = gw
                for _ in range(w):
                    try:
                        next(g)
                    except StopIteration:
                        active.remove(gw)
                        break

    def rope(self, x, t1, t2, H, npk, rt, bx_list, bT, bT2=None):
        P = self.P
        cosf, sinf = self.ropec[0:npk, rt, :], self.ropes[0:npk, rt, :]
        v5 = lambda a: a.rearrange("p (h r f s) -> p h r f s", h=H, r=2, f=2, s=16)
        c4 = cosf.rearrange("p (r f s) -> p r f s", r=2, f=2, s=16)
        s4 = sinf.rearrange("p (r f s) -> p r f s", r=2, f=2, s=16)
        P.op(DVE, lambda e: e.tensor_tensor(out=v5(t1), in0=v5(x), in1=c4.unsqueeze(1).broadcast_to([npk, H, 2, 2, 16]), op=ALU.mult),
             reads=bx_list + [self.b_rope], writes=[bT])
        yield
        for f in range(2):
            P.op(DVE, lambda e, f=f: e.tensor_tensor(out=v5(t2)[:, :, :, f, :], in0=v5(x)[:, :, :, 1 - f, :],
                                                     in1=s4[:, :, f, :].unsqueeze(1).broadcast_to([npk, H, 2, 16]), op=ALU.mult),
                 reads=bx_list + [self.b_rope], writes=[bT2 or bT])
            yield
        P.op(DVE, lambda e: e.tensor_tensor(out=t1, in0=t1, in1=t2, op=ALU.add), reads=[bT, bT2 or bT], writes=[bT])
        yield

    def mixer0(self, a_M, ba_M):
        P = self.P
        self.ropec = self.sb("ropec", [128, 9, 64])
        self.ropes = self.sb("ropes", [128, 9, 64])
        self.cmask = self.sb("cmask", [128, NS_COLS])
        self.b_rope, self.b_cmask = Buf("rope"), Buf("cmask")
        P.dma(SP, self.ropec[:], self.d_ropec[:, :, :], writes=[self.b_rope], key="ropec")
        P.dma(SP, self.ropes[:], self.d_ropes[:, :, :], writes=[self.b_rope], key="ropes")
        P.dma(SP, self.cmask[:], self.d_cmask[:, :], writes=[self.b_cmask])
        P.dma(SP, self.qg[:], self.d_qg[:, :], writes=[self.b_qg])
        P.dma(SP, self.kg[:], self.d_kg[:, :], writes=[self.b_kg])
        aR = self.av(0, NJ * NR).rearrange("p (j n) -> p j n", n=NR)
        xR = self.av(NJ * NR, 2 * 8 * NR, F32).rearrange("p (c n) -> p c n", n=NR)
        hR = self.hbuf[:, :, 0:NR]
        b_aR = [Buf("aR%d" % i) for i in range(2)]
        b_xR = [Buf("xR%d" % i) for i in range(2)]
        Prog.inherit(b_aR + b_xR, ba_M)
        bhR = [Buf("hR0"), Buf("hR1")]
        Prog.inherit(bhR, self.bh_M)
        stg = [self.sq[:, 0:4, :].rearrange("p a b -> p (a b)"), self.sq[:, 4:8, :].rearrange("p a b -> p (a b)")]
        self.load_x(self.d_xs, NS_COLS, xR, [(c0, npk, c0) for (c0, npk, par) in self.tok_R], b_xR,
                    [par for (c0, npk, par) in self.tok_R], stg)
        self.norm(xR, hR, self.sub_R, b_xR, bhR, 0, 0)
        self.ffn(0, 0, xR, hR, aR, self.ffn_tiles_R, b_xR, bhR, b_aR)
        self.norm(xR, hR, self.sub_R, b_xR, bhR, 0, 1)

        kT = self.av(0, 2560)
        V = self.av(2560, 21 * 128).rearrange("p (t f) -> p t f", f=128)
        b_kT = [Buf("kT_p%d" % i) for i in range(4)] + [Buf("kT_s")]
        b_V = [Buf("V_p%d" % i) for i in range(4)] + [Buf("V_s")]
        Prog.inherit(b_kT + b_V, b_aR)
        T = {"tA": [self.sg[0][:, 0:128], self.sg[0][:, 256:384]], "tB": [self.sg[0][:, 128:256], self.sg[0][:, 384:512]],
             "kst": [self.sg[1][:, 0:256], self.sg[1][:, 256:512]], "i": 0,
             "bT": [Buf("kvT0"), Buf("kvT1")], "bkst": [Buf("kst0"), Buf("kst1")], "bsm": [Buf("smk0"), Buf("smk1")]}
        Prog.inherit(T["bT"] + T["bkst"], self.b_sg)
        Wkv, bWkv = self.wload(self.d_wkv[:, :, :], [128, 8, 256])
        self.interleave([self.kv_tile(hR, c0, npk, bhR[par], Wkv, bWkv, kT, V, b_kT[4], b_V[4], 1824 + c0, 15 + i, 3 + i, None, T)
                         for i, (c0, npk, par) in enumerate(self.tok_R)])
        Prog.inherit(self.bh_M, bhR)
        self.norm(self.xres, self.hbuf, self.sub_M, self.bx_M, self.bh_M, 0, 1)
        qT = self.av(5248, 4 * NM).rearrange("p (s n) -> p s n", n=NM)
        attnT = self.av(10496, 4 * NM).rearrange("p (s n) -> p s n", n=NM)
        convT = self.av(15744, 4 * NM).rearrange("p (s n) -> p s n", n=NM)
        pT = [self.av(20992 + 512 * k, 512) for k in range(3)]
        T1s = [self.av(22528, 1024, F32), self.av(25600, 1024, F32)]
        T2s = [self.av(23552, 1024, F32), self.av(20992, 1024, F32)]
        T3s = [self.av(24576, 1024, F32), self.av(15744, 1024, F32)]
        rd = self.av(25600, 1024, F32)
        ckst = self.av(26624, 1024, F32).rearrange("p (t f) -> p t f", f=128)
        b_qT = [Buf("qT_%d" % i) for i in range(5)]
        b_attnT = [Buf("attnT%d" % i) for i in range(3)]
        b_convT = Buf("convT")
        b_pT = [Buf("pT%d" % k) for k in range(3)]
        b_T1s, b_T3s, b_rd, b_ckst = [Buf("T1a"), Buf("T1b")], [Buf("T3a"), Buf("T3b")], Buf("rd"), Buf("ckst")
        b_smq = [Buf("smq0"), Buf("smq1")]
        allnew = b_qT + b_attnT + [b_convT] + b_pT + b_T1s + b_T3s + [b_rd, b_ckst]
        Prog.inherit(allnew, b_aR + b_xR)
        P.dma(SP, ckst[:, :, :], self.d_ck.rearrange("(t p) f -> p t f", p=128), writes=[b_ckst])
        P.dma(POOL, V[:, 8:12, :], self.d_cv.rearrange("(t p) f -> p t f", p=128), writes=[b_V[4]], key="cvload")
        for t in range(4):
            ps2, pb2 = self.psum()
            P.op(PE, lambda e, t=t, ps2=ps2: e.transpose(out=ps2[:, 0:128], in_=ckst[:, t, :], identity=self.ident[:]),
                 reads=[b_ckst, self.b_ident], writes=[pb2])
            P.op(ACT, lambda e, t=t, ps2=ps2: e.activation(out=kT[:, 1024 + 128 * t:1152 + 128 * t], in_=ps2[:, 0:128], func=AF.Copy),
                 reads=[pb2], writes=[b_kT[4]])
        Wq, bWq = self.wload(self.d_wq[:, :, :], [128, 8, 512])
        def mtile(i, c0, npk, par):
            is_s = i >= 8
            bi = 4 if is_s else i // 2
            if is_s:
                kcol, vt, rt, orow = 1536 + (c0 - 1024), 12 + (i - 8), (i - 8), None
            else:
                kcol, vt, rt, orow = c0, i, None, c0
            yield from self.kv_tile(self.hbuf, c0, npk, self.bh_M[par], Wkv, bWkv, kT, V, b_kT[bi], b_V[bi], kcol, vt, rt, orow, T, par_=i % 2)
            qp = i % 2
            T1, T2, b_T1, bsmq = T1s[qp], T2s[qp], b_T1s[qp], b_smq[qp]
            qc = slice(34 + 8 * qp, 42 + 8 * qp)
            ps, pb = self.psum()
            P.op(PE, mm_group([(ps[0:npk, 0:512], self.hbuf[:, c, c0:c0 + npk], Wq[:, c, :], c == 0, c == 7) for c in range(8)]),
                 reads=[bWq, self.bh_M[par]], writes=[pb])
            yield
            small = self.small
            v3 = lambda a: a.rearrange("p (h d) -> p h d", d=64)
            P.op(ACT, lambda e, ps=ps, npk=npk, T1=T1: e.activation(out=T1[0:npk, :], in_=ps[0:npk, :], func=AF.Square),
                 reads=[pb], writes=[b_T1])
            yield
            P.op(DVE, lambda e, npk=npk, T1=T1, qc=qc: e.tensor_reduce(out=small[0:npk, qc], in_=v3(T1[0:npk, :]), axis=AX.X, op=ALU.add),
                 reads=[b_T1], writes=[bsmq])
            yield
            P.op(ACT, lambda e, npk=npk, qc=qc: e.activation(out=small[0:npk, qc], in_=small[0:npk, qc], func=AF.Sqrt, bias=EPS,
                                                      scale=1.0 / 64), reads=[bsmq], writes=[bsmq])
            yield
            P.op(DVE, lambda e, npk=npk, qc=qc: e.reciprocal(out=small[0:npk, qc], in_=small[0:npk, qc]),
                 reads=[bsmq], writes=[bsmq])
            yield
            P.op(DVE, lambda e, ps=ps, npk=npk, T2=T2, qc=qc: e.tensor_tensor(out=v3(T2[0:npk, :]), in0=v3(ps[0:npk, :]),
                                                                in1=small[0:npk, qc].unsqueeze(2).broadcast_to([npk, 8, 64]),
                                                                op=ALU.mult), reads=[pb, bsmq], writes=[b_T1])
            yield
            qsrc = T2
            if is_s:
                P.op(DVE, lambda e, npk=npk, T1=T1, T2=T2: e.tensor_tensor(out=T1[0:npk, :], in0=T2[0:npk, :], in1=self.qg[0:npk, :], op=ALU.mult),
                     reads=[b_T1, self.b_qg], writes=[b_T1])
                yield
            if is_s:
                yield from self.rope(T1[0:npk, :], T2[0:npk, :], T3s[qp][0:npk, :], 8, npk, rt, [b_T1], b_T1, b_T3s[qp])
                qsrc = T2
            pst, pbt = self.psum()
            def trq(e, pst=pst, qsrc=qsrc, npk=npk):
                ins = None
                for s4 in range(4):
                    ins = e.transpose(out=pst[:, s4 * 128:s4 * 128 + npk], in_=qsrc[0:npk, s4 * 128:(s4 + 1) * 128],
                                      identity=self.ident[0:npk, 0:npk])
                return ins
            P.op(PE, trq, reads=[b_T1, self.b_ident], writes=[pbt])
            yield
            if is_s:
                P.op(ACT, lambda e, pst=pst, npk=npk, c0=c0: e.activation(
                    out=qT[:, :, c0:c0 + npk], in_=pst[:, :].rearrange("p (s n) -> p s n", n=128)[:, :, 0:npk], func=AF.Copy),
                    reads=[pbt], writes=[b_qT[bi]])
            else:
                P.op(ACT, lambda e, pst=pst, npk=npk, c0=c0: e.activation(
                    out=qT[:, :, c0:c0 + npk], in_=pst[:, :].rearrange("p (s n) -> p s n", n=128)[:, :, 0:npk], func=AF.Copy,
                    scale=self.qgc[:, 0:1]), reads=[pbt, self.b_qgc], writes=[b_qT[bi]])
            yield


        self.interleave([mtile(i, c0, npk, par) for i, (c0, npk, par) in enumerate(self.tok_M)])

        assert not self.mod_q and self.mod_done.get((1, 2, "g")), "PSUM bank 7 still holds adaLN accumulators"
        Prog.inherit(b_pT + [b_rd], b_T1s)
        Prog.inherit([b_convT], b_T3s)
        s_chunks = [(1024 + 128 * t, 128, 8 + t) for t in range(4)] + [(1536, 128, 12), (1664, 128, 13), (1792, 32, 14)] + \
                   [(1824 + 128 * i, 128, 15 + i) for i in range(5)] + [(2464, 96, 20)]
        groups = []
        for bi in range(4):
            for hh in range(2):
                for sp in range(2):
                    groups.append((bi, hh, (2 * sp, 2 * sp + 2), bi * 256, 256,
                                   [(bi * 256 + 128 * kc, 128, 2 * bi + kc) for kc in range(2)], b_attnT[bi // 2]))
        for hh in range(2):
            for s4 in range(4):
                groups.append((4, hh, (s4, s4 + 1), 1024, NS_COLS, s_chunks, b_attnT[2]))
        rounds = [(g, ci) for g in range(len(groups)) for ci in range(len(groups[g][5]))]
        st = {}

        def emit_s(ri):
            g, ci = rounds[ri]
            bi, hh, (s0, s1), qc0, qn, chunks, bat = groups[g]
            kcol, npk, vt = chunks[ci]
            ps, pb = self.psum_pool("attn", (2, 3, 4, 5, 6, 7))
            ncol = (s1 - s0) * qn
            hs = slice(hh * 64, hh * 64 + 64)
            P.op(PE, lambda e: e.matmul(ps[0:npk, 0:ncol], lhsT=kT[hs, kcol:kcol + npk], rhs=qT[hs, s0:s1, qc0:qc0 + qn],
                                        start=True, stop=True), reads=[b_kT[bi], b_qT[bi]], writes=[pb])
            st[ri] = (ps, pb, ncol)

        def attn_gen():
            emit_s(0)
            yield
            acc = {}
            for ri in range(len(rounds)):
                if ri + 1 < len(rounds):
                    emit_s(ri + 1)
                    yield
                g, ci = rounds[ri]
                bi, hh, (s0, s1), qc0, qn, chunks, bat = groups[g]
                kcol, npk, vt = chunks[ci]
                ps, pb, ncol = st.pop(ri)
                k = ri % 3
                p, bp = pT[k], b_pT[k]
                P.op(ACT, lambda e, p=p, ps=ps, npk=npk, ncol=ncol: e.activation(out=p[0:npk, 0:ncol], in_=ps[0:npk, 0:ncol],
                                                                                 func=AF.Exp, scale=0.125), reads=[pb], writes=[bp])
                yield
                if ci == 0:
                    acc[g] = (self.psum_pool("attn", (2, 3, 4, 5, 6, 7), hold=True), self.psum_pool("attn", (2, 3, 4, 5, 6, 7), hold=True))
                (pn, bn), (pd, bd) = acc[g]
                last = ci == len(chunks) - 1
                P.op(PE, lambda e, pn=pn, p=p, npk=npk, ncol=ncol, vt=vt, ci=ci, last=last: e.matmul(
                    pn[:, 0:ncol], lhsT=V[0:npk, vt, :], rhs=p[0:npk, 0:ncol], start=(ci == 0), stop=last),
                    reads=[bp, b_V[bi]], writes=[bn])
                yield
                P.op(PE, lambda e, pd=pd, p=p, npk=npk, ncol=ncol, ci=ci, last=last: e.matmul(
                    pd[:, 0:ncol], lhsT=self.onesb[0:npk, :], rhs=p[0:npk, 0:ncol], start=(ci == 0), stop=last),
                    reads=[bp, self.b_onesb], writes=[bd])
                yield
                if last:
                    hs = slice(hh * 64, hh * 64 + 64)
                    P.op(ACT, lambda e, pd=pd, hs=hs, ncol=ncol: e.activation(out=rd[hs, 0:ncol], in_=pd[hs, 0:ncol], func=AF.Ln),
                         reads=[bd], writes=[b_rd])
                    yield
                    P.op(ACT, lambda e, hs=hs, ncol=ncol: e.activation(out=rd[hs, 0:ncol], in_=rd[hs, 0:ncol], func=AF.Exp, scale=-1.0),
                         reads=[b_rd], writes=[b_rd])
                    yield
                    P.op(DVE, lambda e, pn=pn, hs=hs, ncol=ncol, s0=s0, s1=s1, qc0=qc0, qn=qn: e.tensor_tensor(
                        out=attnT[hs, s0:s1, qc0:qc0 + qn], in0=pn[hs, 0:ncol].rearrange("p (s n) -> p s n", n=qn),
                        in1=rd[hs, 0:ncol].rearrange("p (s n) -> p s n", n=qn), op=ALU.mult),
                        reads=[bn, b_rd], writes=[bat])
                    yield
                    self.psum_release(bn, fresh=True)
                    self.psum_release(bd)
                    del acc[g]


        def conv_gen():
            upad = self.sq[:, :, :].rearrange("p a b -> p (a b)")
            cacc = self.sq2[:, :, :].rearrange("p a b -> p (a b)")
            bgs = self.rsall
            xcs = [self.nt[0], self.nt[1]]
            b_upad, b_cacc, b_bgs, b_xcs = self.b_sq, self.b_sq2, Buf("bgs"), self.b_nt
            Prog.inherit([b_bgs], getattr(self, "b_rs_prev", []))
            self.b_rs_prev = [b_bgs]
            P.op(DVE, lambda e: e.memset(upad[:, :], 0.0), writes=[b_upad])
            yield
            cw = self.convw
            for cc in range(4):
                Wc, bWc = self.wload(self.d_wcv[cc], [128, 8, 384])
                for ti, (c0, n, v) in enumerate(self.ffn_tiles_M):
                    def proj(q3, pq, bq):
                        return P.op(PE, mm_group([(pq[:, 0:n], Wc[:, c, q3 * 128:(q3 + 1) * 128], self.hbuf[:, c, c0:c0 + n],
                                                   c == 0, c == 7) for c in range(8)]), reads=[bWc, self.bh_M[ti]], writes=[bq])
                    pxc, bpxc = self.psum_pool("conv", (0, 1))
                    proj(2, pxc, bpxc)
                    yield
                    pcg, bpcg = self.psum_pool("conv", (0, 1))
                    proj(1, pcg, bpcg)
                    yield
                    k = (cc * 3 + ti) % 2
                    xc_, bxc = xcs[k], b_xcs[k]
                    P.op(ACT, lambda e, xc_=xc_, n=n, px=pxc: e.activation(out=xc_[:, 0:n], in_=px[:, 0:n], func=AF.Copy),
                         reads=[bpxc], writes=[bxc])
                    yield
                    if ti < 2:
                        uo = upad[:, 1 + 2 * ti * 257:1 + (2 * ti + 2) * 257].rearrange("p (b k) -> p b k", k=257)[:, :, 0:256]
                        P.op(DVE, lambda e, uo=uo, pc=pcg, xc_=xc_: e.tensor_tensor(
                            out=uo, in0=pc[:, 0:512].rearrange("p (b k) -> p b k", k=256),
                            in1=xc_[:, 0:512].rearrange("p (b k) -> p b k", k=256), op=ALU.mult),
                            reads=[bpcg, bxc], writes=[b_upad])
                        yield
                    else:
                        P.op(DVE, lambda e, pc=pcg, xc_=xc_: e.tensor_tensor(out=upad[:, 1029:1317], in0=pc[:, 0:NS_COLS],
                                                                             in1=xc_[:, 0:NS_COLS], op=ALU.mult),
                             reads=[bpcg, bxc], writes=[b_upad])
                        yield
                    pbg, bpbg = self.psum_pool("conv", (0, 1))
                    proj(0, pbg, bpbg)
                    yield
                    P.op(ACT, lambda e, n=n, c0=c0, pb_=pbg: e.activation(out=bgs[:, c0:c0 + n], in_=pb_[:, 0:n], func=AF.Copy),
                         reads=[bpbg], writes=[b_bgs])
                    yield
                P.op(DVE, lambda e: e.tensor_tensor(out=upad[:, 1029:1317], in0=upad[:, 1029:1317], in1=self.cmask[:, :], op=ALU.mult),
                     reads=[b_upad, self.b_cmask], writes=[b_upad])
                yield
                P.op(DVE, lambda e, cc=cc: e.tensor_scalar(out=cacc[:, 1:1317], in0=upad[:, 1:1317], scalar1=cw[:, cc, 1:2], scalar2=None,
                                                           op0=ALU.mult), reads=[b_upad, self.b_convw], writes=[b_cacc])
                yield
                P.op(DVE, lambda e, cc=cc: e.scalar_tensor_tensor(out=cacc[:, 1:1317], in0=upad[:, 0:1316], scalar=cw[:, cc, 0:1],
                                                                  in1=cacc[:, 1:1317], op0=ALU.mult, op1=ALU.add),
                     reads=[b_upad, self.b_convw, b_cacc], writes=[b_cacc])
                yield
                P.op(DVE, lambda e, cc=cc: e.scalar_tensor_tensor(out=cacc[:, 1:1317], in0=upad[:, 2:1318], scalar=cw[:, cc, 2:3],
                                                                  in1=cacc[:, 1:1317], op0=ALU.mult, op1=ALU.add),
                     reads=[b_upad, self.b_convw, b_cacc], writes=[b_cacc])
                yield
                P.op(DVE, lambda e, cc=cc: e.tensor_tensor(
                    out=convT[:, cc, 0:1024].rearrange("p (b k) -> p b k", k=256),
                    in0=cacc[:, 1:1029].rearrange("p (b k) -> p b k", k=257)[:, :, 0:256],
                    in1=bgs[:, 0:1024].rearrange("p (b k) -> p b k", k=256), op=ALU.mult),
                    reads=[b_cacc, b_bgs], writes=[b_convT])
                yield
                P.op(DVE, lambda e, cc=cc: e.tensor_tensor(out=convT[:, cc, 1024:NM], in0=cacc[:, 1029:1317], in1=bgs[:, 1024:NM],
                                                           op=ALU.mult), reads=[b_cacc, b_bgs], writes=[b_convT])
                yield


        self.interleave_w([(attn_gen(), 6), (conv_gen(), 1)])

        Woa, bWoa = self.wload(self.d_wo[0], [128, 4, 1024])
        Woc, bWoc = self.wload(self.d_wo[1], [128, 4, 1024])
        self.act_prefetch(AF.Sqrt)
        self.mod_need(0, 1, gate=True)
        gsc, bm = self.gsc[0], self.b_gate[0][1]
        for ti, (c0, n, v) in enumerate(self.ffn_tiles_M):
            for d in range(8):
                po, bpo = self.psum()
                mms = [(po[:, 0:n], Woa[:, s4, d * 128:(d + 1) * 128], attnT[:, s4, c0:c0 + n], s4 == 0, False) for s4 in range(4)]
                mms += [(po[:, 0:n], Woc[:, s4, d * 128:(d + 1) * 128], convT[:, s4, c0:c0 + n], False, s4 == 3) for s4 in range(4)]
                P.op(PE, mm_group(mms), reads=[bWoa, bWoc, b_attnT[ti], b_convT], writes=[bpo])
                P.op(DVE, lambda e, po=po, n=n, d=d, c0=c0, v=v: e.scalar_tensor_tensor(
                    out=self.xres[:, d, c0:c0 + n], in0=po[:, 0:n], scalar=gsc[:, 1, d, v:v + 1], in1=self.xres[:, d, c0:c0 + n],
                    op0=ALU.mult, op1=ALU.add), reads=[bpo, bm, self.bx_M[ti]], writes=[self.bx_M[ti]])
        self.cur_a = allnew + b_kT + b_V + b_aR + b_xR
        Prog.inherit(self.b_sg, T["bT"] + T["bkst"])

    def kv_tile_R(self, hR, c0, npk, bh, Wkv, bWkv, kT, V, bkT, bV, kcol, vt, ridx, T):
        self.kv_tile(hR, c0, npk, bh, Wkv, bWkv, kT, V, bkT, bV, kcol, vt, ("R", c0 // 128), None, T)

    def mixer1(self, ba_M):
        P = self.P
        z = self.av(0, 11 * 1024).rearrange("p (t f) -> p t f", f=1024)
        mpp = self.av(11264, 2048).rearrange("p (a g t) -> p a g t", g=4, t=256)
        mps = self.av(13312, 3456).rearrange("p (a g t) -> p a g t", g=4, t=NS_COLS)
        b_z = [Buf("z%d" % i) for i in range(5)]
        b_mpp, b_mps = Buf("mpp"), Buf("mps")
        Prog.inherit(b_z + [b_mpp, b_mps], self.cur_a)
        P.dma(POOL, mpp, self.d_mpp[:, :, :, :], writes=[b_mpp])
        P.dma(POOL, mps, self.d_mps[:, :, :, :], writes=[b_mps])
        Wp, bWp = self.wload(self.d_poolw[:, :, :, :], [128, 4, 2, 256])
        for tt, (c0, npk, par) in enumerate(self.tok_M):
            bz = b_z[4 if tt >= 8 else tt // 2]
            for half in range(2):
                ps, pb = self.psum()
                mms = []
                for g2 in range(2):
                    gi = 2 * half + g2
                    for kc in range(2):
                        mms.append((ps[0:npk, g2 * 256:(g2 + 1) * 256], self.hbuf[:, 2 * gi + kc, c0:c0 + npk], Wp[:, gi, kc, :],
                                    kc == 0, kc == 1))
                P.op(PE, mm_group(mms), reads=[bWp, self.bh_M[par]], writes=[pb])
                dst = z[0:npk, tt, half * 512:(half + 1) * 512]
                if half == 0:
                    P.op(ACT, lambda e, dst=dst, ps=ps, npk=npk: e.activation(out=dst, in_=ps[0:npk, :], func=AF.Copy),
                         reads=[pb], writes=[bz])
                else:
                    P.op(DVE, lambda e, dst=dst, ps=ps, npk=npk: e.tensor_copy(out=dst, in_=ps[0:npk, :]), reads=[pb], writes=[bz])
        self.mod_need(1, 1, gate=True)
        gsc, bm = self.gsc[1], self.b_gate[1][1]
        segs = [(bi, [(2 * bi, 128), (2 * bi + 1, 128)], 256, bi * 256, mpp, b_mpp, 0, bi // 2) for bi in range(4)]
        segs.append((4, [(8, 128), (9, 128), (10, 32)], NS_COLS, 1024, mps, b_mps, 1, 2))
        for (bi, stiles, Tn, c0, Mx, bMx, v, par) in segs:
            for fc in range(8):
                gi = fc // 2
                po, bpo = self.psum()
                mms = [(po[:, 0:Tn], z[0:npk, tt, fc * 128:(fc + 1) * 128], Mx[0:npk, sc, gi, 0:Tn], sc == 0, sc == len(stiles) - 1)
                       for sc, (tt, npk) in enumerate(stiles)]
                P.op(PE, mm_group(mms), reads=[b_z[bi], bMx], writes=[bpo])
                P.op(DVE, lambda e, po=po, Tn=Tn, fc=fc, c0=c0, v=v: e.scalar_tensor_tensor(
                    out=self.xres[:, fc, c0:c0 + Tn], in0=po[:, 0:Tn], scalar=gsc[:, 1, fc, v:v + 1], in1=self.xres[:, fc, c0:c0 + Tn],
                    op0=ALU.mult, op1=ALU.add), reads=[bpo, bm, self.bx_M[par]], writes=[self.bx_M[par]])
        self.cur_a = b_z + [b_mpp, b_mps]

    def final_out(self):
        P = self.P
        P.dma(SP, self.finalg[:], self.d_finalg[:, :], writes=[self.b_finalg])
        ot = [self.av(4096 * k, 2048, F32) for k in range(2)]
        jk = [self.av(8192 + 4096 * k, 2048, F32) for k in range(2)]
        b_ot = [Buf("ot0"), Buf("ot1")]
        b_jk = [Buf("jk0"), Buf("jk1")]
        b_sm = [Buf("smf0"), Buf("smf1")]
        Prog.inherit(b_ot + b_jk, self.cur_a)
        Prog.inherit(b_sm, [self.b_small])
        outs = [(128 * i, self.o_yp, 128 * i, i // 4) for i in range(8)] + \
               [(1024 + HALO + 128 * i, self.o_ys, 128 * i, 2) for i in range(2)]
        sm = self.small

        def otile(oi, c0, dram, r0, par):
            k = oi % 2
            o, bo, j, bj, bs = ot[k], b_ot[k], jk[k], b_jk[k], b_sm[k]
            for half in range(2):
                ps, pb = self.psum()

                def tr(e, ps=ps, half=half):
                    ins = None
                    for jj in range(4):
                        c = 4 * half + jj
                        ins = e.transpose(out=ps[:, jj * 128:(jj + 1) * 128], in_=self.xres[:, c, c0:c0 + 128],
                                          identity=self.ident[:])
                    return ins
                P.op(PE, tr, reads=[self.bx_M[par], self.b_ident], writes=[pb])
                yield
                P.op(ACT, lambda e, ps=ps, half=half: e.activation(out=o[:, half * 512:(half + 1) * 512], in_=ps[:, :],
                                                                   func=AF.Copy), reads=[pb], writes=[bo])
                yield
            P.op(ACT, lambda e: e.activation(out=j[:, :], in_=o[:, :], func=AF.Square, accum_out=sm[:, oi:oi + 1]),
                 reads=[bo], writes=[bj, bs])
            yield
            P.op(ACT, lambda e: e.activation(out=sm[:, oi:oi + 1], in_=sm[:, oi:oi + 1], func=AF.Sqrt, bias=EPS,
                                             scale=1.0 / D), reads=[bs], writes=[bs])
            yield
            P.op(DVE, lambda e: e.reciprocal(out=sm[:, oi:oi + 1], in_=sm[:, oi:oi + 1]), reads=[bs], writes=[bs])
            yield
            P.op(DVE, lambda e: e.scalar_tensor_tensor(out=o[:, :], in0=o[:, :], scalar=sm[:, oi:oi + 1],
                                                       in1=self.finalg[:, :], op0=ALU.mult, op1=ALU.mult),
                 reads=[bo, bs, self.b_finalg], writes=[bo])
            yield
            P.dma(SP, dram[r0:r0 + 128, :], o[:, :], reads=[bo], key="ot%d" % k)
            yield

        self.interleave([otile(oi, *t) for oi, t in enumerate(outs)])


def _host_inputs(inp):
    f = lambda a: np.ascontiguousarray(np.asarray(a, dtype=np.float32))
    x_prompt, x_sample, c, c_ctx = f(inp["x_prompt"]), f(inp["x_sample"]), f(inp["c"]), f(inp["c_ctx"])
    ada_w, ada_b, norm_g = f(inp["ada_w"]), f(inp["ada_b"]), f(inp["norm_g"])
    w1, w2 = f(inp["ffn_w1"]), f(inp["ffn_w2"])
    shared = {}
    shared["adaw"] = f(ada_w.reshape(2, 8, 128, 18, 512).transpose(0, 3, 2, 1, 4))
    shared["adab"] = f(ada_b.reshape(2, 72, 128).transpose(2, 0, 1))
    shared["normg"] = f(norm_g.reshape(2, 3, 8, 128).transpose(3, 0, 1, 2))
    shared["finalg"] = f(np.broadcast_to(f(inp["final_g"])[None, :], (128, 1024)))
    g = w1[..., :DFF].reshape(2, 2, 8, 128, 11, 2, 128)
    u = w1[..., DFF:].reshape(2, 2, 8, 128, 11, 2, 128)
    gu = np.stack([g, u], axis=6)
    shared["w1t"] = f(gu.transpose(0, 1, 4, 3, 2, 5, 6, 7).reshape(2, 2, 11, 128, 8, 512))
    w2r = w2.reshape(2, 2, 2, 11, 128, 4, 256)
    shared["w2t"] = f(w2r.transpose(0, 1, 5, 2, 4, 3, 6))
    return shared, x_prompt, x_sample, c, c_ctx


_NC_CACHE = {}


def _get_nc(stop_after=99):
    if stop_after not in _NC_CACHE:
        _NC_CACHE[stop_after] = Builder(stop_after).build()
    return _NC_CACHE[stop_after]


HEAD_PERM = [0, 4, 1, 5, 2, 6, 3, 7]


def _pool_matrix(gpos, S_total):
    n = len(gpos)
    out = np.zeros((4, n, n), np.float32)
    for gi, w in enumerate(POOL_WINDOWS):
        left = w // 2
        right = w - 1 - left
        for t in range(n):
            gt = gpos[t]
            if gt < 0 or gt >= S_total:
                continue
            lo = max(gt - left, 0)
            hi = min(gt + right + 1, S_total)
            inv = np.float32(1.0) / np.float32(hi - lo)
            for s in range(n):
                if lo <= gpos[s] < hi:
                    out[gi, s, t] += inv
            out[gi, t, t] -= 1.0
    return out


def _rope_tables(gtok):
    half = 32
    inv = 10000.0 ** (-np.arange(0, half, 2, dtype=np.float64) / half)
    row = (gtok // GRID_W).astype(np.float64)
    col = (gtok % GRID_W).astype(np.float64)
    ar = row[:, None] * inv[None, :]
    ac = col[:, None] * inv[None, :]
    cr, sr, cc, sc = np.cos(ar), np.sin(ar), np.cos(ac), np.sin(ac)
    cosf = np.concatenate([cr, cr, cc, cc], axis=1).astype(np.float32)
    sinf = np.concatenate([-sr, sr, -sc, sc], axis=1).astype(np.float32)
    return cosf, sinf


def make_in_maps(inp, cores=range(8)):
    f = lambda a: np.ascontiguousarray(np.asarray(a, dtype=np.float32))
    shared, x_prompt, x_sample, c, c_ctx = _host_inputs(inp)
    w_in, w_out = f(inp["mix_w_in"])[0], f(inp["mix_w_out"])[0]
    wq = w_in[:, 0:512].reshape(1024, 8, 64)[:, HEAD_PERM, :].reshape(1024, 512)
    shared["wq"] = f(wq.reshape(8, 128, 512).transpose(1, 0, 2))
    shared["wkv"] = f(w_in[:, 512:768].reshape(8, 128, 256).transpose(1, 0, 2))
    bg, cg, xc = w_in[:, 768:1280], w_in[:, 1280:1792], w_in[:, 1792:2304]
    wcv = np.stack([np.concatenate([bg[:, k * 128:(k + 1) * 128], cg[:, k * 128:(k + 1) * 128],
                                    xc[:, k * 128:(k + 1) * 128]], axis=1) for k in range(4)], axis=0)
    shared["wcv"] = f(wcv.reshape(4, 8, 128, 384).transpose(0, 2, 1, 3))
    wo_a = w_out[0:512].reshape(8, 64, 1024)
    wo_a = np.stack([np.concatenate([wo_a[s], wo_a[4 + s]], axis=0) for s in range(4)], axis=1)
    wo_c = w_out[512:1024].reshape(4, 128, 1024).transpose(1, 0, 2)
    shared["wo"] = f(np.stack([wo_a, wo_c], axis=0))
    shared["qg"] = f(np.broadcast_to(np.tile(f(inp["q_norm"])[0], 8)[None, :], (128, 512)))
    shared["qgc"] = f(np.tile(f(inp["q_norm"])[0], 2).reshape(128, 1))
    shared["kg"] = f(np.broadcast_to(np.tile(f(inp["k_norm"])[0], 2)[None, :], (128, 128)))
    shared["convw"] = f(f(inp["conv_w"])[0].T.reshape(4, 128, 3).transpose(1, 0, 2))
    shared["poolw"] = f(f(inp["pool_w"])[0].reshape(4, 2, 128, 256).transpose(2, 0, 1, 3))
    shared["pscale"] = f(f(inp["pool_scale"])[0].reshape(8, 128).T)
    mp = _pool_matrix(np.arange(256), 256)
    shared["mpp"] = f(mp.reshape(4, 2, 128, 256).transpose(2, 1, 0, 3))
    cache_k, cache_v = f(inp["cache_k"]), f(inp["cache_v"])
    maps = []
    for k in cores:
        b, r = k // 4, k % 4
        gwin = 256 * r - HALO + np.arange(1024)
        idx = gwin % 1024
        m = dict(shared)
        m["xp"] = f(x_prompt[4 * k:4 * k + 4].reshape(1024, 1024))
        m["xs"] = f(x_sample[b][idx])
        m["cvec"] = f(np.stack([c_ctx, c[b]], axis=-1).reshape(8, 128, 2).transpose(1, 0, 2))
        ms = _pool_matrix(gwin[:NS_COLS], 1024)
        msp = np.zeros((4, 384, NS_COLS), np.float32)
        msp[:, :NS_COLS] = ms
        m["mps"] = f(msp.reshape(4, 3, 128, NS_COLS).transpose(2, 1, 0, 3))
        cosf, sinf = _rope_tables(idx)
        def tab(a):
            a = np.concatenate([a, np.zeros((512, 64), np.float32)], axis=0)
            tl = [a[t * 128:(t + 1) * 128] for t in range(3)] + [a[NS_COLS + t * 128:NS_COLS + (t + 1) * 128] for t in range(6)]
            return f(np.stack(tl, axis=1))
        m["ropec"] = tab(cosf)
        m["ropes"] = tab(sinf)
        gw = gwin[:NS_COLS]
        m["cmask"] = f(np.broadcast_to(((gw >= 0) & (gw < 1024)).astype(np.float32)[None, :], (128, NS_COLS)))
        m["ck"] = f(cache_k[b, 0].reshape(512, 128))
        m["cv"] = f(cache_v[b, 0].reshape(512, 128))
        maps.append(m)
    return maps


def assemble(results, cores=range(8)):
    y_prompt = np.zeros((32, 256, 1024), np.float32)
    y_sample = np.zeros((2, 1024, 1024), np.float32)
    nk = np.zeros((32, 1, 256, 2, 64), np.float32)
    nv = np.zeros((32, 1, 256, 2, 64), np.float32)
    for res, k in zip(results, cores):
        b, r = k // 4, k % 4
        y_prompt[4 * k:4 * k + 4] = res["yp"].reshape(4, 256, 1024)
        y_sample[b, 256 * r:256 * r + 256] = res["ys"]
        nk[4 * k:4 * k + 4, 0] = res["nk"].reshape(4, 256, 2, 64)
        nv[4 * k:4 * k + 4, 0] = res["nv"].reshape(4, 256, 2, 64)
    return y_prompt, y_sample, nk, nv


def kernel(**inputs):
    nc = _get_nc()
    maps = make_in_maps(inputs)
    res = run_bass_kernel_spmd(nc, maps, core_ids=list(range(8)))
    return assemble(res.results)
```

```python
import numpy as np
from collections import deque
from contextlib import ExitStack
import concourse.bass as bass
import concourse.mybir as mybir
from concourse.bass_utils import run_bass_kernel_spmd

F32 = mybir.dt.float32
BF16 = mybir.dt.bfloat16
AF = mybir.ActivationFunctionType
ALU = mybir.AluOpType
AX = mybir.AxisListType
PE, ACT, DVE, POOL, SP = "tensor", "scalar", "vector", "gpsimd", "sync"
ENGS = (PE, ACT, DVE, POOL, SP)

D = 1024
DFF = 2816
NJ = 22
EPS = 1e-6
NP_COLS = 1024
NS_COLS = 288
HALO = 16
NM = NP_COLS + NS_COLS
NR = 1024 - NS_COLS
GRID_W = 64
POOL_WINDOWS = (2, 4, 8, 16)


class Buf:
    __slots__ = ("name", "w", "r")

    def __init__(self, name):
        self.name = name
        self.w = None
        self.r = []


class Op:
    __slots__ = ("eng", "fn", "deps", "marked", "sig", "is_dma", "key", "dval")

    def __init__(self, eng, fn, is_dma):
        self.eng = eng
        self.fn = fn
        self.deps = ()
        self.marked = False
        self.sig = 0
        self.is_dma = is_dma
        self.key = None
        self.dval = 0


class Prog:
    def __init__(self, nc):
        self.nc = nc
        self.ops = {e: [] for e in ENGS}
        self.dma_keys = {}

    def op(self, eng, fn, reads=(), writes=(), dma=False, key=None):
        o = Op(eng, fn, dma)
        deps = set()
        for b in reads:
            if b.w is not None:
                deps.add(b.w)
        for b in writes:
            if b.w is not None:
                deps.add(b.w)
            deps.update(b.r)
        if eng == PE and not dma:
            deps = {d for d in deps if not (d.eng == PE and not d.is_dma)}
        for d in deps:
            d.marked = True
        o.deps = deps
        for b in reads:
            b.r.append(o)
        for b in writes:
            b.w = o
            b.r = []
        if dma:
            if key is None:
                key = (writes[0] if writes else reads[0]).name
            o.key = key
            self.dma_keys[key] = self.dma_keys.get(key, 0) + 16
            o.dval = self.dma_keys[key]
        self.ops[eng].append(o)
        return o

    def dma(self, queue, out, in_, reads=(), writes=(), key=None):
        return self.op(queue, lambda e: e.dma_start(out=out, in_=in_), reads, writes, dma=True, key=key)

    @staticmethod
    def inherit(new_bufs, old_bufs):
        hz = []
        for b in old_bufs:
            if b.w is not None:
                hz.append(b.w)
            hz.extend(b.r)
        for nb in new_bufs:
            nb.r = list(nb.r) + hz

    def emit(self):
        nc = self.nc
        with ExitStack() as es:
            esem = {e: es.enter_context(nc.semaphore("s_" + e)) for e in ENGS}
            dsem = {k: es.enter_context(nc.semaphore("d%d" % i)) for i, k in enumerate(self.dma_keys)}
            for e in ENGS:
                c = 0
                for o in self.ops[e]:
                    if not o.is_dma and o.marked:
                        c += 1
                        o.sig = c
            block = es.enter_context(nc.Block())

            def run(e, eng):
                waited = {}
                for o in self.ops[e]:
                    need = {}
                    for d in o.deps:
                        if d.is_dma:
                            s, v = dsem[d.key], d.dval
                        else:
                            s, v = esem[d.eng], d.sig
                        if need.get(s, 0) < v:
                            need[s] = v
                    for s, v in need.items():
                        if waited.get(s, 0) < v:
                            eng.wait_ge(s, v)
                            waited[s] = v
                    ins = o.fn(eng)
                    if o.is_dma:
                        ins.then_inc(dsem[o.key], 16)
                    elif o.marked:
                        ins.then_inc(esem[e], 1)
                if e == SP:
                    for k, v in self.dma_keys.items():
                        eng.wait_ge(dsem[k], v)

            @block.tensor
            def _(eng):
                run(PE, eng)

            @block.scalar
            def _(eng):
                run(ACT, eng)

            @block.vector
            def _(eng):
                run(DVE, eng)

            @block.gpsimd
            def _(eng):
                run(POOL, eng)

            @block.sync
            def _(eng):
                run(SP, eng)


def mm_group(mms):
    def fn(e):
        ins = None
        for (o, l, r, st, sp) in mms:
            ins = e.matmul(o, lhsT=l, rhs=r, start=st, stop=sp)
        return ins
    return fn


class Builder:
    def __init__(self, stop_after=99):
        self.stop_after = stop_after
        self.nc = bass.Bass("TRN2", target_bir_lowering=False)
        self.P = Prog(self.nc)
        self.es = ExitStack()
        self.uid = 0

    def din(self, name, shape):
        return self.nc.dram_tensor(name, list(shape), F32, kind="ExternalInput").ap()

    def dout(self, name, shape):
        return self.nc.dram_tensor(name, list(shape), F32, kind="ExternalOutput").ap()

    def sb(self, name, shape, dt=F32):
        return self.es.enter_context(self.nc.sbuf_tensor("sb_" + name, list(shape), dt))

    def buf(self, name):
        self.uid += 1
        return Buf("%s#%d" % (name, self.uid))

    def av(self, off, n, dt=BF16):
        v = self.areg[:, off:off + n]
        if dt == F32:
            v = v.bitcast(F32)
        return v

    def psum(self, hold=False):
        while True:
            k = self.ps_i % 7
            self.ps_i += 1
            if k not in self.ps_hold:
                break
        if hold:
            self.ps_hold.add(k)
        return self.ps[k], self.psb[k]

    def psum_pool(self, name, banks, hold=False):
        tries = 0
        while True:
            i = self.ps_pool_i.get(name, 0)
            self.ps_pool_i[name] = i + 1
            k = banks[i % len(banks)]
            tries += 1
            if k in self.ps_hold:
                continue
            if k in self.ps_recent and tries <= len(banks):
                continue
            break
        if hold:
            self.ps_hold.add(k)
        return self.ps[k], self.psb[k]

    def psum_release(self, pb, fresh=False):
        k = self.psb.index(pb)
        self.ps_hold.discard(k)
        if fresh:
            self.ps_recent = set()
        self.ps_recent.add(k)

    def wload(self, src, shape):
        k = self.ring_i % len(self.ring)
        self.ring_i += 1
        npart = shape[0]
        n = int(np.prod(shape[1:]))
        v = self.ring[k][0:npart, 0:n]
        if len(shape) == 3:
            v = v.rearrange("p (a b) -> p a b", b=shape[2])
        elif len(shape) == 4:
            v = v.rearrange("p (a b c) -> p a b c", b=shape[2], c=shape[3])
        b = self.ringb[k]
        self.P.dma(POOL, v, src, writes=[b], key="ring%d" % k)
        return v, b

    def build(self):
        nc, P = self.nc, self.P
        self.d_xp = self.din("xp", [1024, 1024])
        self.d_xs = self.din("xs", [1024, 1024])
        self.d_cvec = self.din("cvec", [128, 8, 2])
        self.d_adaw = self.din("adaw", [2, 18, 128, 8, 512])
        self.d_adab = self.din("adab", [128, 2, 72])
        self.d_normg = self.din("normg", [128, 2, 3, 8])
        self.d_finalg = self.din("finalg", [128, 1024])
        self.d_w1 = self.din("w1t", [2, 2, 11, 128, 8, 512])
        self.d_w2 = self.din("w2t", [2, 2, 4, 2, 128, 11, 256])
        self.d_wq = self.din("wq", [128, 8, 512])
        self.d_wkv = self.din("wkv", [128, 8, 256])
        self.d_wcv = self.din("wcv", [4, 128, 8, 384])
        self.d_wo = self.din("wo", [2, 128, 4, 1024])
        self.d_qg = self.din("qg", [128, 512])
        self.d_kg = self.din("kg", [128, 128])
        self.d_qgc = self.din("qgc", [128, 1])
        self.d_convw = self.din("convw", [128, 4, 3])
        self.d_poolw = self.din("poolw", [128, 4, 2, 256])
        self.d_pscale = self.din("pscale", [128, 8])
        self.d_mpp = self.din("mpp", [128, 2, 4, 256])
        self.d_mps = self.din("mps", [128, 3, 4, 288])
        self.d_ropec = self.din("ropec", [128, 9, 64])
        self.d_ropes = self.din("ropes", [128, 9, 64])
        self.d_cmask = self.din("cmask", [128, 288])
        self.d_ck = self.din("ck", [512, 128])
        self.d_cv = self.din("cv", [512, 128])
        self.o_yp = self.dout("yp", [1024, 1024])
        self.o_ys = self.dout("ys", [256, 1024])
        self.o_nk = self.dout("nk", [1024, 128])
        self.o_nv = self.dout("nv", [1024, 128])

        self.xres = self.sb("xres", [128, 8, NM])
        self.hbuf = self.sb("hbuf", [128, 8, NM], BF16)
        self.areg = self.sb("areg", [128, NJ * NM], BF16)
        self.ring = [self.sb("ring%d" % i, [128, 4096], BF16) for i in range(5)]
        self.ringb = [Buf("ring%d" % i) for i in range(5)]
        self.ring_i = 0
        self.sq = self.sb("sq", [128, 8, 256])
        self.b_sq = Buf("sq")
        self.sq2 = self.sb("sq2", [128, 8, 256])
        self.b_sq2 = Buf("sq2")
        self.rsall = self.sb("rsall", [128, NM])
        self.nrm_i = 0
        self.stg_i = 0
        self.sg = [self.sb("sg%d" % i, [128, 512]) for i in range(2)]
        self.b_sg = [Buf("sg%d" % i) for i in range(2)]
        self.sg_i = 0
        self.nt = [self.sb("nt%d" % i, [128, 512]) for i in range(2)]
        self.b_nt = [Buf("nt%d" % i) for i in range(2)]
        self.nt_i = 0
        self.ident = self.sb("ident", [128, 128])
        self.ones = self.sb("ones", [128, 128])
        self.onesb = self.sb("onesb", [128, 128], BF16)
        self.b_ident, self.b_ones, self.b_onesb = Buf("ident"), Buf("ones"), Buf("onesb")
        self.cvec = self.sb("cvec", [128, 8, 2])
        self.scb = self.sb("scb", [128, 8, 2], BF16)
        self.adab = self.sb("adab", [128, 2, 72])
        self.normg = self.sb("normg", [128, 2, 3, 8])
        self.finalg = self.sb("finalg", [128, 1024])
        self.qg = self.sb("qg", [128, 512])
        self.kg = self.sb("kg", [128, 128])
        self.convw = self.sb("convw", [128, 4, 3])
        self.pscale = self.sb("pscale", [128, 8])
        self.b_cvec, self.b_scb, self.b_adab, self.b_normg = Buf("cvec"), Buf("scb"), Buf("adab"), Buf("normg")
        self.b_finalg, self.b_qg, self.b_kg, self.b_convw, self.b_pscale = (
            Buf("finalg"), Buf("qg"), Buf("kg"), Buf("convw"), Buf("pscale"))
        self.modsb = [self.sb("modsb%d" % l, [128, 72, 2]) for l in range(2)]
        self.asc = [self.sb("asc%d" % l, [128, 3, 8, 2]) for l in range(2)]
        self.gsc = [self.sb("gsc%d" % l, [128, 3, 8, 2]) for l in range(2)]
        self.b_mod = [[Buf("mod%d_%d" % (l, i)) for i in range(3)] for l in range(2)]
        self.b_gate = [[Buf("gate%d_%d" % (l, i)) for i in range(3)] for l in range(2)]
        self.small = self.sb("small", [128, 64])
        self.dummy = self.sb("dummy", [128, 8])
        self.b_dummy = Buf("dummy")
        self.qgc = self.sb("qgc", [128, 1])
        self.b_qgc = Buf("qgc")
        self.b_small = Buf("small")
        self.ps = [self.es.enter_context(nc.psum_tensor("ps%d" % i, [128, 512], F32)) for i in range(8)]
        self.psb = [Buf("ps%d" % i) for i in range(8)]
        self.ps_i = 0
        self.ps_hold = set()
        self.ps_pool_i = {}
        self.ps_recent = set()

        self.ffn_tiles_M = [(0, 512, 0), (512, 512, 0), (1024, 288, 1)]
        self.ffn_tiles_R = [(0, 512, 1), (512, 224, 1)]
        self.sub_M = [(256 * i, 256, 0, i // 2) for i in range(4)] + [(1024, 256, 1, 2), (1280, 32, 1, 2)]
        self.sub_R = [(0, 256, 1, 0), (256, 256, 1, 0), (512, 224, 1, 1)]
        self.tok_M = [(128 * i, 128, i // 4) for i in range(8)] + [(1024, 128, 2), (1152, 128, 2), (1280, 32, 2)]
        self.tok_R = [(128 * i, 128, 0) for i in range(4)] + [(512, 128, 1), (640, 96, 1)]
        self.bx_M = [Buf("xM%d" % i) for i in range(3)]
        self.bh_M = [Buf("hM%d" % i) for i in range(3)]

        self.setup_consts()
        self.mod_q = deque()
        for l in range(2):
            for s in range(18):
                self.mod_q.append((l, s))
        self.mod_done = {}

        stg = [self.sq[:, 0:4, :].rearrange("p a b -> p (a b)"), self.sq[:, 4:8, :].rearrange("p a b -> p (a b)")]
        self.load_x(self.d_xp, 0, self.xres, [(128 * i, 128, 128 * i) for i in range(8)], self.bx_M,
                    [i // 4 for i in range(8)], stg)
        self.load_x(self.d_xs, 0, self.xres, [(0, 128, 1024), (128, 128, 1152), (256, 32, 1280)], self.bx_M,
                    [2, 2, 2], stg)

        a_M = self.av(0, NJ * NM).rearrange("p (j n) -> p j n", n=NM)
        ba_M = [Buf("aM%d" % i) for i in range(3)]
        self.cur_a = ba_M

        for l in range(2):
            if self.stop_after < 10 * l + 1:
                break
            self.norm(self.xres, self.hbuf, self.sub_M, self.bx_M, self.bh_M, l, 0)
            self.ffn(l, 0, self.xres, self.hbuf, a_M, self.ffn_tiles_M, self.bx_M, self.bh_M, ba_M)
            if self.stop_after < 10 * l + 2:
                break
            self.mod_need(l, 1)
            if l == 0:
                self.mixer0(a_M, ba_M)
            else:
                self.norm(self.xres, self.hbuf, self.sub_M, self.bx_M, self.bh_M, l, 1)
                self.mixer1(ba_M)
            if self.stop_after < 10 * l + 3:
                break
            nb = [Buf("aM%d" % i) for i in range(3)]
            Prog.inherit(nb, self.cur_a)
            ba_M = nb
            self.cur_a = nb
            self.mod_need(l, 2)
            self.norm(self.xres, self.hbuf, self.sub_M, self.bx_M, self.bh_M, l, 2)
            self.ffn(l, 1, self.xres, self.hbuf, a_M, self.ffn_tiles_M, self.bx_M, self.bh_M, ba_M)

        self.final_out()
        P.emit()
        self.es.close()
        return nc

    def setup_consts(self):
        P = self.P
        P.dma(SP, self.cvec[:], self.d_cvec[:, :, :], writes=[self.b_cvec])
        P.dma(SP, self.adab[:], self.d_adab[:, :, :], writes=[self.b_adab])
        P.dma(SP, self.normg[:], self.d_normg[:, :, :, :], writes=[self.b_normg])
        P.dma(SP, self.pscale[:], self.d_pscale[:, :], writes=[self.b_pscale])
        P.dma(SP, self.convw[:], self.d_convw[:, :, :], writes=[self.b_convw])
        ident, ones, onesb = self.ident, self.ones, self.onesb
        P.op(DVE, lambda e: e.memset(ident[:], 0.0), writes=[self.b_ident])
        P.op(POOL, lambda e: e.affine_select(out=ident[:], in_=ident[:], pattern=[[-1, 128]],
                                             compare_op=ALU.not_equal, fill=1.0, base=0, channel_multiplier=1),
             reads=[self.b_ident], writes=[self.b_ident])
        P.op(DVE, lambda e: e.memset(ones[:], 1.0), writes=[self.b_ones])
        dummy = self.dummy
        P.op(DVE, lambda e: e.memset(dummy[:], 1.0), writes=[self.b_dummy])
        P.dma(SP, self.qgc[:], self.d_qgc[:, :], writes=[self.b_qgc])
        P.op(DVE, lambda e: e.memset(onesb[:], 1.0), writes=[self.b_onesb])
        cvec, scb = self.cvec, self.scb
        P.op(ACT, lambda e: e.activation(out=scb[:], in_=cvec[:], func=AF.Silu), reads=[self.b_cvec], writes=[self.b_scb])

    def mod_emit(self, l, s):
        P = self.P
        W, bW = self.wload(self.d_adaw[l, s], [128, 8, 512])
        ps7 = self.ps[7][:, 0:144].rearrange("p (m v) -> p m v", v=2)
        mms = []
        for mt in range(4):
            m = 4 * s + mt
            for c in range(8):
                mms.append((ps7[:, m, :], W[:, c, mt * 128:(mt + 1) * 128], self.scb[:, c, :], c == 0, c == 7))
        P.op(PE, mm_group(mms), reads=[bW, self.b_scb], writes=[self.psb[7]])
        i = s // 6
        bm = self.b_mod[l][i]
        modsb, adab, asc, gsc, normg, pscale = self.modsb[l], self.adab, self.asc[l], self.gsc[l], self.normg, self.pscale
        if s % 6 == 3:
            lo, hi = 24 * i, 24 * i + 16
            P.op(DVE, lambda e: e.tensor_tensor(out=modsb[:, lo:hi, :], in0=ps7[:, lo:hi, :],
                                                in1=adab[:, l, lo:hi].unsqueeze(2).broadcast_to([128, 16, 2]), op=ALU.add),
                 reads=[self.psb[7], self.b_adab], writes=[bm])
            P.op(DVE, lambda e: e.tensor_scalar(out=asc[:, i, :, :], in0=modsb[:, lo + 8:lo + 16, :], scalar1=1.0,
                                                scalar2=None, op0=ALU.add), reads=[bm], writes=[bm])
            P.op(DVE, lambda e: e.tensor_tensor(out=asc[:, i, :, :], in0=asc[:, i, :, :],
                                                in1=normg[:, l, i, :].unsqueeze(2).broadcast_to([128, 8, 2]), op=ALU.mult),
                 reads=[bm, self.b_normg], writes=[bm])
            self.mod_done[(l, i)] = True
        if s % 6 == 5:
            bg = self.b_gate[l][i]
            lo, hi = 24 * i + 16, 24 * i + 24
            P.op(DVE, lambda e: e.tensor_tensor(out=modsb[:, lo:hi, :], in0=ps7[:, lo:hi, :],
                                                in1=adab[:, l, lo:hi].unsqueeze(2).broadcast_to([128, 8, 2]), op=ALU.add),
                 reads=[self.psb[7], self.b_adab], writes=[bg])
            if i == 1 and l == 1:
                P.op(DVE, lambda e: e.tensor_tensor(out=gsc[:, i, :, :], in0=modsb[:, lo:hi, :],
                                                    in1=pscale[:, :].unsqueeze(2).broadcast_to([128, 8, 2]), op=ALU.mult),
                     reads=[bg, self.b_pscale], writes=[bg])
            else:
                f = 1.0 if i == 1 else 0.5
                P.op(DVE, lambda e: e.tensor_scalar(out=gsc[:, i, :, :], in0=modsb[:, lo:hi, :], scalar1=f,
                                                    scalar2=None, op0=ALU.mult), reads=[bg], writes=[bg])
            self.mod_done[(l, i, "g")] = True

    def act_prefetch(self, func):
        d = self.dummy
        self.P.op(ACT, lambda e: e.activation(out=d[0:1, 0:1], in_=d[0:1, 1:2], func=func), reads=[self.b_dummy], writes=[self.b_dummy])

    def mod_pump(self, n):
        for _ in range(n):
            if not self.mod_q:
                return
            l, s = self.mod_q.popleft()
            self.mod_emit(l, s)

    def mod_need(self, l, i, gate=False):
        key = (l, i, "g") if gate else (l, i)
        while not self.mod_done.get(key):
            self.mod_pump(1)

    def load_x(self, dram, row0, xbuf, tiles, bx, parents, stg=None):
        P = self.P
        stg = [self.sq[:, 0:4, :].rearrange("p a b -> p (a b)"), self.sq[:, 4:8, :].rearrange("p a b -> p (a b)"),
               self.sq2[:, 0:4, :].rearrange("p a b -> p (a b)"), self.sq2[:, 4:8, :].rearrange("p a b -> p (a b)")]
        b_stg = [Buf("stg%d" % i) for i in range(4)]
        Prog.inherit(b_stg, [self.b_sq, self.b_sq2])
        for ti, (r0, npk, c0) in enumerate(tiles):
            k = self.stg_i % 4
            self.stg_i += 1
            st, bst = stg[k], b_stg[k]
            P.dma(SP, st[0:npk, :], dram[row0 + r0:row0 + r0 + npk, :], writes=[bst], key="stg%d" % k)
            for half in range(2):
                ps, pb = self.psum()
                def tr(e, ps=ps, st=st, half=half, npk=npk):
                    ins = None
                    for j in range(4):
                        c = 4 * half + j
                        ins = e.transpose(out=ps[:, j * 128:j * 128 + npk], in_=st[0:npk, c * 128:(c + 1) * 128],
                                          identity=self.ident[0:npk, 0:npk])
                    return ins
                P.op(PE, tr, reads=[bst, self.b_ident], writes=[pb])
                src = ps[:, :].rearrange("p (j n) -> p j n", n=128)[:, :, 0:npk]
                dst = xbuf[:, 4 * half:4 * half + 4, c0:c0 + npk]
                P.op(ACT if half == 0 else DVE,
                     (lambda e, dst=dst, src=src: e.activation(out=dst, in_=src, func=AF.Copy)) if half == 0 else
                     (lambda e, dst=dst, src=src: e.tensor_copy(out=dst, in_=src)),
                     reads=[pb], writes=[bx[parents[ti]]])
        Prog.inherit([self.b_sq, self.b_sq2], b_stg)

    def norm(self, xbuf, hbuf, subs, bx, bh, l, i, tiles=None):
        P = self.P
        if tiles is None:
            tiles = self.ffn_tiles_M if len(subs) == len(self.sub_M) else self.ffn_tiles_R
        asc, modsb, bm = self.asc[l], self.modsb[l], self.b_mod[l][i]
        rsall = self.rsall
        b_rs = [Buf("rs_t%d" % t) for t in range(len(tiles))]
        Prog.inherit(b_rs, getattr(self, "b_rs_prev", []))
        self.b_rs_prev = b_rs
        pend = None

        def fin(pd):
            ps, pb, c0, n, par = pd
            P.op(ACT, lambda e: e.activation(out=rsall[:, c0:c0 + n], in_=ps[:, 0:n], func=AF.Sqrt, bias=EPS, scale=1.0 / D),
                 reads=[pb], writes=[b_rs[par]])
            P.op(DVE, lambda e: e.reciprocal(out=rsall[:, c0:c0 + n], in_=rsall[:, c0:c0 + n]),
                 reads=[b_rs[par]], writes=[b_rs[par]])

        for (c0, n, v, par) in subs:
            k = self.nrm_i % 2
            self.nrm_i += 1
            sq, bsq = (self.sq, self.b_sq) if k == 0 else (self.sq2, self.b_sq2)
            P.op(ACT, lambda e, sq=sq, c0=c0, n=n: e.activation(out=sq[:, :, 0:n], in_=xbuf[:, :, c0:c0 + n], func=AF.Square),
                 reads=[bx[par]], writes=[bsq])
            ps, pb = self.psum()
            P.op(PE, mm_group([(ps[:, 0:n], self.ones[:], sq[:, c, 0:n], c == 0, c == 7) for c in range(8)]),
                 reads=[bsq, self.b_ones], writes=[pb])
            if pend is not None:
                fin(pend)
            pend = (ps, pb, c0, n, par)
        fin(pend)
        if i != 1:
            self.act_prefetch(AF.Silu)
        self.mod_need(l, i)
        for ti, (c0, n, v) in enumerate(tiles):
            for c in range(8):
                k2 = self.nt_i % 2
                self.nt_i += 1
                nt, bnt = self.nt[k2], self.b_nt[k2]
                P.op(DVE, lambda e, nt=nt, c=c, c0=c0, n=n, v=v: e.scalar_tensor_tensor(
                    out=nt[:, 0:n], in0=xbuf[:, c, c0:c0 + n], scalar=asc[:, i, c, v:v + 1], in1=rsall[:, c0:c0 + n],
                    op0=ALU.mult, op1=ALU.mult), reads=[bx[ti], b_rs[ti], bm], writes=[bnt])
                P.op(ACT, lambda e, nt=nt, c=c, c0=c0, n=n, v=v: e.activation(
                    out=hbuf[:, c, c0:c0 + n], in_=nt[:, 0:n], func=AF.Identity,
                    bias=modsb[:, 24 * i + c, v:v + 1], scale=1.0), reads=[bnt, bm], writes=[bh[ti]])

    def ffn(self, l, s, xbuf, hbuf, abuf, tiles, bx, bh, ba, mid=None):
        P = self.P
        gi = 0 if s == 0 else 2
        gsc, bm = self.gsc[l], self.b_gate[l][gi]
        for sl in range(11):
            W, bW = self.wload(self.d_w1[l, s, sl], [128, 8, 512])
            for ti, (c0, n, v) in enumerate(tiles):
                for jj in range(2):
                    j = 2 * sl + jj
                    pg, bg = self.psum()
                    pu, bu = self.psum()
                    P.op(PE, mm_group([(pg[:, 0:n], W[:, c, jj * 256:jj * 256 + 128], hbuf[:, c, c0:c0 + n], c == 0, c == 7)
                                       for c in range(8)]), reads=[bW, bh[ti]], writes=[bg])
                    P.op(PE, mm_group([(pu[:, 0:n], W[:, c, jj * 256 + 128:jj * 256 + 256], hbuf[:, c, c0:c0 + n], c == 0, c == 7)
                                       for c in range(8)]), reads=[bW, bh[ti]], writes=[bu])
                    k = self.sg_i % 2
                    self.sg_i += 1
                    sg, bsg = self.sg[k], self.b_sg[k]
                    P.op(ACT, lambda e, sg=sg, pg=pg, n=n: e.activation(out=sg[:, 0:n], in_=pg[:, 0:n], func=AF.Silu),
                         reads=[bg], writes=[bsg])
                    P.op(DVE, lambda e, sg=sg, pu=pu, n=n, j=j, c0=c0: e.tensor_tensor(
                        out=abuf[:, j, c0:c0 + n], in0=pu[:, 0:n], in1=sg[:, 0:n], op=ALU.mult),
                        reads=[bu, bsg], writes=[ba[ti]])
            self.mod_pump(2)
        self.act_prefetch(AF.Sqrt)
        if mid is not None:
            mid()
        self.mod_need(l, gi, gate=True)
        for g in range(4):
            Wa, bWa = self.wload(self.d_w2[l, s, g, 0], [128, 11, 256])
            Wb, bWb = self.wload(self.d_w2[l, s, g, 1], [128, 11, 256])
            for ti, (c0, n, v) in enumerate(tiles):
                for dd in range(2):
                    d = 2 * g + dd
                    py, by = self.psum()
                    mms = []
                    for j in range(NJ):
                        Wx = Wa if j < 11 else Wb
                        mms.append((py[:, 0:n], Wx[:, j % 11, dd * 128:(dd + 1) * 128], abuf[:, j, c0:c0 + n], j == 0, j == NJ - 1))
                    P.op(PE, mm_group(mms), reads=[bWa, bWb, ba[ti]], writes=[by])
                    P.op(DVE, lambda e, py=py, n=n, d=d, c0=c0, v=v: e.scalar_tensor_tensor(
                        out=xbuf[:, d, c0:c0 + n], in0=py[:, 0:n], scalar=gsc[:, gi, d, v:v + 1], in1=xbuf[:, d, c0:c0 + n],
                        op0=ALU.mult, op1=ALU.add), reads=[by, bm, bx[ti]], writes=[bx[ti]])
            self.mod_pump(1)

    def kv_tile(self, hsrc, c0, npk, bh, Wkv, bWkv, kT, V, bkT, bV, kcol, vt, rope_t, out_row, T, par_=None):
        P = self.P
        if par_ is None:
            par_ = T["i"] % 2
        tA, tB, kst, small, bT, bkst = T["tA"][par_], T["tB"][par_], T["kst"][par_], self.small, T["bT"][par_], T["bkst"][par_]
        bsm = T["bsm"][par_]
        sc = slice(50 + 2 * par_, 52 + 2 * par_)
        T["i"] += 1
        ps, pb = self.psum()
        P.op(PE, mm_group([(ps[0:npk, 0:256], hsrc[:, c, c0:c0 + npk], Wkv[:, c, :], c == 0, c == 7) for c in range(8)]),
             reads=[bWkv, bh], writes=[pb])
        yield
        P.op(ACT, lambda e: e.activation(out=tA[0:npk, 0:128], in_=ps[0:npk, 0:128], func=AF.Square),
             reads=[pb], writes=[bT])
        yield
        P.op(DVE, lambda e: e.tensor_reduce(out=small[0:npk, sc], in_=tA[0:npk, 0:128].rearrange("p (h d) -> p h d", d=64),
                                            axis=AX.X, op=ALU.add), reads=[bT], writes=[bsm])
        yield
        P.op(ACT, lambda e: e.activation(out=small[0:npk, sc], in_=small[0:npk, sc], func=AF.Sqrt, bias=EPS, scale=1.0 / 64),
             reads=[bsm], writes=[bsm])
        yield
        P.op(DVE, lambda e: e.reciprocal(out=small[0:npk, sc], in_=small[0:npk, sc]), reads=[bsm], writes=[bsm])
        yield
        P.op(DVE, lambda e: e.tensor_tensor(out=tB[0:npk, 0:128].rearrange("p (h d) -> p h d", d=64),
                                            in0=ps[0:npk, 0:128].rearrange("p (h d) -> p h d", d=64),
                                            in1=small[0:npk, sc].unsqueeze(2).broadcast_to([npk, 2, 64]), op=ALU.mult),
             reads=[pb, bsm], writes=[bT])
        yield
        P.op(DVE, lambda e: e.tensor_tensor(out=kst[0:npk, 0:128], in0=tB[0:npk, 0:128], in1=self.kg[0:npk, :], op=ALU.mult),
             reads=[bT, self.b_kg], writes=[bkst])
        yield
        P.op(ACT, lambda e: e.activation(out=V[0:npk, vt, :], in_=ps[0:npk, 128:256], func=AF.Copy), reads=[pb], writes=[bV])
        yield
        ksrc = kst
        if out_row is not None:
            P.op(ACT, lambda e: e.activation(out=kst[0:npk, 128:256], in_=ps[0:npk, 128:256], func=AF.Copy), reads=[pb], writes=[bkst])
            yield
            P.dma(SP, self.o_nk[out_row:out_row + npk, :], kst[0:npk, 0:128], reads=[bkst], key=bkst.name + "k")
            yield
            P.dma(SP, self.o_nv[out_row:out_row + npk, :], kst[0:npk, 128:256], reads=[bkst], key=bkst.name + "v")
            yield
        if rope_t is not None:
            yield from self.rope(kst[0:npk, 0:128], tA[0:npk, 0:128], tB[0:npk, 0:128], 2, npk, rope_t, [bkst], bT)
            ksrc = tA
        ps2, pb2 = self.psum()
        P.op(PE, lambda e: e.transpose(out=ps2[:, 0:npk], in_=ksrc[0:npk, 0:128], identity=self.ident[0:npk, 0:npk]),
             reads=[bkst, bT, self.b_ident], writes=[pb2])
        yield
        P.op(ACT, lambda e: e.activation(out=kT[:, kcol:kcol + npk], in_=ps2[:, 0:npk], func=AF.Copy), reads=[pb2], writes=[bkT])
        yield

    def interleave(self, gens, width=2):
        pending = deque(gens)
        active = []
        while pending or active:
            while pending and len(active) < width:
                active.append(pending.popleft())
            for g in list(active):
                try:
                    next(g)
                except StopIteration:
                    active.remove(g)

    def interleave_w(self, gens_w):
        active = [[g, w] for g, w in gens_w]
        while active:
            for gw in list(active):
                g, w = gw
                for _ in range(w):
                    try:
                        next(g)
                    except StopIteration:
                        active.remove(gw)
                        break

    def rope(self, x, t1, t2, H, npk, rt, bx_list, bT, bT2=None):
        P = self.P
        cosf, sinf = self.ropec[0:npk, rt, :], self.ropes[0:npk, rt, :]
        v5 = lambda a: a.rearrange("p (h r f s) -> p h r f s", h=H, r=2, f=2, s=16)
        c4 = cosf.rearrange("p (r f s) -> p r f s", r=2, f=2, s=16)
        s4 = sinf.rearrange("p (r f s) -> p r f s", r=2, f=2, s=16)
        P.op(DVE, lambda e: e.tensor_tensor(out=v5(t1), in0=v5(x), in1=c4.unsqueeze(1).broadcast_to([npk, H, 2, 2, 16]), op=ALU.mult),
             reads=bx_list + [self.b_rope], writes=[bT])
        yield
        for f in range(2):
            P.op(DVE, lambda e, f=f: e.tensor_tensor(out=v5(t2)[:, :, :, f, :], in0=v5(x)[:, :, :, 1 - f, :],
                                                     in1=s4[:, :, f, :].unsqueeze(1).broadcast_to([npk, H, 2, 16]), op=ALU.mult),
                 reads=bx_list + [self.b_rope], writes=[bT2 or bT])
            yield
        P.op(DVE, lambda e: e.tensor_tensor(out=t1, in0=t1, in1=t2, op=ALU.add), reads=[bT, bT2 or bT], writes=[bT])
        yield

    def mixer0(self, a_M, ba_M):
        P = self.P
        self.ropec = self.sb("ropec", [128, 9, 64])
        self.ropes = self.sb("ropes", [128, 9, 64])
        self.cmask = self.sb("cmask", [128, NS_COLS])
        self.b_rope, self.b_cmask = Buf("rope"), Buf("cmask")
        P.dma(SP, self.ropec[:], self.d_ropec[:, :, :], writes=[self.b_rope], key="ropec")
        P.dma(SP, self.ropes[:], self.d_ropes[:, :, :], writes=[self.b_rope], key="ropes")
        P.dma(SP, self.cmask[:], self.d_cmask[:, :], writes=[self.b_cmask])
        P.dma(SP, self.qg[:], self.d_qg[:, :], writes=[self.b_qg])
        P.dma(SP, self.kg[:], self.d_kg[:, :], writes=[self.b_kg])
        aR = self.av(0, NJ * NR).rearrange("p (j n) -> p j n", n=NR)
        xR = self.av(NJ * NR, 2 * 8 * NR, F32).rearrange("p (c n) -> p c n", n=NR)
        hR = self.hbuf[:, :, 0:NR]
        b_aR = [Buf("aR%d" % i) for i in range(2)]
        b_xR = [Buf("xR%d" % i) for i in range(2)]
        Prog.inherit(b_aR + b_xR, ba_M)
        bhR = [Buf("hR0"), Buf("hR1")]
        Prog.inherit(bhR, self.bh_M)
        stg = [self.sq[:, 0:4, :].rearrange("p a b -> p (a b)"), self.sq[:, 4:8, :].rearrange("p a b -> p (a b)")]
        self.load_x(self.d_xs, NS_COLS, xR, [(c0, npk, c0) for (c0, npk, par) in self.tok_R], b_xR,
                    [par for (c0, npk, par) in self.tok_R], stg)
        self.norm(xR, hR, self.sub_R, b_xR, bhR, 0, 0)
        def norm2_M():
            Prog.inherit(self.bh_M, bhR)
            self.norm(self.xres, self.hbuf, self.sub_M, self.bx_M, self.bh_M, 0, 1)
        self.ffn(0, 0, xR, hR, aR, self.ffn_tiles_R, b_xR, bhR, b_aR, mid=norm2_M)
        h2R = self.av(5248, 8 * NR).rearrange("p (c n) -> p c n", n=NR)
        bh2R = [Buf("h2R0"), Buf("h2R1")]
        Prog.inherit(bh2R, b_aR)
        self.norm(xR, h2R, self.sub_R, b_xR, bh2R, 0, 1)

        kT = self.av(0, 2560)
        V = self.av(2560, 21 * 128).rearrange("p (t f) -> p t f", f=128)
        b_kT = [Buf("kT_p%d" % i) for i in range(4)] + [Buf("kT_s")]
        b_V = [Buf("V_p%d" % i) for i in range(4)] + [Buf("V_s")]
        Prog.inherit(b_kT + b_V, b_aR)
        T = {"tA": [self.sg[0][:, 0:128], self.sg[0][:, 256:384]], "tB": [self.sg[0][:, 128:256], self.sg[0][:, 384:512]],
             "kst": [self.sg[1][:, 0:256], self.sg[1][:, 256:512]], "i": 0,
             "bT": [Buf("kvT0"), Buf("kvT1")], "bkst": [Buf("kst0"), Buf("kst1")], "bsm": [Buf("smk0"), Buf("smk1")]}
        Prog.inherit(T["bT"] + T["bkst"], self.b_sg)
        Wkv, bWkv = self.wload(self.d_wkv[:, :, :], [128, 8, 256])
        self.interleave([self.kv_tile(h2R, c0, npk, bh2R[par], Wkv, bWkv, kT, V, b_kT[4], b_V[4], 1824 + c0, 15 + i, 3 + i, None, T)
                         for i, (c0, npk, par) in enumerate(self.tok_R)])
        qT = self.av(5248, 4 * NM).rearrange("p (s n) -> p s n", n=NM)
        attnT = self.av(10496, 4 * NM).rearrange("p (s n) -> p s n", n=NM)
        convT = self.av(15744, 4 * NM).rearrange("p (s n) -> p s n", n=NM)
        pT = [self.av(20992 + 512 * k, 512) for k in range(3)]
        T1s = [self.av(22528, 1024, F32), self.av(25600, 1024, F32)]
        T2s = [self.av(23552, 1024, F32), self.av(20992, 1024, F32)]
        T3s = [self.av(24576, 1024, F32), self.av(15744, 1024, F32)]
        rd = self.av(25600, 1024, F32)
        ckst = self.av(26624, 1024, F32).rearrange("p (t f) -> p t f", f=128)
        b_qT = [Buf("qT_%d" % i) for i in range(5)]
        b_attnT = [Buf("attnT%d" % i) for i in range(3)]
        b_convT = Buf("convT")
        b_pT = [Buf("pT%d" % k) for k in range(3)]
        b_T1s, b_T3s, b_rd, b_ckst = [Buf("T1a"), Buf("T1b")], [Buf("T3a"), Buf("T3b")], Buf("rd"), Buf("ckst")
        b_smq = [Buf("smq0"), Buf("smq1")]
        allnew = b_qT + b_attnT + [b_convT] + b_pT + b_T1s + b_T3s + [b_rd, b_ckst]
        Prog.inherit(allnew, b_aR + b_xR + bh2R)
        P.dma(SP, ckst[:, :, :], self.d_ck.rearrange("(t p) f -> p t f", p=128), writes=[b_ckst])
        P.dma(POOL, V[:, 8:12, :], self.d_cv.rearrange("(t p) f -> p t f", p=128), writes=[b_V[4]], key="cvload")
        for t in range(4):
            ps2, pb2 = self.psum()
            P.op(PE, lambda e, t=t, ps2=ps2: e.transpose(out=ps2[:, 0:128], in_=ckst[:, t, :], identity=self.ident[:]),
                 reads=[b_ckst, self.b_ident], writes=[pb2])
            P.op(ACT, lambda e, t=t, ps2=ps2: e.activation(out=kT[:, 1024 + 128 * t:1152 + 128 * t], in_=ps2[:, 0:128], func=AF.Copy),
                 reads=[pb2], writes=[b_kT[4]])
        Wq, bWq = self.wload(self.d_wq[:, :, :], [128, 8, 512])
        def mtile(i, c0, npk, par):
            is_s = i >= 8
            bi = 4 if is_s else i // 2
            if is_s:
                kcol, vt, rt, orow = 1536 + (c0 - 1024), 12 + (i - 8), (i - 8), None
            else:
                kcol, vt, rt, orow = c0, i, None, c0
            yield from self.kv_tile(self.hbuf, c0, npk, self.bh_M[par], Wkv, bWkv, kT, V, b_kT[bi], b_V[bi], kcol, vt, rt, orow, T, par_=i % 2)
            qp = i % 2
            T1, T2, b_T1, bsmq = T1s[qp], T2s[qp], b_T1s[qp], b_smq[qp]
            qc = slice(34 + 8 * qp, 42 + 8 * qp)
            ps, pb = self.psum()
            P.op(PE, mm_group([(ps[0:npk, 0:512], self.hbuf[:, c, c0:c0 + npk], Wq[:, c, :], c == 0, c == 7) for c in range(8)]),
                 reads=[bWq, self.bh_M[par]], writes=[pb])
            yield
            small = self.small
            v3 = lambda a: a.rearrange("p (h d) -> p h d", d=64)
            P.op(ACT, lambda e, ps=ps, npk=npk, T1=T1: e.activation(out=T1[0:npk, :], in_=ps[0:npk, :], func=AF.Square),
                 reads=[pb], writes=[b_T1])
            yield
            P.op(DVE, lambda e, npk=npk, T1=T1, qc=qc: e.tensor_reduce(out=small[0:npk, qc], in_=v3(T1[0:npk, :]), axis=AX.X, op=ALU.add),
                 reads=[b_T1], writes=[bsmq])
            yield
            P.op(ACT, lambda e, npk=npk, qc=qc: e.activation(out=small[0:npk, qc], in_=small[0:npk, qc], func=AF.Sqrt, bias=EPS,
                                                      scale=1.0 / 64), reads=[bsmq], writes=[bsmq])
            yield
            P.op(DVE, lambda e, npk=npk, qc=qc: e.reciprocal(out=small[0:npk, qc], in_=small[0:npk, qc]),
                 reads=[bsmq], writes=[bsmq])
            yield
            P.op(DVE, lambda e, ps=ps, npk=npk, T2=T2, qc=qc: e.tensor_tensor(out=v3(T2[0:npk, :]), in0=v3(ps[0:npk, :]),
                                                                in1=small[0:npk, qc].unsqueeze(2).broadcast_to([npk, 8, 64]),
                                                                op=ALU.mult), reads=[pb, bsmq], writes=[b_T1])
            yield
            qsrc = T2
            if is_s:
                P.op(DVE, lambda e, npk=npk, T1=T1, T2=T2: e.tensor_tensor(out=T1[0:npk, :], in0=T2[0:npk, :], in1=self.qg[0:npk, :], op=ALU.mult),
                     reads=[b_T1, self.b_qg], writes=[b_T1])
                yield
            if is_s:
                yield from self.rope(T1[0:npk, :], T2[0:npk, :], T3s[qp][0:npk, :], 8, npk, rt, [b_T1], b_T1, b_T3s[qp])
                qsrc = T2
            pst, pbt = self.psum()
            def trq(e, pst=pst, qsrc=qsrc, npk=npk):
                ins = None
                for s4 in range(4):
                    ins = e.transpose(out=pst[:, s4 * 128:s4 * 128 + npk], in_=qsrc[0:npk, s4 * 128:(s4 + 1) * 128],
                                      identity=self.ident[0:npk, 0:npk])
                return ins
            P.op(PE, trq, reads=[b_T1, self.b_ident], writes=[pbt])
            yield
            if is_s:
                P.op(ACT, lambda e, pst=pst, npk=npk, c0=c0: e.activation(
                    out=qT[:, :, c0:c0 + npk], in_=pst[:, :].rearrange("p (s n) -> p s n", n=128)[:, :, 0:npk], func=AF.Copy),
                    reads=[pbt], writes=[b_qT[bi]])
            else:
                P.op(ACT, lambda e, pst=pst, npk=npk, c0=c0: e.activation(
                    out=qT[:, :, c0:c0 + npk], in_=pst[:, :].rearrange("p (s n) -> p s n", n=128)[:, :, 0:npk], func=AF.Copy,
                    scale=self.qgc[:, 0:1]), reads=[pbt, self.b_qgc], writes=[b_qT[bi]])
            yield


        self.interleave([mtile(i, c0, npk, par) for i, (c0, npk, par) in enumerate(self.tok_M)])

        assert not self.mod_q and self.mod_done.get((1, 2, "g")), "PSUM bank 7 still holds adaLN accumulators"
        Prog.inherit(b_pT + [b_rd], b_T1s)
        Prog.inherit([b_convT], b_T3s)
        s_chunks = [(1024 + 128 * t, 128, 8 + t) for t in range(4)] + [(1536, 128, 12), (1664, 128, 13), (1792, 32, 14)] + \
                   [(1824 + 128 * i, 128, 15 + i) for i in range(5)] + [(2464, 96, 20)]
        groups = []
        for bi in range(4):
            for hh in range(2):
                for sp in range(2):
                    groups.append((bi, hh, (2 * sp, 2 * sp + 2), bi * 256, 256,
                                   [(bi * 256 + 128 * kc, 128, 2 * bi + kc) for kc in range(2)], b_attnT[bi // 2]))
        for hh in range(2):
            for s4 in range(4):
                groups.append((4, hh, (s4, s4 + 1), 1024, NS_COLS, s_chunks, b_attnT[2]))
        rounds = [(g, ci) for g in range(len(groups)) for ci in range(len(groups[g][5]))]
        st = {}

        def emit_s(ri):
            g, ci = rounds[ri]
            bi, hh, (s0, s1), qc0, qn, chunks, bat = groups[g]
            kcol, npk, vt = chunks[ci]
            ps, pb = self.psum_pool("attn", (2, 3, 4, 5, 6, 7))
            ncol = (s1 - s0) * qn
            hs = slice(hh * 64, hh * 64 + 64)
            P.op(PE, lambda e: e.matmul(ps[0:npk, 0:ncol], lhsT=kT[hs, kcol:kcol + npk], rhs=qT[hs, s0:s1, qc0:qc0 + qn],
                                        start=True, stop=True), reads=[b_kT[bi], b_qT[bi]], writes=[pb])
            st[ri] = (ps, pb, ncol)

        def attn_gen():
            emit_s(0)
            yield
            acc = {}
            for ri in range(len(rounds)):
                if ri + 1 < len(rounds):
                    emit_s(ri + 1)
                    yield
                g, ci = rounds[ri]
                bi, hh, (s0, s1), qc0, qn, chunks, bat = groups[g]
                kcol, npk, vt = chunks[ci]
                ps, pb, ncol = st.pop(ri)
                k = ri % 3
                p, bp = pT[k], b_pT[k]
                P.op(ACT, lambda e, p=p, ps=ps, npk=npk, ncol=ncol: e.activation(out=p[0:npk, 0:ncol], in_=ps[0:npk, 0:ncol],
                                                                                 func=AF.Exp, scale=0.125), reads=[pb], writes=[bp])
                yield
                if ci == 0:
                    acc[g] = (self.psum_pool("attn", (2, 3, 4, 5, 6, 7), hold=True), self.psum_pool("attn", (2, 3, 4, 5, 6, 7), hold=True))
                (pn, bn), (pd, bd) = acc[g]
                last = ci == len(chunks) - 1
                P.op(PE, lambda e, pn=pn, p=p, npk=npk, ncol=ncol, vt=vt, ci=ci, last=last: e.matmul(
                    pn[:, 0:ncol], lhsT=V[0:npk, vt, :], rhs=p[0:npk, 0:ncol], start=(ci == 0), stop=last),
                    reads=[bp, b_V[bi]], writes=[bn])
                yield
                P.op(PE, lambda e, pd=pd, p=p, npk=npk, ncol=ncol, ci=ci, last=last: e.matmul(
                    pd[:, 0:ncol], lhsT=self.onesb[0:npk, :], rhs=p[0:npk, 0:ncol], start=(ci == 0), stop=last),
                    reads=[bp, self.b_onesb], writes=[bd])
                yield
                if last:
                    hs = slice(hh * 64, hh * 64 + 64)
                    P.op(ACT, lambda e, pd=pd, hs=hs, ncol=ncol: e.activation(out=rd[hs, 0:ncol], in_=pd[hs, 0:ncol], func=AF.Ln),
                         reads=[bd], writes=[b_rd])
                    yield
                    P.op(ACT, lambda e, hs=hs, ncol=ncol: e.activation(out=rd[hs, 0:ncol], in_=rd[hs, 0:ncol], func=AF.Exp, scale=-1.0),
                         reads=[b_rd], writes=[b_rd])
                    yield
                    P.op(DVE, lambda e, pn=pn, hs=hs, ncol=ncol, s0=s0, s1=s1, qc0=qc0, qn=qn: e.tensor_tensor(
                        out=attnT[hs, s0:s1, qc0:qc0 + qn], in0=pn[hs, 0:ncol].rearrange("p (s n) -> p s n", n=qn),
                        in1=rd[hs, 0:ncol].rearrange("p (s n) -> p s n", n=qn), op=ALU.mult),
                        reads=[bn, b_rd], writes=[bat])
                    yield
                    self.psum_release(bn, fresh=True)
                    self.psum_release(bd)
                    del acc[g]


        def conv_gen():
            upad = self.sq[:, :, :].rearrange("p a b -> p (a b)")
            cacc = self.sq2[:, :, :].rearrange("p a b -> p (a b)")
            bgs = self.rsall
            xcs = [self.nt[0], self.nt[1]]
            b_upad, b_cacc, b_bgs, b_xcs = self.b_sq, self.b_sq2, Buf("bgs"), self.b_nt
            Prog.inherit([b_bgs], getattr(self, "b_rs_prev", []))
            self.b_rs_prev = [b_bgs]
            P.op(DVE, lambda e: e.memset(upad[:, :], 0.0), writes=[b_upad])
            yield
            cw = self.convw
            for cc in range(4):
                Wc, bWc = self.wload(self.d_wcv[cc], [128, 8, 384])
                for ti, (c0, n, v) in enumerate(self.ffn_tiles_M):
                    def proj(q3, pq, bq):
                        return P.op(PE, mm_group([(pq[:, 0:n], Wc[:, c, q3 * 128:(q3 + 1) * 128], self.hbuf[:, c, c0:c0 + n],
                                                   c == 0, c == 7) for c in range(8)]), reads=[bWc, self.bh_M[ti]], writes=[bq])
                    pxc, bpxc = self.psum_pool("conv", (0, 1))
                    proj(2, pxc, bpxc)
                    yield
                    pcg, bpcg = self.psum_pool("conv", (0, 1))
                    proj(1, pcg, bpcg)
                    yield
                    k = (cc * 3 + ti) % 2
                    xc_, bxc = xcs[k], b_xcs[k]
                    P.op(ACT, lambda e, xc_=xc_, n=n, px=pxc: e.activation(out=xc_[:, 0:n], in_=px[:, 0:n], func=AF.Copy),
                         reads=[bpxc], writes=[bxc])
                    yield
                    if ti < 2:
                        uo = upad[:, 1 + 2 * ti * 257:1 + (2 * ti + 2) * 257].rearrange("p (b k) -> p b k", k=257)[:, :, 0:256]
                        P.op(DVE, lambda e, uo=uo, pc=pcg, xc_=xc_: e.tensor_tensor(
                            out=uo, in0=pc[:, 0:512].rearrange("p (b k) -> p b k", k=256),
                            in1=xc_[:, 0:512].rearrange("p (b k) -> p b k", k=256), op=ALU.mult),
                            reads=[bpcg, bxc], writes=[b_upad])
                        yield
                    else:
                        P.op(DVE, lambda e, pc=pcg, xc_=xc_: e.tensor_tensor(out=upad[:, 1029:1317], in0=pc[:, 0:NS_COLS],
                                                                             in1=xc_[:, 0:NS_COLS], op=ALU.mult),
                             reads=[bpcg, bxc], writes=[b_upad])
                        yield
                    pbg, bpbg = self.psum_pool("conv", (0, 1))
                    proj(0, pbg, bpbg)
                    yield
                    P.op(ACT, lambda e, n=n, c0=c0, pb_=pbg: e.activation(out=bgs[:, c0:c0 + n], in_=pb_[:, 0:n], func=AF.Copy),
                         reads=[bpbg], writes=[b_bgs])
                    yield
                P.op(DVE, lambda e: e.tensor_tensor(out=upad[:, 1029:1317], in0=upad[:, 1029:1317], in1=self.cmask[:, :], op=ALU.mult),
                     reads=[b_upad, self.b_cmask], writes=[b_upad])
                yield
                P.op(DVE, lambda e, cc=cc: e.tensor_scalar(out=cacc[:, 1:1317], in0=upad[:, 1:1317], scalar1=cw[:, cc, 1:2], scalar2=None,
                                                           op0=ALU.mult), reads=[b_upad, self.b_convw], writes=[b_cacc])
                yield
                P.op(DVE, lambda e, cc=cc: e.scalar_tensor_tensor(out=cacc[:, 1:1317], in0=upad[:, 0:1316], scalar=cw[:, cc, 0:1],
                                                                  in1=cacc[:, 1:1317], op0=ALU.mult, op1=ALU.add),
                     reads=[b_upad, self.b_convw, b_cacc], writes=[b_cacc])
                yield
                P.op(DVE, lambda e, cc=cc: e.scalar_tensor_tensor(out=cacc[:, 1:1317], in0=upad[:, 2:1318], scalar=cw[:, cc, 2:3],
                                                                  in1=cacc[:, 1:1317], op0=ALU.mult, op1=ALU.add),
                     reads=[b_upad, self.b_convw, b_cacc], writes=[b_cacc])
                yield
                P.op(DVE, lambda e, cc=cc: e.tensor_tensor(
                    out=convT[:, cc, 0:1024].rearrange("p (b k) -> p b k", k=256),
                    in0=cacc[:, 1:1029].rearrange("p (b k) -> p b k", k=257)[:, :, 0:256],
                    in1=bgs[:, 0:1024].rearrange("p (b k) -> p b k", k=256), op=ALU.mult),
                    reads=[b_cacc, b_bgs], writes=[b_convT])
                yield
                P.op(DVE, lambda e, cc=cc: e.tensor_tensor(out=convT[:, cc, 1024:NM], in0=cacc[:, 1029:1317], in1=bgs[:, 1024:NM],
                                                           op=ALU.mult), reads=[b_cacc, b_bgs], writes=[b_convT])
                yield


        self.interleave_w([(attn_gen(), 6), (conv_gen(), 1)])

        Woa, bWoa = self.wload(self.d_wo[0], [128, 4, 1024])
        Woc, bWoc = self.wload(self.d_wo[1], [128, 4, 1024])
        self.act_prefetch(AF.Sqrt)
        self.mod_need(0, 1, gate=True)
        gsc, bm = self.gsc[0], self.b_gate[0][1]
        for ti, (c0, n, v) in enumerate(self.ffn_tiles_M):
            for d in range(8):
                po, bpo = self.psum()
                mms = [(po[:, 0:n], Woa[:, s4, d * 128:(d + 1) * 128], attnT[:, s4, c0:c0 + n], s4 == 0, False) for s4 in range(4)]
                mms += [(po[:, 0:n], Woc[:, s4, d * 128:(d + 1) * 128], convT[:, s4, c0:c0 + n], False, s4 == 3) for s4 in range(4)]
                P.op(PE, mm_group(mms), reads=[bWoa, bWoc, b_attnT[ti], b_convT], writes=[bpo])
                P.op(DVE, lambda e, po=po, n=n, d=d, c0=c0, v=v: e.scalar_tensor_tensor(
                    out=self.xres[:, d, c0:c0 + n], in0=po[:, 0:n], scalar=gsc[:, 1, d, v:v + 1], in1=self.xres[:, d, c0:c0 + n],
                    op0=ALU.mult, op1=ALU.add), reads=[bpo, bm, self.bx_M[ti]], writes=[self.bx_M[ti]])
        self.cur_a = allnew + b_kT + b_V + b_aR + b_xR
        Prog.inherit(self.b_sg, T["bT"] + T["bkst"])

    def kv_tile_R(self, hR, c0, npk, bh, Wkv, bWkv, kT, V, bkT, bV, kcol, vt, ridx, T):
        self.kv_tile(hR, c0, npk, bh, Wkv, bWkv, kT, V, bkT, bV, kcol, vt, ("R", c0 // 128), None, T)

    def mixer1(self, ba_M):
        P = self.P
        z = self.av(0, 11 * 1024).rearrange("p (t f) -> p t f", f=1024)
        mpp = self.av(11264, 2048).rearrange("p (a g t) -> p a g t", g=4, t=256)
        mps = self.av(13312, 3456).rearrange("p (a g t) -> p a g t", g=4, t=NS_COLS)
        b_z = [Buf("z%d" % i) for i in range(5)]
        b_mpp, b_mps = Buf("mpp"), Buf("mps")
        Prog.inherit(b_z + [b_mpp, b_mps], self.cur_a)
        P.dma(POOL, mpp, self.d_mpp[:, :, :, :], writes=[b_mpp])
        P.dma(POOL, mps, self.d_mps[:, :, :, :], writes=[b_mps])
        Wp, bWp = self.wload(self.d_poolw[:, :, :, :], [128, 4, 2, 256])
        for tt, (c0, npk, par) in enumerate(self.tok_M):
            bz = b_z[4 if tt >= 8 else tt // 2]
            for half in range(2):
                ps, pb = self.psum()
                mms = []
                for g2 in range(2):
                    gi = 2 * half + g2
                    for kc in range(2):
                        mms.append((ps[0:npk, g2 * 256:(g2 + 1) * 256], self.hbuf[:, 2 * gi + kc, c0:c0 + npk], Wp[:, gi, kc, :],
                                    kc == 0, kc == 1))
                P.op(PE, mm_group(mms), reads=[bWp, self.bh_M[par]], writes=[pb])
                dst = z[0:npk, tt, half * 512:(half + 1) * 512]
                if half == 0:
                    P.op(ACT, lambda e, dst=dst, ps=ps, npk=npk: e.activation(out=dst, in_=ps[0:npk, :], func=AF.Copy),
                         reads=[pb], writes=[bz])
                else:
                    P.op(DVE, lambda e, dst=dst, ps=ps, npk=npk: e.tensor_copy(out=dst, in_=ps[0:npk, :]), reads=[pb], writes=[bz])
        self.mod_need(1, 1, gate=True)
        gsc, bm = self.gsc[1], self.b_gate[1][1]
        segs = [(bi, [(2 * bi, 128), (2 * bi + 1, 128)], 256, bi * 256, mpp, b_mpp, 0, bi // 2) for bi in range(4)]
        segs.append((4, [(8, 128), (9, 128), (10, 32)], NS_COLS, 1024, mps, b_mps, 1, 2))
        for (bi, stiles, Tn, c0, Mx, bMx, v, par) in segs:
            for fc in range(8):
                gi = fc // 2
                po, bpo = self.psum()
                mms = [(po[:, 0:Tn], z[0:npk, tt, fc * 128:(fc + 1) * 128], Mx[0:npk, sc, gi, 0:Tn], sc == 0, sc == len(stiles) - 1)
                       for sc, (tt, npk) in enumerate(stiles)]
                P.op(PE, mm_group(mms), reads=[b_z[bi], bMx], writes=[bpo])
                P.op(DVE, lambda e, po=po, Tn=Tn, fc=fc, c0=c0, v=v: e.scalar_tensor_tensor(
                    out=self.xres[:, fc, c0:c0 + Tn], in0=po[:, 0:Tn], scalar=gsc[:, 1, fc, v:v + 1], in1=self.xres[:, fc, c0:c0 + Tn],
                    op0=ALU.mult, op1=ALU.add), reads=[bpo, bm, self.bx_M[par]], writes=[self.bx_M[par]])
        self.cur_a = b_z + [b_mpp, b_mps]

    def final_out(self):
        P = self.P
        P.dma(SP, self.finalg[:], self.d_finalg[:, :], writes=[self.b_finalg])
        ot = [self.av(4096 * k, 2048, F32) for k in range(2)]
        jk = [self.av(8192 + 4096 * k, 2048, F32) for k in range(2)]
        b_ot = [Buf("ot0"), Buf("ot1")]
        b_jk = [Buf("jk0"), Buf("jk1")]
        b_sm = [Buf("smf0"), Buf("smf1")]
        Prog.inherit(b_ot + b_jk, self.cur_a)
        Prog.inherit(b_sm, [self.b_small])
        outs = [(128 * i, self.o_yp, 128 * i, i // 4) for i in range(8)] + \
               [(1024 + HALO + 128 * i, self.o_ys, 128 * i, 2) for i in range(2)]
        sm = self.small

        def otile(oi, c0, dram, r0, par):
            k = oi % 2
            o, bo, j, bj, bs = ot[k], b_ot[k], jk[k], b_jk[k], b_sm[k]
            for half in range(2):
                ps, pb = self.psum()

                def tr(e, ps=ps, half=half):
                    ins = None
                    for jj in range(4):
                        c = 4 * half + jj
                        ins = e.transpose(out=ps[:, jj * 128:(jj + 1) * 128], in_=self.xres[:, c, c0:c0 + 128],
                                          identity=self.ident[:])
                    return ins
                P.op(PE, tr, reads=[self.bx_M[par], self.b_ident], writes=[pb])
                yield
                P.op(ACT, lambda e, ps=ps, half=half: e.activation(out=o[:, half * 512:(half + 1) * 512], in_=ps[:, :],
                                                                   func=AF.Copy), reads=[pb], writes=[bo])
                yield
            P.op(ACT, lambda e: e.activation(out=j[:, :], in_=o[:, :], func=AF.Square, accum_out=sm[:, oi:oi + 1]),
                 reads=[bo], writes=[bj, bs])
            yield
            P.op(ACT, lambda e: e.activation(out=sm[:, oi:oi + 1], in_=sm[:, oi:oi + 1], func=AF.Sqrt, bias=EPS,
                                             scale=1.0 / D), reads=[bs], writes=[bs])
            yield
            P.op(DVE, lambda e: e.reciprocal(out=sm[:, oi:oi + 1], in_=sm[:, oi:oi + 1]), reads=[bs], writes=[bs])
            yield
            P.op(DVE, lambda e: e.scalar_tensor_tensor(out=o[:, :], in0=o[:, :], scalar=sm[:, oi:oi + 1],
                                                       in1=self.finalg[:, :], op0=ALU.mult, op1=ALU.mult),
                 reads=[bo, bs, self.b_finalg], writes=[bo])
            yield
            P.dma(SP, dram[r0:r0 + 128, :], o[:, :], reads=[bo], key="ot%d" % k)
            yield

        self.interleave([otile(oi, *t) for oi, t in enumerate(outs)])


def _host_inputs(inp):
    f = lambda a: np.ascontiguousarray(np.asarray(a, dtype=np.float32))
    x_prompt, x_sample, c, c_ctx = f(inp["x_prompt"]), f(inp["x_sample"]), f(inp["c"]), f(inp["c_ctx"])
    ada_w, ada_b, norm_g = f(inp["ada_w"]), f(inp["ada_b"]), f(inp["norm_g"])
    w1, w2 = f(inp["ffn_w1"]), f(inp["ffn_w2"])
    shared = {}
    shared["adaw"] = f(ada_w.reshape(2, 8, 128, 18, 512).transpose(0, 3, 2, 1, 4))
    shared["adab"] = f(ada_b.reshape(2, 72, 128).transpose(2, 0, 1))
    shared["normg"] = f(norm_g.reshape(2, 3, 8, 128).transpose(3, 0, 1, 2))
    shared["finalg"] = f(np.broadcast_to(f(inp["final_g"])[None, :], (128, 1024)))
    g = w1[..., :DFF].reshape(2, 2, 8, 128, 11, 2, 128)
    u = w1[..., DFF:].reshape(2, 2, 8, 128, 11, 2, 128)
    gu = np.stack([g, u], axis=6)
    shared["w1t"] = f(gu.transpose(0, 1, 4, 3, 2, 5, 6, 7).reshape(2, 2, 11, 128, 8, 512))
    w2r = w2.reshape(2, 2, 2, 11, 128, 4, 256)
    shared["w2t"] = f(w2r.transpose(0, 1, 5, 2, 4, 3, 6))
    return shared, x_prompt, x_sample, c, c_ctx


_NC_CACHE = {}


def _get_nc(stop_after=99):
    if stop_after not in _NC_CACHE:
        _NC_CACHE[stop_after] = Builder(stop_after).build()
    return _NC_CACHE[stop_after]


HEAD_PERM = [0, 4, 1, 5, 2, 6, 3, 7]


def _pool_matrix(gpos, S_total):
    n = len(gpos)
    out = np.zeros((4, n, n), np.float32)
    for gi, w in enumerate(POOL_WINDOWS):
        left = w // 2
        right = w - 1 - left
        for t in range(n):
            gt = gpos[t]
            if gt < 0 or gt >= S_total:
                continue
            lo = max(gt - left, 0)
            hi = min(gt + right + 1, S_total)
            inv = np.float32(1.0) / np.float32(hi - lo)
            for s in range(n):
                if lo <= gpos[s] < hi:
                    out[gi, s, t] += inv
            out[gi, t, t] -= 1.0
    return out


def _rope_tables(gtok):
    half = 32
    inv = 10000.0 ** (-np.arange(0, half, 2, dtype=np.float64) / half)
    row = (gtok // GRID_W).astype(np.float64)
    col = (gtok % GRID_W).astype(np.float64)
    ar = row[:, None] * inv[None, :]
    ac = col[:, None] * inv[None, :]
    cr, sr, cc, sc = np.cos(ar), np.sin(ar), np.cos(ac), np.sin(ac)
    cosf = np.concatenate([cr, cr, cc, cc], axis=1).astype(np.float32)
    sinf = np.concatenate([-sr, sr, -sc, sc], axis=1).astype(np.float32)
    return cosf, sinf


def make_in_maps(inp, cores=range(8)):
    f = lambda a: np.ascontiguousarray(np.asarray(a, dtype=np.float32))
    shared, x_prompt, x_sample, c, c_ctx = _host_inputs(inp)
    w_in, w_out = f(inp["mix_w_in"])[0], f(inp["mix_w_out"])[0]
    wq = w_in[:, 0:512].reshape(1024, 8, 64)[:, HEAD_PERM, :].reshape(1024, 512)
    shared["wq"] = f(wq.reshape(8, 128, 512).transpose(1, 0, 2))
    shared["wkv"] = f(w_in[:, 512:768].reshape(8, 128, 256).transpose(1, 0, 2))
    bg, cg, xc = w_in[:, 768:1280], w_in[:, 1280:1792], w_in[:, 1792:2304]
    wcv = np.stack([np.concatenate([bg[:, k * 128:(k + 1) * 128], cg[:, k * 128:(k + 1) * 128],
                                    xc[:, k * 128:(k + 1) * 128]], axis=1) for k in range(4)], axis=0)
    shared["wcv"] = f(wcv.reshape(4, 8, 128, 384).transpose(0, 2, 1, 3))
    wo_a = w_out[0:512].reshape(8, 64, 1024)
    wo_a = np.stack([np.concatenate([wo_a[s], wo_a[4 + s]], axis=0) for s in range(4)], axis=1)
    wo_c = w_out[512:1024].reshape(4, 128, 1024).transpose(1, 0, 2)
    shared["wo"] = f(np.stack([wo_a, wo_c], axis=0))
    shared["qg"] = f(np.broadcast_to(np.tile(f(inp["q_norm"])[0], 8)[None, :], (128, 512)))
    shared["qgc"] = f(np.tile(f(inp["q_norm"])[0], 2).reshape(128, 1))
    shared["kg"] = f(np.broadcast_to(np.tile(f(inp["k_norm"])[0], 2)[None, :], (128, 128)))
    shared["convw"] = f(f(inp["conv_w"])[0].T.reshape(4, 128, 3).transpose(1, 0, 2))
    shared["poolw"] = f(f(inp["pool_w"])[0].reshape(4, 2, 128, 256).transpose(2, 0, 1, 3))
    shared["pscale"] = f(f(inp["pool_scale"])[0].reshape(8, 128).T)
    mp = _pool_matrix(np.arange(256), 256)
    shared["mpp"] = f(mp.reshape(4, 2, 128, 256).transpose(2, 1, 0, 3))
    cache_k, cache_v = f(inp["cache_k"]), f(inp["cache_v"])
    maps = []
    for k in cores:
        b, r = k // 4, k % 4
        gwin = 256 * r - HALO + np.arange(1024)
        idx = gwin % 1024
        m = dict(shared)
        m["xp"] = f(x_prompt[4 * k:4 * k + 4].reshape(1024, 1024))
        m["xs"] = f(x_sample[b][idx])
        m["cvec"] = f(np.stack([c_ctx, c[b]], axis=-1).reshape(8, 128, 2).transpose(1, 0, 2))
        ms = _pool_matrix(gwin[:NS_COLS], 1024)
        msp = np.zeros((4, 384, NS_COLS), np.float32)
        msp[:, :NS_COLS] = ms
        m["mps"] = f(msp.reshape(4, 3, 128, NS_COLS).transpose(2, 1, 0, 3))
        cosf, sinf = _rope_tables(idx)
        def tab(a):
            a = np.concatenate([a, np.zeros((512, 64), np.float32)], axis=0)
            tl = [a[t * 128:(t + 1) * 128] for t in range(3)] + [a[NS_COLS + t * 128:NS_COLS + (t + 1) * 128] for t in range(6)]
            return f(np.stack(tl, axis=1))
        m["ropec"] = tab(cosf)
        m["ropes"] = tab(sinf)
        gw = gwin[:NS_COLS]
        m["cmask"] = f(np.broadcast_to(((gw >= 0) & (gw < 1024)).astype(np.float32)[None, :], (128, NS_COLS)))
        m["ck"] = f(cache_k[b, 0].reshape(512, 128))
        m["cv"] = f(cache_v[b, 0].reshape(512, 128))
        maps.append(m)
    return maps


def assemble(results, cores=range(8)):
    y_prompt = np.zeros((32, 256, 1024), np.float32)
    y_sample = np.zeros((2, 1024, 1024), np.float32)
    nk = np.zeros((32, 1, 256, 2, 64), np.float32)
    nv = np.zeros((32, 1, 256, 2, 64), np.float32)
    for res, k in zip(results, cores):
        b, r = k // 4, k % 4
        y_prompt[4 * k:4 * k + 4] = res["yp"].reshape(4, 256, 1024)
        y_sample[b, 256 * r:256 * r + 256] = res["ys"]
        nk[4 * k:4 * k + 4, 0] = res["nk"].reshape(4, 256, 2, 64)
        nv[4 * k:4 * k + 4, 0] = res["nv"].reshape(4, 256, 2, 64)
    return y_prompt, y_sample, nk, nv


def kernel(**inputs):
    nc = _get_nc()
    maps = make_in_maps(inputs)
    res = run_bass_kernel_spmd(nc, maps, core_ids=list(range(8)))
    return assemble(res.results)
```
